# Optimizing a Trainium2 kernel written in Bass

```python
import jax, jax.numpy as jnp
from jax import lax
import numpy as np

D_MODEL = 2048
BATCH = 4
SEQ = 8192
DEPTH = 1

CHUNK = 64
N_HEADS = 8
HEAD_DIM = 128
ATT_WIDTH = N_HEADS * HEAD_DIM
ROPE_THETA = 500000.0
ROT_DIM = HEAD_DIM // 4
IDX_HEADS = 16
IDX_DIM = 64
IDX_ROT_DIM = IDX_DIM // 4
TOPK_MAX = 256
Q_BLOCK = 128
SGU_LEN = 128
SGU_GROUPS = 8
SGU_GROUP_DIM = 128
SGU_WIDTH = SGU_GROUPS * SGU_GROUP_DIM
D_FF = 5632
EPS = 1e-6

IN_SIZES = [ATT_WIDTH, ATT_WIDTH, ATT_WIDTH,
            IDX_HEADS * IDX_DIM, IDX_DIM, IDX_HEADS,
            SGU_WIDTH, SGU_WIDTH,
            D_MODEL, D_MODEL]
IN_COLS = sum(IN_SIZES)
SPLIT_POINTS = [sum(IN_SIZES[:i + 1]) for i in range(len(IN_SIZES) - 1)]

kernel_name = "hybrid_dsa_gmlp_gated_macaron_block"


def rms_norm(t, g):
    tf = t.astype(jnp.float32)
    y = tf * lax.rsqrt(jnp.mean(tf * tf, axis=-1, keepdims=True) + EPS)
    return (y * g.astype(jnp.float32)).astype(t.dtype)


def rope_tables(positions, rot_dim):
    inv_freq = ROPE_THETA ** (-jnp.arange(0, rot_dim, 2, dtype=jnp.float32) / rot_dim)
    ang = positions.astype(jnp.float32)[..., None] * inv_freq
    return jnp.cos(ang), jnp.sin(ang)


def partial_rope(t, cos, sin):
    half = cos.shape[-1]
    cos = cos.astype(t.dtype)
    sin = sin.astype(t.dtype)
    x1 = t[..., :half]
    x2 = t[..., half:2 * half]
    return jnp.concatenate([x1 * cos - x2 * sin, x2 * cos + x1 * sin, t[..., 2 * half:]], axis=-1)


def swiglu(h, w_gate, w_up, w_down):
    return (jax.nn.silu(h @ w_gate) * (h @ w_up)) @ w_down


def dsa_attention(q, k, v, qi, ki, wi):
    b, s = q.shape[:2]
    topk = min(TOPK_MAX, s // 4)
    nb = s // Q_BLOCK
    key_chunk = jnp.arange(s) // CHUNK
    gather = jax.vmap(lambda t, i: t[i])

    def to_blocks(t):
        return jnp.swapaxes(t.reshape((b, nb, Q_BLOCK) + t.shape[2:]), 0, 1)

    def one_block(args):
        blk, qb, qib, wib = args
        q_chunk = (blk * Q_BLOCK + jnp.arange(Q_BLOCK)) // CHUNK
        admissible = key_chunk[None, :] <= q_chunk[:, None]
        dots = jnp.einsum('bqhd,bsd->bqhs', qib, ki)
        score = jnp.einsum('bqhs,bqh->bqs', jax.nn.relu(dots), wib).astype(jnp.float32)
        score = jnp.where(admissible[None], score, -jnp.inf)
        _, idx = lax.top_k(score, topk)
        valid = key_chunk[idx] <= q_chunk[None, :, None]
        kg = gather(k, idx)
        vg = gather(v, idx)
        logits = jnp.einsum('bqhd,bqkhd->bhqk', qb, kg).astype(jnp.float32) * (HEAD_DIM ** -0.5)
        logits = jnp.where(valid[:, None], logits, -jnp.inf)
        p = jax.nn.softmax(logits, axis=-1).astype(v.dtype)
        return jnp.einsum('bhqk,bqkhd->bqhd', p, vg)

    out = lax.map(one_block, (jnp.arange(nb), to_blocks(q), to_blocks(qi), to_blocks(wi)))
    return jnp.swapaxes(out, 0, 1).reshape(b, s, N_HEADS * HEAD_DIM)


def spatial_gating(u, v, g_v, w_s, b_s):
    b, s, _ = v.shape
    v = rms_norm(v, g_v)
    pos_chunk = jnp.arange(SGU_LEN) // CHUNK
    mask = pos_chunk[None, :] <= pos_chunk[:, None]
    w = jnp.where(mask[None], w_s, jnp.zeros_like(w_s))
    vb = v.reshape(b, s // SGU_LEN, SGU_LEN, SGU_GROUPS, SGU_GROUP_DIM)
    mixed = jnp.einsum('gts,bnsgc->bntgc', w, vb) + jnp.swapaxes(b_s, 0, 1)[:, :, None]
    return u * mixed.reshape(b, s, SGU_WIDTH)


def setup_inputs(seed: int = 0) -> dict:
    key = jax.random.key(seed)
    ks = jax.random.split(key, 24)
    f32 = jnp.float32

    def w(k, shape, fan_in):
        return jax.random.normal(k, shape, f32) * (fan_in ** -0.5)

    def gain(k, n):
        return 1.0 + 0.02 * jax.random.normal(k, (DEPTH, n), f32)

    x = jax.random.normal(ks[0], (BATCH, SEQ, D_MODEL), f32)
    offsets = jax.random.randint(ks[1], (BATCH, 1), 0, 100000, dtype=jnp.int32)
    positions = offsets + jnp.arange(SEQ, dtype=jnp.int32)[None, :]
    return {
        "x": x,
        "positions": positions,
        "norm_ffn1": gain(ks[2], D_MODEL),
        "ffn1_w_gate": w(ks[3], (DEPTH, D_MODEL, D_FF), D_MODEL),
        "ffn1_w_up": w(ks[4], (DEPTH, D_MODEL, D_FF), D_MODEL),
        "ffn1_w_down": w(ks[5], (DEPTH, D_FF, D_MODEL), D_FF),
        "norm_mix": gain(ks[6], D_MODEL),
        "w_in": w(ks[7], (DEPTH, D_MODEL, IN_COLS), D_MODEL),
        "q_norm": gain(ks[8], HEAD_DIM),
        "k_norm": gain(ks[9], HEAD_DIM),
        "idx_k_norm": gain(ks[10], IDX_DIM),
        "sgu_v_norm": gain(ks[11], SGU_WIDTH),
        "sgu_w_s": w(ks[12], (DEPTH, SGU_GROUPS, SGU_LEN, SGU_LEN), SGU_LEN),
        "sgu_b_s": 0.02 * jax.random.normal(ks[13], (DEPTH, SGU_GROUPS, SGU_LEN), f32),
        "w_up_attn": w(ks[14], (DEPTH, ATT_WIDTH, D_MODEL), ATT_WIDTH),
        "w_up_sgu": w(ks[15], (DEPTH, SGU_WIDTH, D_MODEL), SGU_WIDTH),
        "w_out": w(ks[16], (DEPTH, D_MODEL, D_MODEL), D_MODEL),
        "norm_ffn2": gain(ks[17], D_MODEL),
        "ffn2_w_gate": w(ks[18], (DEPTH, D_MODEL, D_FF), D_MODEL),
        "ffn2_w_up": w(ks[19], (DEPTH, D_MODEL, D_FF), D_MODEL),
        "ffn2_w_down": w(ks[20], (DEPTH, D_FF, D_MODEL), D_FF),
    }


def reference(x, positions, norm_ffn1, ffn1_w_gate, ffn1_w_up, ffn1_w_down, norm_mix, w_in,
              q_norm, k_norm, idx_k_norm, sgu_v_norm, sgu_w_s, sgu_b_s, w_up_attn, w_up_sgu,
              w_out, norm_ffn2, ffn2_w_gate, ffn2_w_up, ffn2_w_down):
    b, s, _ = x.shape
    cos_a, sin_a = rope_tables(positions, ROT_DIM)
    cos_i, sin_i = rope_tables(positions, IDX_ROT_DIM)
    idx_w_scale = (IDX_HEADS ** -0.5) * (IDX_DIM ** -0.5)
    for l in range(DEPTH):
        x = x + 0.5 * swiglu(rms_norm(x, norm_ffn1[l]), ffn1_w_gate[l], ffn1_w_up[l], ffn1_w_down[l])

        h = rms_norm(x, norm_mix[l])
        proj = h @ w_in[l]
        q, k, v, qi, ki, wi, u, vs, ga, gb = jnp.split(proj, SPLIT_POINTS, axis=-1)

        q = partial_rope(rms_norm(q.reshape(b, s, N_HEADS, HEAD_DIM), q_norm[l]),
                         cos_a[:, :, None], sin_a[:, :, None])
        k = partial_rope(rms_norm(k.reshape(b, s, N_HEADS, HEAD_DIM), k_norm[l]),
                         cos_a[:, :, None], sin_a[:, :, None])
        v = v.reshape(b, s, N_HEADS, HEAD_DIM)
        qi = partial_rope(qi.reshape(b, s, IDX_HEADS, IDX_DIM), cos_i[:, :, None], sin_i[:, :, None])
        ki = partial_rope(rms_norm(ki, idx_k_norm[l]), cos_i, sin_i)
        y_a = dsa_attention(q, k, v, qi, ki, wi * idx_w_scale)

        y_b = spatial_gating(jax.nn.gelu(u), jax.nn.gelu(vs), sgu_v_norm[l], sgu_w_s[l], sgu_b_s[l])

        merged = jax.nn.sigmoid(ga) * (y_a @ w_up_attn[l]) + jax.nn.sigmoid(gb) * (y_b @ w_up_sgu[l])
        x = x + merged @ w_out[l]

        x = x + 0.5 * swiglu(rms_norm(x, norm_ffn2[l]), ffn2_w_gate[l], ffn2_w_up[l], ffn2_w_down[l])
    return x
```

```python
import numpy as np
import ml_dtypes
import concourse.bass as bass
import concourse.mybir as mybir
from concourse.bass_utils import run_bass_kernel_spmd

F32 = mybir.dt.float32
BF16 = mybir.dt.bfloat16
I32 = mybir.dt.int32
AF = mybir.ActivationFunctionType
ALU = mybir.AluOpType
AX = mybir.AxisListType

D = 2048
KC = 16
FF = 5632
JC = 44
S = 8192
NBLK = 64
NOWN = 32
T = 512
NT = 4
NG_OWN = 8
NG_ALL = 16
IN_COLS = 10320
EPS = 1e-6
ROPE_THETA = 500000.0
TOPK = 256
NEG = -1.0e30
C_Q, C_K, C_V, C_QI, C_KI, C_WI, C_U, C_VS, C_GA, C_GB = 0, 1024, 2048, 3072, 4096, 4160, 4176, 5200, 6224, 8272


class LT:
    def __init__(self, name, parent=None):
        self.name = name
        self.writers = {}
        self.readers = {}
        self.parent = parent
        self.children = []
        if parent is not None:
            parent.children.append(self)

    def rel(self):
        out = [self]
        if self.parent is not None:
            out.append(self.parent)
        out += self.children
        return out


class K:
    def __init__(self, nc):
        self.nc = nc
        self.eng = {}
        self.sems = {}
        self.cnt = {}
        self.prog = {n: [] for n in ("pe", "act", "dve", "pool", "sp")}
        self.waited = {n: {} for n in self.prog}
        self.dma_sems = []
        self.ctx = []

    def add_engine_sem(self, name, sem):
        self.sems[name] = sem
        self.cnt[name] = 0

    def new_dma_sem(self, sem):
        key = "dma%d" % len(self.dma_sems)
        self.dma_sems.append(key)
        self.sems[key] = sem
        self.cnt[key] = 0
        return key

    def _deps(self, eng, reads, writes):
        deps = {}
        def add(d, skip_own=False):
            for k, v in d.items():
                if skip_own and k == eng:
                    continue
                if deps.get(k, 0) < v:
                    deps[k] = v
        for t0 in reads:
            for t in t0.rel():
                add(t.writers)
        for t0 in writes:
            for t in t0.rel():
                add(t.writers, True)
                add(t.readers, True)
        out = []
        for k, v in deps.items():
            if self.waited[eng].get(k, 0) >= v:
                continue
            self.waited[eng][k] = v
            out.append((k, v))
        return out

    def op(self, eng, fn, reads=(), writes=(), sig=True):
        waits = self._deps(eng, reads, writes)
        if sig:
            self.cnt[eng] += 1
            v = self.cnt[eng]
        else:
            v = self.cnt[eng] + 1
        self.prog[eng].append((waits, fn, eng, 1 if sig else 0))
        for t in reads:
            if t.readers.get(eng, 0) < v:
                t.readers[eng] = v
        for t in writes:
            t.writers[eng] = v
        return (eng, v)

    def begin_write(self, t):
        pass

    def dma(self, queue, semkey, fn, reads=(), writes=()):
        waits = self._deps(queue, reads, writes)
        self.cnt[semkey] += 16
        v = self.cnt[semkey]
        self.prog[queue].append((waits, fn, semkey, 16))
        for t in reads:
            if t.readers.get(semkey, 0) < v:
                t.readers[semkey] = v
        for t in writes:
            t.writers[semkey] = v
        return (semkey, v)

    def fresh(self, t):
        return t

    def barrier(self):
        keys = list(self.sems.keys())
        for eng in ("sp", "pool", "act", "dve", "pe"):
            self.final_wait(eng, keys)

    def final_wait(self, eng, keys):
        waits = []
        for k in keys:
            v = self.cnt[k]
            if v > 0 and self.waited[eng].get(k, 0) < v:
                waits.append((k, v))
                self.waited[eng][k] = v
        self.prog[eng].append((waits, None, None, 0))

    def replay(self, block):
        nc = self.nc
        sems = self.sems
        def run(e, lst):
            for waits, fn, inckey, inc in lst:
                for k, v in waits:
                    e.wait_ge(sems[k], v)
                if fn is not None:
                    ins = fn(e)
                    if inc:
                        ins.then_inc(sems[inckey], inc)
        @block.tensor
        def _(e):
            run(e, self.prog["pe"])
        @block.scalar
        def _(e):
            run(e, self.prog["act"])
        @block.vector
        def _(e):
            run(e, self.prog["dve"])
        @block.gpsimd
        def _(e):
            run(e, self.prog["pool"])
        @block.sync
        def _(e):
            run(e, self.prog["sp"])


def own_blocks(r):
    out = []
    for kq in range(NBLK // 4):
        out += [4 * kq + (0 if r == 0 else 1), 4 * kq + (3 if r == 0 else 2)]
    return out


def other_blocks(r):
    return own_blocks(1 - r)


def _round_bits(x, bits):
    import math
    if x == 0:
        return 0.0
    e = math.floor(math.log2(abs(x)))
    q = 2.0 ** (e - bits + 1)
    return round(x / q) * q


def build_program(stage=99, debug=False):
    from contextlib import ExitStack
    nc = bass.Bass("TRN2", target_bir_lowering=False)
    dt = nc.dram_tensor
    x_in = dt("x", [S, D], F32, kind="ExternalInput").ap()
    pos_in = dt("pos", [128, NBLK], I32, kind="ExternalInput").ap()
    kch_in = dt("kch", [1, S], BF16, kind="ExternalInput").ap()
    qch_in = dt("qch", [128, NOWN], F32, kind="ExternalInput").ap()
    cst_in = dt("cst", [128, 128 + 24], F32, kind="ExternalInput").ap()
    w_names = ["ffn1_w_gate", "ffn1_w_up", "ffn1_w_down", "w_in", "w_up_attn", "w_up_sgu", "w_out",
               "ffn2_w_gate", "ffn2_w_up", "ffn2_w_down"]
    w_shapes = {"ffn1_w_gate": [D, FF], "ffn1_w_up": [D, FF], "ffn1_w_down": [FF, D], "w_in": [D, IN_COLS],
                "w_up_attn": [1024, D], "w_up_sgu": [1024, D], "w_out": [D, D],
                "ffn2_w_gate": [D, FF], "ffn2_w_up": [D, FF], "ffn2_w_down": [FF, D]}
    w_f32 = {n: dt(n, w_shapes[n], F32, kind="ExternalInput").ap() for n in w_names}
    vec_names = {"norm_ffn1": D, "norm_mix": D, "norm_ffn2": D, "q_norm": 128, "k_norm": 128,
                 "idx_k_norm": 64, "sgu_v_norm": 1024}
    vec_in = {n: dt(n, [1, l], F32, kind="ExternalInput").ap() for n, l in vec_names.items()}
    ws_in = dt("sgu_w_s", [8, 128, 128], F32, kind="ExternalInput").ap()
    bs_in = dt("sgu_b_s", [8, 128], F32, kind="ExternalInput").ap()
    out_d = dt("out", [NOWN * 128, D], F32, kind="ExternalOutput").ap()
    w_bf = {n: dt(n + "_bf", w_shapes[n], BF16).ap() for n in w_names}
    xmid_s = dt("xmid_s", [NOWN * 128, D], F32).ap()
    qT_s = dt("qT_s", [1024, NOWN * 128], BF16).ap()
    kT_s = dt("kT_s", [1024, S], BF16).ap()
    v_s = dt("v_s", [8, 128, NBLK, 129], BF16).ap()
    kiT_s = dt("kiT_s", [64, S], BF16).ap()
    qiT_s = dt("qiT_s", [1024, NOWN * 128], BF16).ap()
    sgn_s = dt("sgn_s", [NOWN * 128, 16], F32).ap()
    sgaT_s = dt("sgaT_s", [D, NOWN * 128], F32).ap()
    mbT_s = dt("mbT_s", [D, NOWN * 128], F32).ap()
    yaT_s = dt("yaT_s", [1024, NOWN * 128], BF16).ap()
    dbg = {}
    dbg_copies = []
    if debug:
        dbg["thr"] = dt("dbg_thr", [128, NOWN], F32, kind="ExternalOutput").ap()
        dbg["cnt"] = dt("dbg_cnt", [128, NOWN], F32, kind="ExternalOutput").ap()
        for nm, src in (("q", qT_s[:, 0:128]), ("k", kT_s[:, 0:128]), ("ki", kiT_s[:, 0:128]), ("qi", qiT_s[:, 0:128]),
                        ("sgn", sgn_s[0:128, :]), ("sga", sgaT_s[:, 0:128]), ("mb", mbT_s[:, 0:128]), ("ya", yaT_s[:, 0:128]),
                        ("xmid", xmid_s[0:128, :]), ("v", v_s[:, :, 0, :]), ("k1", kT_s[:, 4096:4224])):
            o = dt("dbg_" + nm, list(src.shape), src.dtype, kind="ExternalOutput").ap()
            dbg_copies.append((o, src))

    top = ExitStack()
    with top:
        def sem(name):
            return top.enter_context(nc.semaphore(name))
        k = K(nc)
        for n in ("pe", "act", "dve", "pool", "sp"):
            k.add_engine_sem(n, sem("s_" + n))
        def dsem(name):
            return k.new_dma_sem(sem("d_" + name))
        L = {}
        def lt(name, parent=None):
            if name not in L:
                L[name] = LT(name, L[parent] if parent else None)
            return L[name]
        pbank = [top.enter_context(nc.psum_tensor("pb%d" % i, [128, 512], F32)) for i in range(8)]
        def pl(i):
            return lt("pb%d" % i)

        def mm(out, lhsT, rhs, start, stop, reads, writes, sig=None):
            return k.op("pe", lambda e: e.matmul(out, lhsT, rhs, start=start, stop=stop), reads, writes,
                        sig=(stop if sig is None else sig))
        def act(out, in_, func, reads, writes, **kw):
            return k.op("act", lambda e: e.activation(out=out, in_=in_, func=func, **kw), reads, writes)
        def dve(fn, reads, writes):
            return k.op("dve", fn, reads, writes)
        def tt(out, in0, in1, op, reads, writes):
            return k.op("dve", lambda e: e.tensor_tensor(out=out, in0=in0, in1=in1, op=op), reads, writes)
        def ts(out, in0, s1, s2, op0, op1, reads, writes, **kw):
            if op1 is None:
                return k.op("dve", lambda e: e.tensor_scalar(out=out, in0=in0, scalar1=s1, scalar2=None, op0=op0, **kw), reads, writes)
            return k.op("dve", lambda e: e.tensor_scalar(out=out, in0=in0, scalar1=s1, scalar2=s2, op0=op0, op1=op1, **kw), reads, writes)
        def stt(out, in0, scalar, in1, op0, op1, reads, writes):
            return k.op("dve", lambda e: e.scalar_tensor_tensor(out=out, in0=in0, scalar=scalar, in1=in1, op0=op0, op1=op1), reads, writes)
        def dma(queue, semkey, out, in_, reads, writes, **kw):
            return k.dma(queue, semkey, lambda e: e.dma_start(out=out, in_=in_, **kw), reads, writes)

        d_cast = {n: dsem("cast_" + n) for n in w_names if not n.startswith("ffn1")}
        GU_PIECES = [(0, 1536), (1536, 3072), (3072, 4608), (4608, 5632)]
        d_misc = dsem("misc")
        d_x = dsem("xload")
        NWS = 3
        d_w = [dsem("wsl%d" % i) for i in range(NWS)]
        d_out = dsem("outst")

        for pi, (c0_, c1_) in enumerate(GU_PIECES):
            for n in ("ffn1_w_gate", "ffn1_w_up"):
                ds_ = dsem("cast_%s_p%d" % (n, pi))
                dma("pool", ds_, w_bf[n][:, c0_:c1_], w_f32[n][:, c0_:c1_], [], [lt("wbf_%s_p%d" % (n, pi))])
        for pi in range(4):
            ds_ = dsem("cast_ffn1_w_down_p%d" % pi)
            dma("pool", ds_, w_bf["ffn1_w_down"][pi * 1408:(pi + 1) * 1408, :], w_f32["ffn1_w_down"][pi * 1408:(pi + 1) * 1408, :],
                [], [lt("wbf_ffn1_w_down_p%d" % pi)])
        for n in w_names:
            if n.startswith("ffn1"):
                continue
            rows, cols = w_shapes[n]
            f = 1
            while cols // f > 2048 or cols % f:
                f += 1
            nsplit = 4 if rows >= 2048 else 1
            rs = rows // nsplit
            for i in range(nsplit):
                s_ap = w_f32[n][i * rs:(i + 1) * rs, :].rearrange("r (a b) -> r a b", a=f)
                d_ap = w_bf[n][i * rs:(i + 1) * rs, :].rearrange("r (a b) -> r a b", a=f)
                dma("pool", d_cast[n], d_ap, s_ap, [], [lt("wbf_" + n)])

        class FFNTiles:
            pass

        def alloc_ffn(es, gnames, sfx=""):
            ft = FFNTiles()
            def sb(name, shape, dtype):
                return es.enter_context(nc.sbuf_tensor(name + sfx, shape, dtype))
            ft.ident_f = sb("ident_f", [128, 128], F32)
            ft.ident = sb("ident", [128, 128], BF16)
            ft.gbc = {n: sb("g_" + n, [128, D], F32) for n in gnames}
            ft.xg = sb("xg", [128, NT, D], F32)
            ft.xn = [sb("xn%d" % i, [128, D], BF16) for i in range(2)]
            ft.hT = sb("hT", [128, KC, T], BF16)
            ft.aT = sb("aT", [128, JC, T], BF16)
            ft.wsl = [sb("wsl%d" % i, [128, 8192], BF16) for i in range(NWS)]
            ft.sgt = [sb("sgt%d" % i, [128, 512], F32) for i in range(2)]
            ft.st1 = sb("st1", [128, 8], F32)
            ft.junk = sb("junk", [128, D], BF16)
            ft.sb = sb
            ft.wrr = 0
            cl = []
            dma("sp", d_misc, ft.ident_f[:], cst_in[:, 0:128], [], [lt("ident_f")]); cl.append("ident_f")
            for n in gnames:
                dma("sp", d_misc, ft.gbc[n][:], vec_in[n].partition_broadcast(128), [], [lt("g_" + n)]); cl.append("g_" + n)
            ft.cl = cl
            return ft

        def finish_consts(ft):
            for nm in ft.cl:
                lt(nm).writers[d_misc] = k.cnt[d_misc]
            dve(lambda e: e.tensor_copy(out=ft.ident[:], in_=ft.ident_f[:]), [lt("ident_f")], [lt("ident")])

        def load_w(ft, parts, reads_lt):
            s = ft.wrr % NWS
            ft.wrr += 1
            t = lt("wsl%d" % s)
            for dst_fn, src in parts:
                dma("sp", d_w[s], dst_fn(ft.wsl[s]), src, reads_lt, [t])
            return s, t

        def transposes(ft, srcs, src_lts, dst_fn, dst_lt, nparts=128):
            n = len(srcs)
            i0 = 0
            while i0 < n:
                m = min(4, n - i0)
                bi = 6 + (ft.trr % 2)
                ft.trr += 1
                pbv = pbank[bi][:].bitcast(BF16)
                for j in range(m):
                    src = srcs[i0 + j]
                    w = src.shape[1]
                    k.op("pe", lambda e, pbv=pbv, j=j, src=src, w=w: e.transpose(
                        pbv[0:w, j * 128:(j + 1) * 128], src, ft.ident[:]),
                        list(src_lts) + [lt("ident")], [pl(bi)], sig=(j == m - 1))
                w = srcs[i0].shape[1]
                act(dst_fn(i0, m), pbv[0:w, 0:m * 128].rearrange("p (a b) -> p a b", a=m), AF.Copy, [pl(bi)], [dst_lt])
                i0 += m

        def rmsnorm_T(ft, gname):
            g_tile = ft.gbc[gname]
            st1 = ft.st1
            for t in range(NT):
                xt = ft.xg[:, t, :]
                sl = ft.xn[t % 2]
                sl_lt = lt("xn%d" % (t % 2))
                act(ft.junk[:], xt, AF.Square, [lt("xg")], [lt("junk"), lt("st1")], accum_out=st1[:, 0:1])
                act(st1[:, 1:2], st1[:, 0:1], AF.Sqrt, [lt("st1")], [lt("st1b")], scale=1.0 / D, bias=EPS)
                dve(lambda e: e.reciprocal(st1[:, 2:3], st1[:, 1:2]), [lt("st1b")], [lt("st1c")])
                stt(sl[:], xt, st1[:, 2:3], g_tile[:], ALU.mult, ALU.mult, [lt("xg"), lt("st1c"), lt("g_" + gname)], [sl_lt])
                transposes(ft, [sl[:, kc * 128:(kc + 1) * 128] for kc in range(KC)], [sl_lt],
                           lambda i0, m, t=t: ft.hT[:, i0:i0 + m, t * 128:(t + 1) * 128], lt("hT"))

        def ffn(ft, wg, wu, wd, first_readers, fine=None):
            JB = 2
            for jb in range(JC // JB):
                c0 = jb * JB * 128
                if fine:
                    pi_ = [p for p, (a_, b_) in enumerate(GU_PIECES) if a_ <= c0 < b_][0]
                    first_readers = [lt("wbf_%s_w_gate_p%d" % (fine, pi_)), lt("wbf_%s_w_up_p%d" % (fine, pi_))]
                s, wl = load_w(ft, [
                    (lambda w: w[:, 0:4096].rearrange("p (a b) -> p a b", a=KC),
                     wg[:, c0:c0 + 256].rearrange("(a p) c -> p a c", p=128)),
                    (lambda w: w[:, 4096:8192].rearrange("p (a b) -> p a b", a=KC),
                     wu[:, c0:c0 + 256].rearrange("(a p) c -> p a c", p=128))], first_readers)
                wgv = ft.wsl[s][:, 0:4096].rearrange("p (a b) -> p a b", a=KC)
                wuv = ft.wsl[s][:, 4096:8192].rearrange("p (a b) -> p a b", a=KC)
                for jj in range(JB):
                    j = jb * JB + jj
                    bi = (j % 2) * 2
                    pg, pu = pbank[bi], pbank[bi + 1]
                    for kc in range(KC):
                        mm(pg[:], wgv[:, kc, jj * 128:(jj + 1) * 128], ft.hT[:, kc, :], kc == 0, kc == KC - 1,
                           [wl, lt("hT")], [pl(bi)])
                    for kc in range(KC):
                        mm(pu[:], wuv[:, kc, jj * 128:(jj + 1) * 128], ft.hT[:, kc, :], kc == 0, kc == KC - 1,
                           [wl, lt("hT")], [pl(bi + 1)])
                    sg = ft.sgt[j % 2]
                    sgl = lt("sgt%d" % (j % 2))
                    act(sg[:], pg[:], AF.Silu, [pl(bi)], [sgl])
                    tt(ft.aT[:, j, :], sg[:], pu[:], ALU.mult, [sgl, pl(bi + 1)], [lt("aT")])
            for c in range(4):
                for jq in range(4):
                    if fine:
                        first_readers = [lt("wbf_%s_w_down_p%d" % (fine, jq))]
                    s, wl = load_w(ft, [
                        (lambda w: w[:, 0:11 * 512].rearrange("p (a b) -> p a b", a=11),
                         wd[jq * 11 * 128:(jq + 1) * 11 * 128, c * 512:(c + 1) * 512].rearrange("(a p) c -> p a c", p=128))],
                        first_readers)
                    wv = ft.wsl[s][:, 0:11 * 512].rearrange("p (a b) -> p a b", a=11)
                    for ji in range(11):
                        j = jq * 11 + ji
                        for t in range(NT):
                            bi = (c % 2) * 4 + t
                            mm(pbank[bi][:], ft.aT[:, j, t * 128:(t + 1) * 128], wv[:, ji, :], j == 0, j == JC - 1,
                               [wl, lt("aT")], [pl(bi)], sig=(j == JC - 1 or (ji == 10 and t == NT - 1)))
                for t in range(NT):
                    bi = (c % 2) * 4 + t
                    xs = ft.xg[:, t, c * 512:(c + 1) * 512]
                    stt(xs, pbank[bi][:], 0.5, xs, ALU.mult, ALU.add, [pl(bi), lt("xg")], [lt("xg")])

        p1 = ExitStack()
        with p1:
            ft = alloc_ffn(p1, ["norm_ffn1", "norm_mix"])
            ft.trr = 0
            sb = ft.sb
            gq = sb("gq", [128, 128], F32); gk = sb("gk", [128, 128], F32)
            gki = sb("gki", [128, 64], F32); gv = sb("gv", [128, 1024], F32)
            cosA = sb("cosA", [128, NBLK, 16], F32); sinA = sb("sinA", [128, NBLK, 16], F32)
            cosI = sb("cosI", [128, NBLK, 8], F32); sinI = sb("sinI", [128, NBLK, 8], F32)
            WsT = sb("WsT", [128, 8, 128], BF16)
            bsT = sb("bsT", [128, 8], F32)
            invf = sb("invf", [128, 24], F32)
            posi = sb("posi", [128, NBLK], I32)
            st2 = sb("st2", [128, 16], F32)
            wabs = sb("wabs", [128, NT, 16], F32)
            sgn_st = sb("sgn_st", [128, NT, 16], F32)
            rr = [sb("rr%d" % i, [128, 64], F32) for i in range(4)]
            kiT_st = sb("kiT_st", [64, T], BF16)
            kib = sb("kib", [128, 64], BF16)
            for (tile_, nm) in ((gq, "q_norm"), (gk, "k_norm"), (gki, "idx_k_norm"), (gv, "sgu_v_norm")):
                dma("sp", d_misc, tile_[:], vec_in[nm].partition_broadcast(128), [], [lt("c_" + nm)]); ft.cl.append("c_" + nm)
            dma("sp", d_misc, invf[:], cst_in[:, 128:152], [], [lt("invf")]); ft.cl.append("invf")
            dma("sp", d_misc, posi[:], pos_in[:], [], [lt("posi")]); ft.cl.append("posi")
            dma("sp", d_misc, bsT[:], bs_in.rearrange("g t -> t g"), [], [lt("bsT")], allow_slow_non_contiguous=True); ft.cl.append("bsT")
            finish_consts(ft)
            aflat = ft.aT[:].rearrange("p a b -> p (a b)")
            xflat = ft.xg[:].rearrange("p a b -> p (a b)")
            class Arena:
                def __init__(self, flat, esz, parent):
                    self.flat, self.esz, self.off, self.parent = flat, esz, 0, parent
                def take(self, name, shape, dtype):
                    n = int(np.prod(shape[1:])) * (4 if dtype == F32 else 2)
                    assert self.off + n <= self.flat.shape[1] * self.esz, (name, self.off, n)
                    v = self.flat[:, self.off // self.esz:(self.off + n) // self.esz]
                    if self.esz == 2 and dtype == F32:
                        v = v.bitcast(F32)
                    if self.esz == 4 and dtype == BF16:
                        v = v.bitcast(BF16)
                    self.off += n
                    lt(name, self.parent)
                    if len(shape) == 3:
                        v = v.rearrange("p (a b) -> p a b", a=shape[1])
                    return v
            A1 = Arena(aflat, 2, "aT")
            A2 = Arena(xflat, 4, "xg")
            lt("aT"); lt("xg")
            vsn = A1.take("vsn", [128, NT, 1024], BF16)
            ybT = A1.take("ybT", [128, 8, T], BF16)
            qT_st = A1.take("qT_st", [128, 8, T], BF16)
            kT_st = A1.take("kT_st", [128, 8, T], BF16)
            ost = A1.take("ost", [128, 4, 512], F32)
            yb = A1.take("yb", [128, 1024], BF16)
            qb = A1.take("qb", [128, 512], BF16)
            qb2 = A1.take("qb2", [128, 512], BF16)
            qbs = [(qb, "qb"), (qb2, "qb2")]
            v_st = A2.take("v_st", [128, NT * 8 * 129], BF16).rearrange("p (t h d) -> p t h d", t=NT, h=8)
            qiT_st = A2.take("qiT_st", [128, 8, T], BF16)
            tmpA = A2.take("tmpA", [128, 1024], F32)
            tmpB = A2.take("tmpB", [128, 1024], F32)
            tmpC = A2.take("tmpC", [128, 1024], F32)
            sgb = A2.take("sgb", [128, 512], F32)
            d_st = {n: dsem("st_" + n) for n in ("xmid", "qT", "kT", "v", "kiT", "qiT", "sgn", "ost")}

            TWO_PI = 2.0 * np.pi
            C1 = 6.28125
            C2 = _round_bits(TWO_PI - C1, 9)
            C3 = float(np.float32(TWO_PI - C1 - C2))
            MAGIC = 12582912.0
            PI_LO = 3.1415925
            posf = sb("posf", [128, NBLK], F32)
            dve(lambda e: e.tensor_copy(out=posf[:], in_=posi[:]), [lt("posi")], [lt("posf")])
            def sincos(nf, f0, cos_t, sin_t):
                n = NBLK * nf
                ang = tmpA[:, 0:n].rearrange("p (a b) -> p a b", a=NBLK)
                tt(ang, posf[:].unsqueeze(2).broadcast_to([128, NBLK, nf]),
                   invf[:, f0:f0 + nf].unsqueeze(1).broadcast_to([128, NBLK, nf]), ALU.mult,
                   [lt("posf"), lt("invf")], [lt("tmpA")])
                angf = tmpA[:, 0:n]
                kk = tmpB[:, 0:n]
                r = tmpC[:, 0:n]
                for (shift, dst) in ((0.0, sin_t), (0.25, cos_t)):
                    ts(kk, angf, 1.0 / TWO_PI, shift, ALU.mult, ALU.add, [lt("tmpA")], [lt("tmpB")])
                    ts(kk, kk, MAGIC, None, ALU.add, None, [lt("tmpB")], [lt("tmpB")])
                    ts(kk, kk, -MAGIC, None, ALU.add, None, [lt("tmpB")], [lt("tmpB")])
                    stt(r, kk, -C1, angf, ALU.mult, ALU.add, [lt("tmpB"), lt("tmpA")], [lt("tmpC")])
                    stt(r, kk, -C2, r, ALU.mult, ALU.add, [lt("tmpB"), lt("tmpC")], [lt("tmpC")])
                    stt(r, kk, -C3, r, ALU.mult, ALU.add, [lt("tmpB"), lt("tmpC")], [lt("tmpC")])
                    if shift:
                        ts(r, r, float(np.pi / 2), None, ALU.add, None, [lt("tmpC")], [lt("tmpC")])
                    ts(kk, r, PI_LO, -TWO_PI, ALU.is_gt, ALU.mult, [lt("tmpC")], [lt("tmpB")])
                    tt(r, r, kk, ALU.add, [lt("tmpC"), lt("tmpB")], [lt("tmpC")])
                    ts(kk, r, -PI_LO, TWO_PI, ALU.is_lt, ALU.mult, [lt("tmpC")], [lt("tmpB")])
                    tt(r, r, kk, ALU.add, [lt("tmpC"), lt("tmpB")], [lt("tmpC")])
                    ts(r, r, PI_LO, -PI_LO, ALU.min, ALU.max, [lt("tmpC")], [lt("tmpC")])
                    act(dst[:].rearrange("p a b -> p (a b)"), r, AF.Sin, [lt("tmpC")], [lt("rope")])
            sincos(16, 0, cosA, sinA)
            sincos(8, 16, cosI, sinI)
            for g8 in range(8):
                wtmp = tmpA[:, 0:128]
                dma("sp", d_misc, wtmp, ws_in[g8], [], [lt("tmpA")])
                dve(lambda e: e.memset(tmpA[0:64, 64:128], 0.0), [lt("tmpA")], [lt("tmpA")])
                dve(lambda e: e.tensor_copy(out=qb[:, 0:128], in_=tmpA[:, 0:128]), [lt("tmpA")], [lt("qb")])
                transposes(ft, [qb[:, 0:128]], [lt("qb")], lambda i0, m, g8=g8: WsT[:, g8:g8 + 1, :], lt("WsT"))

            def rope_apply(src3, dst3, nh, half, cos2, sin2, src_lt, dst_lt):
                x1 = src3[:, :, 0:half]; x2 = src3[:, :, half:2 * half]
                cb = cos2.unsqueeze(1).broadcast_to([128, nh, half])
                sbb = sin2.unsqueeze(1).broadcast_to([128, nh, half])
                rv = [rr[i][:, 0:nh * half].rearrange("p (a b) -> p a b", a=nh) for i in range(4)]
                tt(rv[0], x1, cb, ALU.mult, [src_lt, lt("rope")], [lt("rr0")])
                tt(rv[1], x2, sbb, ALU.mult, [src_lt, lt("rope")], [lt("rr1")])
                tt(rv[2], x2, cb, ALU.mult, [src_lt, lt("rope")], [lt("rr2")])
                tt(rv[3], x1, sbb, ALU.mult, [src_lt, lt("rope")], [lt("rr3")])
                tt(dst3[:, :, 0:half], rv[0], rv[1], ALU.subtract, [lt("rr0"), lt("rr1")], [dst_lt])
                tt(dst3[:, :, half:2 * half], rv[2], rv[3], ALU.add, [lt("rr2"), lt("rr3")], [dst_lt])

            prr = [0]
            def proj_chunk(col0, ncols, epilogue, wname="w_in", kcs=KC, hsrc=None, hsrc_lt=None):
                W = w_bf[wname]
                s, wl = load_w(ft, [(lambda w: w[:, 0:kcs * ncols].rearrange("p (a b) -> p a b", a=kcs),
                                     W[:, col0:col0 + ncols].rearrange("(a p) c -> p a c", p=128))], [lt("wbf_" + wname)])
                wv = ft.wsl[s][:, 0:kcs * ncols].rearrange("p (a b) -> p a b", a=kcs)
                pending = None
                for t in range(NT):
                    bi = prr[0] % 6
                    prr[0] += 1
                    for kc in range(kcs):
                        mm(pbank[bi][:, 0:ncols], ft.hT[:, kc, t * 128:(t + 1) * 128], wv[:, kc, :], kc == 0, kc == kcs - 1,
                           [wl, lt("hT")], [pl(bi)])
                    if pending is not None:
                        pending()
                    pending = epilogue(t, pbank[bi], pl(bi))
                if pending is not None:
                    pending()

            def qk_epilogue(gvec, gname, stT, st_lt, c, blk0):
                def ep(t, pb, pbl):
                    psv = pb[:].rearrange("p (a b) -> p a b", a=4)
                    act(tmpA[:, 0:512], pb[:], AF.Square, [pbl], [lt("tmpA")])
                    dve(lambda e: e.tensor_reduce(out=st2[:, 0:4], in_=tmpA[:, 0:512].rearrange("p (a b) -> p a b", a=4),
                                                  axis=AX.X, op=ALU.add), [lt("tmpA")], [lt("st2")])
                    act(st2[:, 4:8], st2[:, 0:4], AF.Sqrt, [lt("st2")], [lt("st2b")], scale=1.0 / 128, bias=EPS)
                    dve(lambda e: e.reciprocal(st2[:, 8:12], st2[:, 4:8]), [lt("st2b")], [lt("st2c")])
                    tb = tmpB[:, 0:512].rearrange("p (a b) -> p a b", a=4)
                    tt(tb, psv, st2[:, 8:12].unsqueeze(2).broadcast_to([128, 4, 128]), ALU.mult, [pbl, lt("st2c")], [lt("tmpB")])
                    tt(tb, tb, gvec[:].unsqueeze(1).broadcast_to([128, 4, 128]), ALU.mult, [lt("tmpB"), lt("c_" + gname)], [lt("tmpB")])
                    qb_, qbn = qbs[t % 2]
                    qbv = qb_.rearrange("p (a b) -> p a b", a=4)
                    act(qb_, tmpB[:, 0:512], AF.Copy, [lt("tmpB")], [lt(qbn)])
                    rope_apply(tb, qbv, 4, 16, cosA[:, blk0 + t, :], sinA[:, blk0 + t, :], lt("tmpB"), lt(qbn))
                    def deferred(t=t, qb_=qb_, qbn=qbn):
                        transposes(ft, [qb_[:, j * 128:(j + 1) * 128] for j in range(4)], [lt(qbn)],
                                   lambda i0, m, t=t: stT[:, 4 * c + i0:4 * c + i0 + m, t * 128:(t + 1) * 128], st_lt)
                    return deferred
                return ep

            def gelu_to(dst, pb, pbl, dst_lt, ncols=512):
                act(tmpB[:, 0:ncols], pb[:, 0:ncols], AF.Square, [pbl], [lt("tmpB")])
                ts(tmpB[:, 0:ncols], tmpB[:, 0:ncols], 0.044715, 1.0, ALU.mult, ALU.add, [lt("tmpB")], [lt("tmpB")])
                tt(tmpB[:, 0:ncols], tmpB[:, 0:ncols], pb[:, 0:ncols], ALU.mult, [lt("tmpB"), pbl], [lt("tmpB")])
                act(tmpB[:, 0:ncols], tmpB[:, 0:ncols], AF.Sigmoid, [lt("tmpB")], [lt("tmpB")], scale=1.5957691216057308)
                tt(dst, tmpB[:, 0:ncols], pb[:, 0:ncols], ALU.mult, [lt("tmpB"), pbl], [dst_lt])

            ngroups = NG_OWN if stage <= 1 else NG_ALL
            for g in range(ngroups):
                own = g < NG_OWN
                blk0 = g * NT
                dma("sp", d_x, ft.xg[:], x_in[g * T:(g + 1) * T, :].rearrange("(t p) d -> p t d", p=128), [], [lt("xg")])
                rmsnorm_T(ft, "norm_ffn1")
                ffn(ft, w_bf["ffn1_w_gate"], w_bf["ffn1_w_up"], w_bf["ffn1_w_down"], [], fine="ffn1")
                if stage <= 1:
                    dma("pool", d_out, out_d[g * T:(g + 1) * T, :].rearrange("(t p) d -> p t d", p=128), ft.xg[:],
                        [lt("xg")], [lt("out")])
                    continue
                if own:
                    dma("pool", d_st["xmid"], xmid_s[g * T:(g + 1) * T, :].rearrange("(t p) d -> p t d", p=128), ft.xg[:],
                        [lt("xg")], [lt("xmid_s")])
                rmsnorm_T(ft, "norm_mix")
                def kiwi_ep(t, pb, pbl):
                    act(tmpA[:, 0:64], pb[:, 0:64], AF.Square, [pbl], [lt("tmpA"), lt("st2")], accum_out=st2[:, 0:1])
                    act(st2[:, 4:5], st2[:, 0:1], AF.Sqrt, [lt("st2")], [lt("st2b")], scale=1.0 / 64, bias=EPS)
                    dve(lambda e: e.reciprocal(st2[:, 8:9], st2[:, 4:5]), [lt("st2b")], [lt("st2c")])
                    stt(tmpB[:, 0:64], pb[:, 0:64], st2[:, 8:9], gki[:], ALU.mult, ALU.mult, [pbl, lt("st2c"), lt("c_idx_k_norm")], [lt("tmpB")])
                    dve(lambda e: e.tensor_copy(out=kib[:], in_=tmpB[:, 0:64]), [lt("tmpB")], [lt("kib")])
                    rope_apply(tmpB[:, 0:64].rearrange("p (a b) -> p a b", a=1), kib[:].rearrange("p (a b) -> p a b", a=1),
                               1, 8, cosI[:, blk0 + t, :], sinI[:, blk0 + t, :], lt("tmpB"), lt("kib"))
                    transposes(ft, [kib[:, 0:64]], [lt("kib")],
                               lambda i0, m, t=t: kiT_st[:, t * 128:(t + 1) * 128].rearrange("p (a b) -> p a b", a=1), lt("kiT_st"))
                    if own:
                        sc = (16 ** -0.5) * (64 ** -0.5)
                        act(wabs[:, t, :], pb[:, 64:80], AF.Abs, [pbl], [lt("wabs")], scale=sc)
                        ts(sgn_st[:, t, :], pb[:, 64:80], 0.0, 2.0, ALU.is_ge, ALU.mult, [pbl], [lt("sgn_st")])
                        ts(sgn_st[:, t, :], sgn_st[:, t, :], -1.0, None, ALU.add, None, [lt("sgn_st")], [lt("sgn_st")])
                proj_chunk(C_KI, 80, kiwi_ep)
                dma("pool", d_st["kiT"], kiT_s[:, g * T:(g + 1) * T], kiT_st[:], [lt("kiT_st")], [lt("kiT_s")])
                if own:
                    dma("pool", d_st["sgn"], sgn_s[g * T:(g + 1) * T, :].rearrange("(t p) c -> p t c", p=128), sgn_st[:],
                        [lt("sgn_st")], [lt("sgn_s")])
                if own:
                    for c in range(2):
                        proj_chunk(C_Q + c * 512, 512, qk_epilogue(gq, "q_norm", qT_st, lt("qT_st"), c, blk0))
                    dma("pool", d_st["qT"], qT_s.rearrange("(h d) n -> d h n", d=128)[:, :, g * T:(g + 1) * T], qT_st,
                        [lt("qT_st")], [lt("qT_s")])
                for c in range(2):
                    proj_chunk(C_K + c * 512, 512, qk_epilogue(gk, "k_norm", kT_st, lt("kT_st"), c, blk0))
                dma("pool", d_st["kT"], kT_s.rearrange("(h d) n -> d h n", d=128)[:, :, g * T:(g + 1) * T], kT_st,
                    [lt("kT_st")], [lt("kT_s")])
                for t4 in range(NT):
                    dve(lambda e, t4=t4: e.memset(v_st[:, t4, :, 128:129], 1.0), [], [lt("v_st")])
                for c in range(2):
                    def v_ep(t, pb, pbl, c=c):
                        act(v_st[:, t, 4 * c:4 * c + 4, 0:128], pb[:].rearrange("p (a b) -> p a b", a=4), AF.Copy, [pbl], [lt("v_st")])
                    proj_chunk(C_V + c * 512, 512, v_ep)
                for h8 in range(8):
                    dma("pool", d_st["v"], v_s[h8, :, blk0:blk0 + NT, :], v_st[:, :, h8, :],
                        [lt("v_st")], [lt("v_s")])
                if not own:
                    continue
                for c in range(2):
                    def qi_ep(t, pb, pbl, c=c):
                        act(tmpA[:, 0:512], pb[:], AF.Copy, [pbl], [lt("tmpA")])
                        ta = tmpA[:, 0:512].rearrange("p (a b) -> p a b", a=8)
                        tbv = tmpB[:, 0:512].rearrange("p (a b) -> p a b", a=8)
                        dve(lambda e: e.tensor_copy(out=tmpB[:, 0:512], in_=tmpA[:, 0:512]), [lt("tmpA")], [lt("tmpB")])
                        rope_apply(ta, tbv, 8, 8, cosI[:, blk0 + t, :], sinI[:, blk0 + t, :], lt("tmpA"), lt("tmpB"))
                        qb_, qbn = qbs[t % 2]
                        qbv = qb_.rearrange("p (a b) -> p a b", a=8)
                        tt(qbv, tbv, wabs[:, t, 8 * c:8 * c + 8].unsqueeze(2).broadcast_to([128, 8, 64]), ALU.mult,
                           [lt("tmpB"), lt("wabs")], [lt(qbn)])
                        def deferred(t=t, qb_=qb_, qbn=qbn, c=c):
                            transposes(ft, [qb_[:, j * 128:(j + 1) * 128] for j in range(4)], [lt(qbn)],
                                       lambda i0, m, t=t: qiT_st[:, 4 * c + i0:4 * c + i0 + m, t * 128:(t + 1) * 128], lt("qiT_st"))
                        return deferred
                    proj_chunk(C_QI + c * 512, 512, qi_ep)
                dma("pool", d_st["qiT"], qiT_s.rearrange("(h d) n -> d h n", d=128)[:, :, g * T:(g + 1) * T], qiT_st,
                    [lt("qiT_st")], [lt("qiT_s")])
                vsacc = {}
                for c in range(2):
                    def vs_ep(t, pb, pbl, c=c):
                        gelu_to(tmpA[:, c * 512:(c + 1) * 512], pb, pbl, lt("tmpA"))
                        if c == 1:
                            act(tmpC[:, 0:1024], tmpA[:, 0:1024], AF.Square, [lt("tmpA")], [lt("tmpC"), lt("st2")], accum_out=st2[:, 0:1])
                            act(st2[:, 4:5], st2[:, 0:1], AF.Sqrt, [lt("st2")], [lt("st2b")], scale=1.0 / 1024, bias=EPS)
                            dve(lambda e: e.reciprocal(st2[:, 8:9], st2[:, 4:5]), [lt("st2b")], [lt("st2c")])
                            stt(vsn[:, t, :], tmpA[:, 0:1024], st2[:, 8:9], gv[:], ALU.mult, ALU.mult,
                                [lt("tmpA"), lt("st2c"), lt("c_sgu_v_norm")], [lt("vsn")])
                    vsacc[c] = vs_ep
                Wn = w_bf["w_in"]
                sl = []
                for c in range(2):
                    s_, wl_ = load_w(ft, [(lambda w: w[:, 0:KC * 512].rearrange("p (a b) -> p a b", a=KC),
                                           Wn[:, C_VS + c * 512:C_VS + (c + 1) * 512].rearrange("(a p) c -> p a c", p=128))], [lt("wbf_w_in")])
                    sl.append((s_, wl_))
                for t in range(NT):
                    for c in range(2):
                        s_, wl_ = sl[c]
                        wv = ft.wsl[s_][:, 0:KC * 512].rearrange("p (a b) -> p a b", a=KC)
                        bi = prr[0] % 6; prr[0] += 1
                        for kc in range(KC):
                            mm(pbank[bi][:], ft.hT[:, kc, t * 128:(t + 1) * 128], wv[:, kc, :], kc == 0, kc == KC - 1, [wl_, lt("hT")], [pl(bi)])
                        vsacc[c](t, pbank[bi], pl(bi))
                sl = []
                for c in range(2):
                    s_, wl_ = load_w(ft, [(lambda w: w[:, 0:KC * 512].rearrange("p (a b) -> p a b", a=KC),
                                           Wn[:, C_U + c * 512:C_U + (c + 1) * 512].rearrange("(a p) c -> p a c", p=128))], [lt("wbf_w_in")])
                    sl.append((s_, wl_))
                for t in range(NT):
                    for c in range(2):
                        s_, wl_ = sl[c]
                        wv = ft.wsl[s_][:, 0:KC * 512].rearrange("p (a b) -> p a b", a=KC)
                        bi = prr[0] % 6; prr[0] += 1
                        for kc in range(KC):
                            mm(pbank[bi][:], ft.hT[:, kc, t * 128:(t + 1) * 128], wv[:, kc, :], kc == 0, kc == KC - 1, [wl_, lt("hT")], [pl(bi)])
                        gelu_to(tmpA[:, c * 512:(c + 1) * 512], pbank[bi], pl(bi), lt("tmpA"))
                    for hb in range(2):
                        bi = prr[0] % 6; prr[0] += 1
                        for g4 in range(4):
                            g8 = hb * 4 + g4
                            mm(pbank[bi][:, g4 * 128:(g4 + 1) * 128], WsT[:, g8, :], vsn[:, t, g8 * 128:(g8 + 1) * 128], True, True,
                               [lt("WsT"), lt("vsn")], [pl(bi)], sig=(g4 == 3))
                        for g4 in range(4):
                            g8 = hb * 4 + g4
                            stt(yb[:, g8 * 128:(g8 + 1) * 128], pbank[bi][:, g4 * 128:(g4 + 1) * 128], bsT[:, g8:g8 + 1],
                                tmpA[:, g8 * 128:(g8 + 1) * 128], ALU.add, ALU.mult, [pl(bi), lt("bsT"), lt("tmpA")], [lt("yb")])
                    transposes(ft, [yb[:, j * 128:(j + 1) * 128] for j in range(8)], [lt("yb")],
                               lambda i0, m, t=t: ybT[:, i0:i0 + m, t * 128:(t + 1) * 128], lt("ybT"))
                def fm_chunk(col0, wname, kcs, rhs_fn, rhs_lt, consume):
                    W = w_bf[wname]
                    s_, wl_ = load_w(ft, [(lambda w: w[:, 0:kcs * 512].rearrange("p (a b) -> p a b", a=kcs),
                                           W[:, col0:col0 + 512].rearrange("(a p) c -> p a c", p=128))], [lt("wbf_" + wname)])
                    wv = ft.wsl[s_][:, 0:kcs * 512].rearrange("p (a b) -> p a b", a=kcs)
                    for fl in range(4):
                        bi = prr[0] % 6; prr[0] += 1
                        for kc in range(kcs):
                            mm(pbank[bi][:], wv[:, kc, fl * 128:(fl + 1) * 128], rhs_fn(kc), kc == 0, kc == kcs - 1, [wl_, rhs_lt], [pl(bi)])
                        consume(fl, pbank[bi], pl(bi))
                for c in range(4):
                    def ga_c(fl, pb, pbl):
                        act(ost[:, fl, :], pb[:], AF.Sigmoid, [pbl], [lt("ost")])
                    fm_chunk(C_GA + c * 512, "w_in", KC, lambda kc: ft.hT[:, kc, :], lt("hT"), ga_c)
                    dma("pool", d_st["ost"], sgaT_s[c * 512:(c + 1) * 512, g * T:(g + 1) * T].rearrange("(f p) n -> p f n", p=128),
                        ost, [lt("ost")], [lt("sgaT_s")])
                for c in range(4):
                    W = w_bf["w_in"]
                    s1, wl1 = load_w(ft, [(lambda w: w[:, 0:KC * 512].rearrange("p (a b) -> p a b", a=KC),
                                           W[:, C_GB + c * 512:C_GB + (c + 1) * 512].rearrange("(a p) c -> p a c", p=128))], [lt("wbf_w_in")])
                    W2 = w_bf["w_up_sgu"]
                    s2, wl2 = load_w(ft, [(lambda w: w[:, 0:8 * 512].rearrange("p (a b) -> p a b", a=8),
                                           W2[:, c * 512:(c + 1) * 512].rearrange("(a p) c -> p a c", p=128))], [lt("wbf_w_up_sgu")])
                    wv1 = ft.wsl[s1][:, 0:KC * 512].rearrange("p (a b) -> p a b", a=KC)
                    wv2 = ft.wsl[s2][:, 0:8 * 512].rearrange("p (a b) -> p a b", a=8)
                    for fl in range(4):
                        b1 = prr[0] % 6; prr[0] += 1
                        for kc in range(KC):
                            mm(pbank[b1][:], wv1[:, kc, fl * 128:(fl + 1) * 128], ft.hT[:, kc, :], kc == 0, kc == KC - 1, [wl1, lt("hT")], [pl(b1)])
                        act(sgb[:], pbank[b1][:], AF.Sigmoid, [pl(b1)], [lt("sgb")])
                        b2 = prr[0] % 6; prr[0] += 1
                        for kc in range(8):
                            mm(pbank[b2][:], wv2[:, kc, fl * 128:(fl + 1) * 128], ybT[:, kc, :], kc == 0, kc == 7, [wl2, lt("ybT")], [pl(b2)])
                        tt(ost[:, fl, :], pbank[b2][:], sgb[:], ALU.mult, [pl(b2), lt("sgb")], [lt("ost")])
                    dma("pool", d_st["ost"], mbT_s[c * 512:(c + 1) * 512, g * T:(g + 1) * T].rearrange("(f p) n -> p f n", p=128),
                        ost, [lt("ost")], [lt("mbT_s")])
            k.barrier()
        if stage <= 1:
            with nc.Block() as block:
                k.replay(block)
            return nc
        p2 = ExitStack()
        with p2:
            def sb(name, shape, dtype):
                return p2.enter_context(nc.sbuf_tensor(name, shape, dtype))
            identf2 = sb("identf2", [128, 128], F32)
            ident2 = sb("ident2", [128, 128], BF16)
            kiT2 = sb("kiT2", [128, S], BF16)
            kc2 = sb("kc2", [128, 2, 128], BF16)
            qch = sb("qch_sb", [128, NOWN], F32)
            score2 = [sb("score%d" % i_, [128, S], F32) for i_ in range(2)]
            mask01 = sb("mask01", [128, S], BF16)
            maskT = sb("maskT", [128, NBLK, 128], BF16)
            qiq = sb("qiq", [128, 8, 128], BF16)
            sgq = sb("sgq", [128, 16], F32)
            qTq = sb("qTq", [128, 8, 128], BF16)
            kTh = [sb("kTh%d" % i, [128, S], BF16) for i in range(2)]
            Vh = [sb("Vh%d" % i, [128, NBLK, 129], BF16) for i in range(2)]
            pT = [sb("pT%d" % i, [128, 512], BF16) for i in range(2)]
            pmT = [sb("pmT%d" % i, [128, 512], BF16) for i in range(2)]
            ya = sb("ya", [128, 1024], BF16)
            yaT_st = sb("yaT_st", [128, 8, 128], BF16)
            bst = sb("bst", [128, 16], F32)
            thr_all = sb("thr_all", [128, NOWN], F32)
            cnt_all = sb("cnt_all", [128, NOWN], F32)
            d2 = {n: dsem("p2_" + n) for n in ("c", "qiq", "sgq", "qTq", "kT0", "kT1", "V0", "V1", "yaT", "kc")}
            p1_out = [lt(n) for n in ("kiT_s", "kT_s", "v_s", "qT_s", "qiT_s", "sgn_s")]
            cl = []
            dma("sp", d2["c"], identf2[:], cst_in[:, 0:128], [], [lt("identf2")]); cl.append("identf2")
            dma("sp", d2["c"], kiT2[0:64, :], kiT_s[:, :], p1_out, [lt("kiT2")]); cl.append("kiT2")
            dma("sp", d2["c"], kiT2[64:128, :], kiT_s[:, :], p1_out, [lt("kiT2")])
            dma("sp", d2["c"], qch[:], qch_in[:], [], [lt("qch")]); cl.append("qch")
            for nm in cl:
                lt(nm).writers[d2["c"]] = k.cnt[d2["c"]]
            dve(lambda e: e.tensor_copy(out=ident2[:], in_=identf2[:]), [lt("identf2")], [lt("ident2")])
            R0 = 16.0
            NIT = 28
            SCALE = 128.0 ** -0.5
            hrr = [0]
            irr = [0]
            osb = sb("osb", [128, 8, 132], F32)
            NB2 = NOWN if stage >= 3 else 0

            def geom(i):
                n1 = (i + 1) * 128
                return n1, 2 * n1, 2 * (i + 1)

            Dg = sb("Dg", [128, 16, 128], BF16)
            rtb = [sb("rtb%d" % i_, [128, 512], BF16) for i_ in range(4)]
            negbb = [sb("negbb%d" % i_, [128, 128], BF16) for i_ in range(2)]
            crr = [0]
            LAG = 2

            def idx_stage(i):
                n1, nk, nkb = geom(i)
                score = score2[i % 2]
                sl_ = lt("score%d" % (i % 2))
                segs = [(0, 0, n1), (n1, NOWN * 128, n1)]
                dma("sp", d2["qiq"], qiq[:], qiT_s.rearrange("(h d) n -> d h n", d=128)[:, :, i * 128:(i + 1) * 128], p1_out, [lt("qiq")])
                dma("sp", d2["sgq"], sgq[:], sgn_s[i * 128:(i + 1) * 128, :], p1_out, [lt("sgq")])
                for si_ in range(2):
                    dma("sp", d2["kc"], kc2[:, si_, :], kch_in[:, si_ * NOWN * 128 + i * 128:si_ * NOWN * 128 + (i + 1) * 128].partition_broadcast(128),
                        [], [lt("kc2")])
                k.op("pool", lambda e: e.tensor_tensor(
                    out=Dg[:], in0=ident2[:].unsqueeze(1).broadcast_to([128, 16, 128]),
                    in1=sgq[:].unsqueeze(2).broadcast_to([128, 16, 128]), op=ALU.mult),
                    [lt("ident2"), lt("sgq")], [lt("Dg")])
                for si_, (l0, c0, n) in enumerate(segs):
                    off = 0
                    while off < n:
                        w = min(512, n - off)
                        cc = crr[0] % 2; crr[0] += 1
                        need_bias = (off + w == n)
                        nb_ = negbb[cc]
                        if need_bias:
                            kv = kc2[:, si_, :]
                            k.op("pool", lambda e, nb_=nb_, kv=kv, i=i: e.tensor_scalar(
                                out=nb_[:, 0:128], in0=kv, scalar1=qch[:, i:i + 1], scalar2=NEG, op0=ALU.is_gt, op1=ALU.mult),
                                [lt("kc2"), lt("qch")], [lt("negbb%d" % cc)])
                        sbk = 3 + cc
                        for hh in range(16 + LAG):
                            if hh < 16:
                                h = hh
                                bi = irr[0] % 3; irr[0] += 1
                                pb0 = (h % 2) * 64
                                mm(pbank[bi][:, 0:w], qiq[pb0:pb0 + 64, h // 2, :], kiT2[pb0:pb0 + 64, c0 + off:c0 + off + w], True, True,
                                   [lt("qiq"), lt("kiT2")], [pl(bi)])
                                act(rtb[h % 4][:, 0:w], pbank[bi][:, 0:w], AF.Relu, [pl(bi)], [lt("rtb%d" % (h % 4))])
                            h2 = hh - LAG
                            if h2 >= 0:
                                last = (h2 == 15) and not need_bias
                                mm(pbank[sbk][:, 0:w], Dg[:, h2, :], rtb[h2 % 4][:, 0:w], h2 == 0, last,
                                   [lt("Dg"), lt("rtb%d" % (h2 % 4))], [pl(sbk)], sig=True)
                        if need_bias:
                            mm(pbank[sbk][:, w - 128:w], ident2[:], nb_[:, 0:128], False, True, [lt("ident2"), lt("negbb%d" % cc)], [pl(sbk)], sig=True)
                        act(score[:, l0 + off:l0 + off + w], pbank[sbk][:, 0:w], AF.Copy, [pl(sbk)], [sl_])
                        off += w

            def bis_stage(i):
                n1, nk, nkb = geom(i)
                score = score2[i % 2]
                sl_ = lt("score%d" % (i % 2))
                dve(lambda e: e.memset(bst[:, 0:1], 0.0), [], [lt("bst_t")])
                for it in range(NIT):
                    step = R0 / (2 ** it)
                    ts(mask01[:, 0:nk], score[:, 0:nk], bst[:, 0:1], None, ALU.is_ge, ALU.add, [sl_, lt("bst_t")],
                       [lt("mask01"), lt("bst_c")], accum_out=bst[:, 1:2])
                    ts(bst[:, 2:3], bst[:, 1:2], TOPK - 0.5, step, ALU.is_ge, ALU.mult, [lt("bst_c")], [lt("bst_m")])
                    dec = -step if it == NIT - 1 else -step / 2
                    stt(bst[:, 0:1], bst[:, 2:3], dec, bst[:, 0:1], ALU.add, ALU.add, [lt("bst_m"), lt("bst_t")], [lt("bst_t")])
                ts(mask01[:, 0:nk], score[:, 0:nk], bst[:, 0:1], None, ALU.is_ge, ALU.add, [sl_, lt("bst_t")],
                   [lt("mask01"), lt("bst_c")], accum_out=bst[:, 1:2])
                if debug:
                    dve(lambda e, i=i: e.tensor_copy(out=thr_all[:, i:i + 1], in_=bst[:, 0:1]), [lt("bst_t")], [lt("thr_all")])
                    dve(lambda e, i=i: e.tensor_copy(out=cnt_all[:, i:i + 1], in_=bst[:, 1:2]), [lt("bst_c")], [lt("cnt_all")])

            def mT_stage(i):
                n1, nk, nkb = geom(i)
                kb = 0
                while kb < nkb:
                    m = min(4, nkb - kb)
                    pbv = pbank[4][:].bitcast(BF16)
                    for j in range(m):
                        k.op("pe", lambda e, pbv=pbv, j=j, kb=kb: e.transpose(
                            pbv[:, j * 128:(j + 1) * 128], mask01[:, (kb + j) * 128:(kb + j + 1) * 128], ident2[:]),
                            [lt("mask01"), lt("ident2")], [pl(4)], sig=(j == m - 1))
                    act(maskT[:, kb:kb + m, :], pbv[:, 0:m * 128].rearrange("p (a b) -> p a b", a=m), AF.Copy, [pl(4)], [lt("maskT")])
                    kb += m

            def att_stage(i):
                n1, nk, nkb = geom(i)
                dma("sp", d2["qTq"], qTq[:], qT_s.rearrange("(h d) n -> d h n", d=128)[:, :, i * 128:(i + 1) * 128], p1_out, [lt("qTq")])
                for h in range(8):
                    s_ = hrr[0] % 2; hrr[0] += 1
                    kl, vl = lt("kTh%d" % s_), lt("Vh%d" % s_)
                    dma("sp", d2["kT%d" % s_], kTh[s_][:, 0:n1], kT_s[h * 128:(h + 1) * 128, 0:n1], p1_out, [kl])
                    dma("sp", d2["kT%d" % s_], kTh[s_][:, n1:nk], kT_s[h * 128:(h + 1) * 128, NOWN * 128:NOWN * 128 + n1], p1_out, [kl])
                    dma("sp", d2["V%d" % s_], Vh[s_][:, 0:i + 1, :], v_s[h, :, 0:i + 1, :], p1_out, [vl])
                    dma("sp", d2["V%d" % s_], Vh[s_][:, i + 1:nkb, :], v_s[h, :, NOWN:NOWN + i + 1, :], p1_out, [vl])
                    kb = 0
                    gi = 0
                    while kb < nkb:
                        m = min(4, nkb - kb)
                        lb = 5 + (gi % 2)
                        for j in range(m):
                            mm(pbank[lb][:, j * 128:(j + 1) * 128], kTh[s_][:, (kb + j) * 128:(kb + j + 1) * 128], qTq[:, h, :], True, True,
                               [kl, lt("qTq")], [pl(lb)], sig=(j == m - 1))
                        p_ = pT[gi % 2]; pm_ = pmT[gi % 2]
                        act(p_[:, 0:m * 128], pbank[lb][:, 0:m * 128], AF.Exp, [pl(lb)], [lt("pT%d" % (gi % 2))], scale=SCALE)
                        mview = maskT[:, kb:kb + m, :].rearrange("p a b -> p (a b)")
                        k.op("pool", lambda e, pm_=pm_, p_=p_, mview=mview, m=m: e.tensor_tensor(
                            out=pm_[:, 0:m * 128], in0=p_[:, 0:m * 128], in1=mview, op=ALU.mult),
                            [lt("pT%d" % (gi % 2)), lt("maskT")], [lt("pmT%d" % (gi % 2))])
                        for j in range(m):
                            mm(pbank[7][:, 0:129], pm_[:, j * 128:(j + 1) * 128], Vh[s_][:, kb + j, :], kb + j == 0, kb + j == nkb - 1,
                               [lt("pmT%d" % (gi % 2)), vl], [pl(7)], sig=(j == m - 1))
                        kb += m
                        gi += 1
                    act(osb[:, h, 0:129], pbank[7][:, 0:129], AF.Copy, [pl(7)], [lt("osb")])

            def fin_stage(i):
                dve(lambda e: e.reciprocal(bst[:, 8:16], osb[:, :, 128]), [lt("osb")], [lt("bst_r")])
                tt(ya[:].rearrange("p (a b) -> p a b", a=8), osb[:, :, 0:128], bst[:, 8:16].unsqueeze(2).broadcast_to([128, 8, 128]),
                   ALU.mult, [lt("osb"), lt("bst_r")], [lt("ya")])
                i0 = 0
                while i0 < 8:
                    pbv = pbank[4][:].bitcast(BF16)
                    for j in range(4):
                        k.op("pe", lambda e, pbv=pbv, j=j, i0=i0: e.transpose(
                            pbv[:, j * 128:(j + 1) * 128], ya[:, (i0 + j) * 128:(i0 + j + 1) * 128], ident2[:]),
                            [lt("ya"), lt("ident2")], [pl(4)], sig=(j == 3))
                    act(yaT_st[:, i0:i0 + 4, :], pbv[:, 0:512].rearrange("p (a b) -> p a b", a=4), AF.Copy, [pl(4)], [lt("yaT_st")])
                    i0 += 4
                dma("pool", d2["yaT"], yaT_s.rearrange("(h d) n -> d h n", d=128)[:, :, i * 128:(i + 1) * 128], yaT_st[:],
                    [lt("yaT_st")], [lt("yaT_s")])

            if NB2:
                idx_stage(0); idx_stage(1); bis_stage(0); mT_stage(0)
            for i in range(NB2):
                if i + 2 < NB2:
                    idx_stage(i + 2)
                att_stage(i)
                if i + 1 < NB2:
                    bis_stage(i + 1)
                fin_stage(i)
                if i + 1 < NB2:
                    mT_stage(i + 1)
            if debug:
                dma("pool", d2["yaT"], dbg["thr"][:], thr_all[:], [lt("thr_all")], [lt("dbg_thr")])
                dma("pool", d2["yaT"], dbg["cnt"][:], cnt_all[:], [lt("cnt_all")], [lt("dbg_cnt")])
            k.barrier()

        p3 = ExitStack()
        with p3:
            ft = alloc_ffn(p3, ["norm_ffn2"], "_p3")
            ft.trr = 0
            finish_consts(ft)
            aflat = ft.aT[:].rearrange("p a b -> p (a b)")
            lt("aT")
            off = [0]
            def take(name, shape, dtype):
                n = int(np.prod(shape[1:])) * (4 if dtype == F32 else 2)
                v = aflat[:, off[0] // 2:(off[0] + n) // 2]
                if dtype == F32:
                    v = v.bitcast(F32)
                off[0] += n
                lt(name, "aT")
                if len(shape) == 3:
                    v = v.rearrange("p (a b) -> p a b", a=shape[1])
                return v
            yaT = take("yaT", [128, 8, T], BF16)
            sga_c = take("sga_c", [128, 4, 512], F32)
            mb_c = take("mb_c", [128, 4, 512], F32)
            tmpm = take("tmpm", [128, 512], F32)
            d3 = {n: dsem("p3_" + n) for n in ("ya", "sga", "mb")}
            for g in range(NG_OWN):
                dma("sp", d_x, ft.xg[:], xmid_s[g * T:(g + 1) * T, :].rearrange("(t p) d -> p t d", p=128), [lt("xmid_s")], [lt("xg")])
                if stage >= 3:
                    dma("sp", d3["ya"], yaT, yaT_s.rearrange("(h d) n -> d h n", d=128)[:, :, g * T:(g + 1) * T], [lt("yaT_s")], [lt("yaT")])
                mT = ft.hT
                for c in range(4):
                    dma("sp", d3["sga"], sga_c, sgaT_s[c * 512:(c + 1) * 512, g * T:(g + 1) * T].rearrange("(f p) n -> p f n", p=128),
                        [lt("sgaT_s")], [lt("sga_c")])
                    dma("sp", d3["mb"], mb_c, mbT_s[c * 512:(c + 1) * 512, g * T:(g + 1) * T].rearrange("(f p) n -> p f n", p=128),
                        [lt("mbT_s")], [lt("mb_c")])
                    if stage >= 3:
                        W = w_bf["w_up_attn"]
                        s_, wl_ = load_w(ft, [(lambda w: w[:, 0:8 * 512].rearrange("p (a b) -> p a b", a=8),
                                               W[:, c * 512:(c + 1) * 512].rearrange("(a p) c -> p a c", p=128))], [lt("wbf_w_up_attn")])
                        wv = ft.wsl[s_][:, 0:8 * 512].rearrange("p (a b) -> p a b", a=8)
                    for fl in range(4):
                        if stage >= 3:
                            bi = (c * 4 + fl) % 6
                            for kc in range(8):
                                mm(pbank[bi][:], wv[:, kc, fl * 128:(fl + 1) * 128], yaT[:, kc, :], kc == 0, kc == 7, [wl_, lt("yaT")], [pl(bi)])
                            tt(tmpm[:], pbank[bi][:], sga_c[:, fl, :], ALU.mult, [pl(bi), lt("sga_c")], [lt("tmpm")])
                            tt(mT[:, 4 * c + fl, :], tmpm[:], mb_c[:, fl, :], ALU.add, [lt("tmpm"), lt("mb_c")], [lt("hT")])
                        else:
                            dve(lambda e, c=c, fl=fl: e.tensor_copy(out=mT[:, 4 * c + fl, :], in_=mb_c[:, fl, :]), [lt("mb_c"), lt("sga_c")], [lt("hT")])
                for c in range(4):
                    W = w_bf["w_out"]
                    s_, wl_ = load_w(ft, [(lambda w: w[:, 0:KC * 512].rearrange("p (a b) -> p a b", a=KC),
                                           W[:, c * 512:(c + 1) * 512].rearrange("(a p) c -> p a c", p=128))], [lt("wbf_w_out")])
                    wv = ft.wsl[s_][:, 0:KC * 512].rearrange("p (a b) -> p a b", a=KC)
                    for t in range(NT):
                        bi = (c * 4 + t) % 6
                        for kc in range(KC):
                            mm(pbank[bi][:], mT[:, kc, t * 128:(t + 1) * 128], wv[:, kc, :], kc == 0, kc == KC - 1, [wl_, lt("hT")], [pl(bi)])
                        xs = ft.xg[:, t, c * 512:(c + 1) * 512]
                        tt(xs, pbank[bi][:], xs, ALU.add, [pl(bi), lt("xg")], [lt("xg")])
                rmsnorm_T(ft, "norm_ffn2")
                ffn(ft, w_bf["ffn2_w_gate"], w_bf["ffn2_w_up"], w_bf["ffn2_w_down"],
                    [lt("wbf_ffn2_w_gate"), lt("wbf_ffn2_w_up"), lt("wbf_ffn2_w_down")])
                dma("pool", d_out, out_d[g * T:(g + 1) * T, :].rearrange("(t p) d -> p t d", p=128), ft.xg[:], [lt("xg")], [lt("out")])
            k.barrier()
        if debug:
            d_dbg = dsem("dbg")
            for o, src in dbg_copies:
                dma("pool", d_dbg, o, src, [], [lt("dbgout")])
            k.barrier()
        with nc.Block() as block:
            k.replay(block)
    return nc


_CACHE = {}


def _consts():
    c = np.zeros((128, 152), np.float32)
    c[:, 0:128] = np.eye(128, dtype=np.float32)
    f_a = (np.float32(ROPE_THETA) ** (-(np.arange(0, 32, 2, dtype=np.float32)) / np.float32(32))).astype(np.float32)
    f_i = (np.float32(ROPE_THETA) ** (-(np.arange(0, 16, 2, dtype=np.float32)) / np.float32(16))).astype(np.float32)
    c[:, 128:144] = f_a[None, :]
    c[:, 144:152] = f_i[None, :]
    return c


def kernel(**inputs):
    stage = inputs.pop("_stage", 99)
    debug = inputs.pop("_debug", False)
    x = np.asarray(inputs["x"])
    positions = np.asarray(inputs["positions"])
    if (stage, debug) not in _CACHE:
        _CACHE[(stage, debug)] = build_program(stage, debug)
    nc = _CACHE[(stage, debug)]
    cst = _consts()
    in_maps = []
    for c in range(8):
        b, r = c // 2, c % 2
        order = own_blocks(r) + other_blocks(r)
        tok = (np.asarray(order)[:, None] * 128 + np.arange(128)[None, :]).reshape(-1)
        m = {"x": np.ascontiguousarray(x[b][tok]),
             "pos": np.ascontiguousarray(positions[b][tok].reshape(NBLK, 128).T.astype(np.int32)),
             "kch": np.ascontiguousarray((tok // 64).astype(np.float32)[None, :].astype(ml_dtypes.bfloat16)),
             "qch": np.ascontiguousarray((tok[:NOWN * 128] // 64).astype(np.float32).reshape(NOWN, 128).T),
             "cst": cst}
        for n in ["ffn1_w_gate", "ffn1_w_up", "ffn1_w_down", "w_in", "w_up_attn", "w_up_sgu", "w_out",
                  "ffn2_w_gate", "ffn2_w_up", "ffn2_w_down"]:
            m[n] = np.asarray(inputs[n])[0]
        for n in ["norm_ffn1", "norm_mix", "norm_ffn2", "q_norm", "k_norm", "idx_k_norm", "sgu_v_norm"]:
            m[n] = np.asarray(inputs[n])[0][None, :]
        m["sgu_w_s"] = np.asarray(inputs["sgu_w_s"])[0]
        m["sgu_b_s"] = np.asarray(inputs["sgu_b_s"])[0]
        in_maps.append(m)
    res = run_bass_kernel_spmd(nc, in_maps, core_ids=list(range(8)))
    if debug:
        _CACHE["last_results"] = res.results
    out = np.zeros((4, S, D), np.float32)
    for c in range(8):
        b, r = c // 2, c % 2
        ob = own_blocks(r)
        o = np.asarray(res.results[c]["out"]).reshape(NOWN, 128, D)
        for i, blk in enumerate(ob):
            out[b, blk * 128:(blk + 1) * 128] = o[i]
    return out
```

```python
import numpy as np
import ml_dtypes
import concourse.bass as bass
import concourse.mybir as mybir
from concourse.bass_utils import run_bass_kernel_spmd

F32 = mybir.dt.float32
BF16 = mybir.dt.bfloat16
I32 = mybir.dt.int32
AF = mybir.ActivationFunctionType
ALU = mybir.AluOpType
AX = mybir.AxisListType

D = 2048
KC = 16
FF = 5632
JC = 44
S = 8192
NBLK = 64
NOWN = 32
T = 512
NT = 4
NG_OWN = 8
NG_ALL = 16
IN_COLS = 10320
EPS = 1e-6
ROPE_THETA = 500000.0
TOPK = 256
NEG = -1.0e30
C_Q, C_K, C_V, C_QI, C_KI, C_WI, C_U, C_VS, C_GA, C_GB = 0, 1024, 2048, 3072, 4096, 4160, 4176, 5200, 6224, 8272


class LT:
    def __init__(self, name, parent=None):
        self.name = name
        self.writers = {}
        self.readers = {}
        self.parent = parent
        self.children = []
        if parent is not None:
            parent.children.append(self)

    def rel(self):
        out = [self]
        if self.parent is not None:
            out.append(self.parent)
        out += self.children
        return out


class K:
    def __init__(self, nc):
        self.nc = nc
        self.eng = {}
        self.sems = {}
        self.cnt = {}
        self.prog = {n: [] for n in ("pe", "act", "dve", "pool", "sp")}
        self.waited = {n: {} for n in self.prog}
        self.dma_sems = []
        self.ctx = []

    def add_engine_sem(self, name, sem):
        self.sems[name] = sem
        self.cnt[name] = 0

    def new_dma_sem(self, sem):
        key = "dma%d" % len(self.dma_sems)
        self.dma_sems.append(key)
        self.sems[key] = sem
        self.cnt[key] = 0
        return key

    def _deps(self, eng, reads, writes):
        deps = {}
        def add(d, skip_own=False):
            for k, v in d.items():
                if skip_own and k == eng:
                    continue
                if deps.get(k, 0) < v:
                    deps[k] = v
        for t0 in reads:
            for t in t0.rel():
                add(t.writers)
        for t0 in writes:
            for t in t0.rel():
                add(t.writers, True)
                add(t.readers, True)
        out = []
        for k, v in deps.items():
            if self.waited[eng].get(k, 0) >= v:
                continue
            self.waited[eng][k] = v
            out.append((k, v))
        return out

    def op(self, eng, fn, reads=(), writes=(), sig=True):
        waits = self._deps(eng, reads, writes)
        if sig:
            self.cnt[eng] += 1
            v = self.cnt[eng]
        else:
            v = self.cnt[eng] + 1
        self.prog[eng].append((waits, fn, eng, 1 if sig else 0))
        for t in reads:
            if t.readers.get(eng, 0) < v:
                t.readers[eng] = v
        for t in writes:
            t.writers[eng] = v
        return (eng, v)

    def begin_write(self, t):
        pass

    def dma(self, queue, semkey, fn, reads=(), writes=()):
        waits = self._deps(queue, reads, writes)
        self.cnt[semkey] += 16
        v = self.cnt[semkey]
        self.prog[queue].append((waits, fn, semkey, 16))
        for t in reads:
            if t.readers.get(semkey, 0) < v:
                t.readers[semkey] = v
        for t in writes:
            t.writers[semkey] = v
        return (semkey, v)

    def fresh(self, t):
        return t

    def barrier(self):
        keys = list(self.sems.keys())
        for eng in ("sp", "pool", "act", "dve", "pe"):
            self.final_wait(eng, keys)

    def final_wait(self, eng, keys):
        waits = []
        for k in keys:
            v = self.cnt[k]
            if v > 0 and self.waited[eng].get(k, 0) < v:
                waits.append((k, v))
                self.waited[eng][k] = v
        self.prog[eng].append((waits, None, None, 0))

    def replay(self, block):
        nc = self.nc
        sems = self.sems
        def run(e, lst):
            for waits, fn, inckey, inc in lst:
                for k, v in waits:
                    e.wait_ge(sems[k], v)
                if fn is not None:
                    ins = fn(e)
                    if inc:
                        ins.then_inc(sems[inckey], inc)
        @block.tensor
        def _(e):
            run(e, self.prog["pe"])
        @block.scalar
        def _(e):
            run(e, self.prog["act"])
        @block.vector
        def _(e):
            run(e, self.prog["dve"])
        @block.gpsimd
        def _(e):
            run(e, self.prog["pool"])
        @block.sync
        def _(e):
            run(e, self.prog["sp"])


def own_blocks(r):
    out = []
    for kq in range(NBLK // 4):
        out += [4 * kq + (0 if r == 0 else 1), 4 * kq + (3 if r == 0 else 2)]
    return out


def other_blocks(r):
    return own_blocks(1 - r)


def _round_bits(x, bits):
    import math
    if x == 0:
        return 0.0
    e = math.floor(math.log2(abs(x)))
    q = 2.0 ** (e - bits + 1)
    return round(x / q) * q


def build_program(stage=99, debug=False):
    from contextlib import ExitStack
    nc = bass.Bass("TRN2", target_bir_lowering=False)
    dt = nc.dram_tensor
    x_in = dt("x", [S, D], F32, kind="ExternalInput").ap()
    pos_in = dt("pos", [128, NBLK], I32, kind="ExternalInput").ap()
    kch_in = dt("kch", [1, S], BF16, kind="ExternalInput").ap()
    qch_in = dt("qch", [128, NOWN], F32, kind="ExternalInput").ap()
    cst_in = dt("cst", [128, 128 + 24], F32, kind="ExternalInput").ap()
    w_names = ["ffn1_w_gate", "ffn1_w_up", "ffn1_w_down", "w_in", "w_up_attn", "w_up_sgu", "w_out",
               "ffn2_w_gate", "ffn2_w_up", "ffn2_w_down"]
    w_shapes = {"ffn1_w_gate": [D, FF], "ffn1_w_up": [D, FF], "ffn1_w_down": [FF, D], "w_in": [D, IN_COLS],
                "w_up_attn": [1024, D], "w_up_sgu": [1024, D], "w_out": [D, D],
                "ffn2_w_gate": [D, FF], "ffn2_w_up": [D, FF], "ffn2_w_down": [FF, D]}
    w_f32 = {n: dt(n, w_shapes[n], F32, kind="ExternalInput").ap() for n in w_names}
    vec_names = {"norm_ffn1": D, "norm_mix": D, "norm_ffn2": D, "q_norm": 128, "k_norm": 128,
                 "idx_k_norm": 64, "sgu_v_norm": 1024}
    vec_in = {n: dt(n, [1, l], F32, kind="ExternalInput").ap() for n, l in vec_names.items()}
    ws_in = dt("sgu_w_s", [8, 128, 128], F32, kind="ExternalInput").ap()
    bs_in = dt("sgu_b_s", [8, 128], F32, kind="ExternalInput").ap()
    out_d = dt("out", [NOWN * 128, D], F32, kind="ExternalOutput").ap()
    w_bf = {n: dt(n + "_bf", w_shapes[n], BF16).ap() for n in w_names}
    xmid_s = dt("xmid_s", [NOWN * 128, D], F32).ap()
    qT_s = dt("qT_s", [1024, NOWN * 128], BF16).ap()
    kT_s = dt("kT_s", [1024, S], BF16).ap()
    v_s = dt("v_s", [8, 128, NBLK, 129], BF16).ap()
    kiT_s = dt("kiT_s", [64, S], BF16).ap()
    qiT_s = dt("qiT_s", [1024, NOWN * 128], BF16).ap()
    sgn_s = dt("sgn_s", [NOWN * 128, 16], F32).ap()
    sgaT_s = dt("sgaT_s", [D, NOWN * 128], F32).ap()
    mbT_s = dt("mbT_s", [D, NOWN * 128], F32).ap()
    yaT_s = dt("yaT_s", [1024, NOWN * 128], BF16).ap()
    dbg = {}
    dbg_copies = []
    if debug:
        dbg["thr"] = dt("dbg_thr", [128, NOWN], F32, kind="ExternalOutput").ap()
        dbg["cnt"] = dt("dbg_cnt", [128, NOWN], F32, kind="ExternalOutput").ap()
        for nm, src in (("q", qT_s[:, 0:128]), ("k", kT_s[:, 0:128]), ("ki", kiT_s[:, 0:128]), ("qi", qiT_s[:, 0:128]),
                        ("sgn", sgn_s[0:128, :]), ("sga", sgaT_s[:, 0:128]), ("mb", mbT_s[:, 0:128]), ("ya", yaT_s[:, 0:128]),
                        ("xmid", xmid_s[0:128, :]), ("v", v_s[:, :, 0, :]), ("k1", kT_s[:, 4096:4224])):
            o = dt("dbg_" + nm, list(src.shape), src.dtype, kind="ExternalOutput").ap()
            dbg_copies.append((o, src))

    top = ExitStack()
    with top:
        def sem(name):
            return top.enter_context(nc.semaphore(name))
        k = K(nc)
        for n in ("pe", "act", "dve", "pool", "sp"):
            k.add_engine_sem(n, sem("s_" + n))
        def dsem(name):
            return k.new_dma_sem(sem("d_" + name))
        L = {}
        def lt(name, parent=None):
            if name not in L:
                L[name] = LT(name, L[parent] if parent else None)
            return L[name]
        pbank = [top.enter_context(nc.psum_tensor("pb%d" % i, [128, 512], F32)) for i in range(8)]
        def pl(i):
            return lt("pb%d" % i)

        def mm(out, lhsT, rhs, start, stop, reads, writes, sig=None):
            return k.op("pe", lambda e: e.matmul(out, lhsT, rhs, start=start, stop=stop), reads, writes,
                        sig=(stop if sig is None else sig))
        def act(out, in_, func, reads, writes, **kw):
            return k.op("act", lambda e: e.activation(out=out, in_=in_, func=func, **kw), reads, writes)
        def dve(fn, reads, writes):
            return k.op("dve", fn, reads, writes)
        def tt(out, in0, in1, op, reads, writes):
            return k.op("dve", lambda e: e.tensor_tensor(out=out, in0=in0, in1=in1, op=op), reads, writes)
        def ts(out, in0, s1, s2, op0, op1, reads, writes, **kw):
            if op1 is None:
                return k.op("dve", lambda e: e.tensor_scalar(out=out, in0=in0, scalar1=s1, scalar2=None, op0=op0, **kw), reads, writes)
            return k.op("dve", lambda e: e.tensor_scalar(out=out, in0=in0, scalar1=s1, scalar2=s2, op0=op0, op1=op1, **kw), reads, writes)
        def stt(out, in0, scalar, in1, op0, op1, reads, writes):
            return k.op("dve", lambda e: e.scalar_tensor_tensor(out=out, in0=in0, scalar=scalar, in1=in1, op0=op0, op1=op1), reads, writes)
        def dma(queue, semkey, out, in_, reads, writes, **kw):
            return k.dma(queue, semkey, lambda e: e.dma_start(out=out, in_=in_, **kw), reads, writes)

        d_cast = {n: dsem("cast_" + n) for n in w_names if not n.startswith("ffn1")}
        GU_PIECES = [(0, 1536), (1536, 3072), (3072, 4608), (4608, 5632)]
        d_misc = dsem("misc")
        d_x = dsem("xload")
        NWS = 3
        d_w = [dsem("wsl%d" % i) for i in range(NWS)]
        d_out = dsem("outst")

        for pi, (c0_, c1_) in enumerate(GU_PIECES):
            for n in ("ffn1_w_gate", "ffn1_w_up"):
                ds_ = dsem("cast_%s_p%d" % (n, pi))
                dma("pool", ds_, w_bf[n][:, c0_:c1_], w_f32[n][:, c0_:c1_], [], [lt("wbf_%s_p%d" % (n, pi))])
        for pi in range(4):
            ds_ = dsem("cast_ffn1_w_down_p%d" % pi)
            dma("pool", ds_, w_bf["ffn1_w_down"][pi * 1408:(pi + 1) * 1408, :], w_f32["ffn1_w_down"][pi * 1408:(pi + 1) * 1408, :],
                [], [lt("wbf_ffn1_w_down_p%d" % pi)])
        for n in w_names:
            if n.startswith("ffn1"):
                continue
            rows, cols = w_shapes[n]
            f = 1
            while cols // f > 2048 or cols % f:
                f += 1
            nsplit = 4 if rows >= 2048 else 1
            rs = rows // nsplit
            for i in range(nsplit):
                s_ap = w_f32[n][i * rs:(i + 1) * rs, :].rearrange("r (a b) -> r a b", a=f)
                d_ap = w_bf[n][i * rs:(i + 1) * rs, :].rearrange("r (a b) -> r a b", a=f)
                dma("pool", d_cast[n], d_ap, s_ap, [], [lt("wbf_" + n)])

        class FFNTiles:
            pass

        def alloc_ffn(es, gnames, sfx=""):
            ft = FFNTiles()
            def sb(name, shape, dtype):
                return es.enter_context(nc.sbuf_tensor(name + sfx, shape, dtype))
            ft.ident_f = sb("ident_f", [128, 128], F32)
            ft.ident = sb("ident", [128, 128], BF16)
            ft.gbc = {n: sb("g_" + n, [128, D], F32) for n in gnames}
            ft.xg = sb("xg", [128, NT, D], F32)
            ft.xn = [sb("xn%d" % i, [128, D], BF16) for i in range(2)]
            ft.hT = sb("hT", [128, KC, T], BF16)
            ft.aT = sb("aT", [128, JC, T], BF16)
            ft.wsl = [sb("wsl%d" % i, [128, 8192], BF16) for i in range(NWS)]
            ft.sgt = [sb("sgt%d" % i, [128, 512], F32) for i in range(2)]
            ft.st1 = sb("st1", [128, 8], F32)
            ft.junk = sb("junk", [128, D], BF16)
            ft.sb = sb
            ft.wrr = 0
            cl = []
            dma("sp", d_misc, ft.ident_f[:], cst_in[:, 0:128], [], [lt("ident_f")]); cl.append("ident_f")
            for n in gnames:
                dma("sp", d_misc, ft.gbc[n][:], vec_in[n].partition_broadcast(128), [], [lt("g_" + n)]); cl.append("g_" + n)
            ft.cl = cl
            return ft

        def finish_consts(ft):
            for nm in ft.cl:
                lt(nm).writers[d_misc] = k.cnt[d_misc]
            dve(lambda e: e.tensor_copy(out=ft.ident[:], in_=ft.ident_f[:]), [lt("ident_f")], [lt("ident")])

        def load_w(ft, parts, reads_lt):
            s = ft.wrr % NWS
            ft.wrr += 1
            t = lt("wsl%d" % s)
            for dst_fn, src in parts:
                dma("sp", d_w[s], dst_fn(ft.wsl[s]), src, reads_lt, [t])
            return s, t

        def transposes(ft, srcs, src_lts, dst_fn, dst_lt, nparts=128):
            n = len(srcs)
            i0 = 0
            while i0 < n:
                m = min(4, n - i0)
                bi = 6 + (ft.trr % 2)
                ft.trr += 1
                pbv = pbank[bi][:].bitcast(BF16)
                for j in range(m):
                    src = srcs[i0 + j]
                    w = src.shape[1]
                    k.op("pe", lambda e, pbv=pbv, j=j, src=src, w=w: e.transpose(
                        pbv[0:w, j * 128:(j + 1) * 128], src, ft.ident[:]),
                        list(src_lts) + [lt("ident")], [pl(bi)], sig=(j == m - 1))
                w = srcs[i0].shape[1]
                act(dst_fn(i0, m), pbv[0:w, 0:m * 128].rearrange("p (a b) -> p a b", a=m), AF.Copy, [pl(bi)], [dst_lt])
                i0 += m

        def rmsnorm_T(ft, gname):
            g_tile = ft.gbc[gname]
            st1 = ft.st1
            for t in range(NT):
                xt = ft.xg[:, t, :]
                sl = ft.xn[t % 2]
                sl_lt = lt("xn%d" % (t % 2))
                act(ft.junk[:], xt, AF.Square, [lt("xg")], [lt("junk"), lt("st1")], accum_out=st1[:, 0:1])
                act(st1[:, 1:2], st1[:, 0:1], AF.Sqrt, [lt("st1")], [lt("st1b")], scale=1.0 / D, bias=EPS)
                dve(lambda e: e.reciprocal(st1[:, 2:3], st1[:, 1:2]), [lt("st1b")], [lt("st1c")])
                stt(sl[:], xt, st1[:, 2:3], g_tile[:], ALU.mult, ALU.mult, [lt("xg"), lt("st1c"), lt("g_" + gname)], [sl_lt])
                transposes(ft, [sl[:, kc * 128:(kc + 1) * 128] for kc in range(KC)], [sl_lt],
                           lambda i0, m, t=t: ft.hT[:, i0:i0 + m, t * 128:(t + 1) * 128], lt("hT"))

        def ffn(ft, wg, wu, wd, first_readers, fine=None):
            JB = 2
            for jb in range(JC // JB):
                c0 = jb * JB * 128
                if fine:
                    pi_ = [p for p, (a_, b_) in enumerate(GU_PIECES) if a_ <= c0 < b_][0]
                    first_readers = [lt("wbf_%s_w_gate_p%d" % (fine, pi_)), lt("wbf_%s_w_up_p%d" % (fine, pi_))]
                s, wl = load_w(ft, [
                    (lambda w: w[:, 0:4096].rearrange("p (a b) -> p a b", a=KC),
                     wg[:, c0:c0 + 256].rearrange("(a p) c -> p a c", p=128)),
                    (lambda w: w[:, 4096:8192].rearrange("p (a b) -> p a b", a=KC),
                     wu[:, c0:c0 + 256].rearrange("(a p) c -> p a c", p=128))], first_readers)
                wgv = ft.wsl[s][:, 0:4096].rearrange("p (a b) -> p a b", a=KC)
                wuv = ft.wsl[s][:, 4096:8192].rearrange("p (a b) -> p a b", a=KC)
                for jj in range(JB):
                    j = jb * JB + jj
                    bi = (j % 2) * 2
                    pg, pu = pbank[bi], pbank[bi + 1]
                    for kc in range(KC):
                        mm(pg[:], wgv[:, kc, jj * 128:(jj + 1) * 128], ft.hT[:, kc, :], kc == 0, kc == KC - 1,
                           [wl, lt("hT")], [pl(bi)])
                    for kc in range(KC):
                        mm(pu[:], wuv[:, kc, jj * 128:(jj + 1) * 128], ft.hT[:, kc, :], kc == 0, kc == KC - 1,
                           [wl, lt("hT")], [pl(bi + 1)])
                    sg = ft.sgt[j % 2]
                    sgl = lt("sgt%d" % (j % 2))
                    act(sg[:], pg[:], AF.Silu, [pl(bi)], [sgl])
                    tt(ft.aT[:, j, :], sg[:], pu[:], ALU.mult, [sgl, pl(bi + 1)], [lt("aT")])
            for c in range(4):
                for jq in range(4):
                    if fine:
                        first_readers = [lt("wbf_%s_w_down_p%d" % (fine, jq))]
                    s, wl = load_w(ft, [
                        (lambda w: w[:, 0:11 * 512].rearrange("p (a b) -> p a b", a=11),
                         wd[jq * 11 * 128:(jq + 1) * 11 * 128, c * 512:(c + 1) * 512].rearrange("(a p) c -> p a c", p=128))],
                        first_readers)
                    wv = ft.wsl[s][:, 0:11 * 512].rearrange("p (a b) -> p a b", a=11)
                    for ji in range(11):
                        j = jq * 11 + ji
                        for t in range(NT):
                            bi = (c % 2) * 4 + t
                            mm(pbank[bi][:], ft.aT[:, j, t * 128:(t + 1) * 128], wv[:, ji, :], j == 0, j == JC - 1,
                               [wl, lt("aT")], [pl(bi)], sig=(j == JC - 1 or (ji == 10 and t == NT - 1)))
                for t in range(NT):
                    bi = (c % 2) * 4 + t
                    xs = ft.xg[:, t, c * 512:(c + 1) * 512]
                    stt(xs, pbank[bi][:], 0.5, xs, ALU.mult, ALU.add, [pl(bi), lt("xg")], [lt("xg")])

        p1 = ExitStack()
        with p1:
            ft = alloc_ffn(p1, ["norm_ffn1", "norm_mix"])
            ft.trr = 0
            sb = ft.sb
            gq = sb("gq", [128, 128], F32); gk = sb("gk", [128, 128], F32)
            gki = sb("gki", [128, 64], F32); gv = sb("gv", [128, 1024], F32)
            cosA = sb("cosA", [128, NBLK, 16], F32); sinA = sb("sinA", [128, NBLK, 16], F32)
            cosI = sb("cosI", [128, NBLK, 8], F32); sinI = sb("sinI", [128, NBLK, 8], F32)
            WsT = sb("WsT", [128, 8, 128], BF16)
            bsT = sb("bsT", [128, 8], F32)
            invf = sb("invf", [128, 24], F32)
            posi = sb("posi", [128, NBLK], I32)
            st2 = sb("st2", [128, 16], F32)
            wabs = sb("wabs", [128, NT, 16], F32)
            sgn_st = sb("sgn_st", [128, NT, 16], F32)
            rr = [sb("rr%d" % i, [128, 64], F32) for i in range(4)]
            kiT_st = sb("kiT_st", [64, T], BF16)
            kib = sb("kib", [128, 64], BF16)
            for (tile_, nm) in ((gq, "q_norm"), (gk, "k_norm"), (gki, "idx_k_norm"), (gv, "sgu_v_norm")):
                dma("sp", d_misc, tile_[:], vec_in[nm].partition_broadcast(128), [], [lt("c_" + nm)]); ft.cl.append("c_" + nm)
            dma("sp", d_misc, invf[:], cst_in[:, 128:152], [], [lt("invf")]); ft.cl.append("invf")
            dma("sp", d_misc, posi[:], pos_in[:], [], [lt("posi")]); ft.cl.append("posi")
            dma("sp", d_misc, bsT[:], bs_in.rearrange("g t -> t g"), [], [lt("bsT")], allow_slow_non_contiguous=True); ft.cl.append("bsT")
            finish_consts(ft)
            aflat = ft.aT[:].rearrange("p a b -> p (a b)")
            xflat = ft.xg[:].rearrange("p a b -> p (a b)")
            class Arena:
                def __init__(self, flat, esz, parent):
                    self.flat, self.esz, self.off, self.parent = flat, esz, 0, parent
                def take(self, name, shape, dtype):
                    n = int(np.prod(shape[1:])) * (4 if dtype == F32 else 2)
                    assert self.off + n <= self.flat.shape[1] * self.esz, (name, self.off, n)
                    v = self.flat[:, self.off // self.esz:(self.off + n) // self.esz]
                    if self.esz == 2 and dtype == F32:
                        v = v.bitcast(F32)
                    if self.esz == 4 and dtype == BF16:
                        v = v.bitcast(BF16)
                    self.off += n
                    lt(name, self.parent)
                    if len(shape) == 3:
                        v = v.rearrange("p (a b) -> p a b", a=shape[1])
                    return v
            A1 = Arena(aflat, 2, "aT")
            A2 = Arena(xflat, 4, "xg")
            lt("aT"); lt("xg")
            vsn = A1.take("vsn", [128, NT, 1024], BF16)
            ybT = A1.take("ybT", [128, 8, T], BF16)
            qT_st = A1.take("qT_st", [128, 8, T], BF16)
            kT_st = A1.take("kT_st", [128, 8, T], BF16)
            ost = A1.take("ost", [128, 4, 512], F32)
            yb = A1.take("yb", [128, 1024], BF16)
            qb = A1.take("qb", [128, 512], BF16)
            qb2 = A1.take("qb2", [128, 512], BF16)
            qbs = [(qb, "qb"), (qb2, "qb2")]
            v_st = A2.take("v_st", [128, NT * 8 * 129], BF16).rearrange("p (t h d) -> p t h d", t=NT, h=8)
            qiT_st = A2.take("qiT_st", [128, 8, T], BF16)
            tmpA = A2.take("tmpA", [128, 1024], F32)
            tmpB = A2.take("tmpB", [128, 1024], F32)
            tmpC = A2.take("tmpC", [128, 1024], F32)
            sgb = A2.take("sgb", [128, 512], F32)
            d_st = {n: dsem("st_" + n) for n in ("xmid", "qT", "kT", "v", "kiT", "qiT", "sgn", "ost")}

            TWO_PI = 2.0 * np.pi
            C1 = 6.28125
            C2 = _round_bits(TWO_PI - C1, 9)
            C3 = float(np.float32(TWO_PI - C1 - C2))
            MAGIC = 12582912.0
            PI_LO = 3.1415925
            posf = sb("posf", [128, NBLK], F32)
            dve(lambda e: e.tensor_copy(out=posf[:], in_=posi[:]), [lt("posi")], [lt("posf")])
            def sincos(nf, f0, cos_t, sin_t):
                n = NBLK * nf
                ang = tmpA[:, 0:n].rearrange("p (a b) -> p a b", a=NBLK)
                tt(ang, posf[:].unsqueeze(2).broadcast_to([128, NBLK, nf]),
                   invf[:, f0:f0 + nf].unsqueeze(1).broadcast_to([128, NBLK, nf]), ALU.mult,
                   [lt("posf"), lt("invf")], [lt("tmpA")])
                angf = tmpA[:, 0:n]
                kk = tmpB[:, 0:n]
                r = tmpC[:, 0:n]
                for (shift, dst) in ((0.0, sin_t), (0.25, cos_t)):
                    ts(kk, angf, 1.0 / TWO_PI, shift, ALU.mult, ALU.add, [lt("tmpA")], [lt("tmpB")])
                    ts(kk, kk, MAGIC, None, ALU.add, None, [lt("tmpB")], [lt("tmpB")])
                    ts(kk, kk, -MAGIC, None, ALU.add, None, [lt("tmpB")], [lt("tmpB")])
                    stt(r, kk, -C1, angf, ALU.mult, ALU.add, [lt("tmpB"), lt("tmpA")], [lt("tmpC")])
                    stt(r, kk, -C2, r, ALU.mult, ALU.add, [lt("tmpB"), lt("tmpC")], [lt("tmpC")])
                    stt(r, kk, -C3, r, ALU.mult, ALU.add, [lt("tmpB"), lt("tmpC")], [lt("tmpC")])
                    if shift:
                        ts(r, r, float(np.pi / 2), None, ALU.add, None, [lt("tmpC")], [lt("tmpC")])
                    ts(kk, r, PI_LO, -TWO_PI, ALU.is_gt, ALU.mult, [lt("tmpC")], [lt("tmpB")])
                    tt(r, r, kk, ALU.add, [lt("tmpC"), lt("tmpB")], [lt("tmpC")])
                    ts(kk, r, -PI_LO, TWO_PI, ALU.is_lt, ALU.mult, [lt("tmpC")], [lt("tmpB")])
                    tt(r, r, kk, ALU.add, [lt("tmpC"), lt("tmpB")], [lt("tmpC")])
                    ts(r, r, PI_LO, -PI_LO, ALU.min, ALU.max, [lt("tmpC")], [lt("tmpC")])
                    act(dst[:].rearrange("p a b -> p (a b)"), r, AF.Sin, [lt("tmpC")], [lt("rope")])
            sincos(16, 0, cosA, sinA)
            sincos(8, 16, cosI, sinI)
            for g8 in range(8):
                wtmp = tmpA[:, 0:128]
                dma("sp", d_misc, wtmp, ws_in[g8], [], [lt("tmpA")])
                dve(lambda e: e.memset(tmpA[0:64, 64:128], 0.0), [lt("tmpA")], [lt("tmpA")])
                dve(lambda e: e.tensor_copy(out=qb[:, 0:128], in_=tmpA[:, 0:128]), [lt("tmpA")], [lt("qb")])
                transposes(ft, [qb[:, 0:128]], [lt("qb")], lambda i0, m, g8=g8: WsT[:, g8:g8 + 1, :], lt("WsT"))

            def rope_apply(src3, dst3, nh, half, cos2, sin2, src_lt, dst_lt):
                x1 = src3[:, :, 0:half]; x2 = src3[:, :, half:2 * half]
                cb = cos2.unsqueeze(1).broadcast_to([128, nh, half])
                sbb = sin2.unsqueeze(1).broadcast_to([128, nh, half])
                rv = [rr[i][:, 0:nh * half].rearrange("p (a b) -> p a b", a=nh) for i in range(4)]
                tt(rv[0], x1, cb, ALU.mult, [src_lt, lt("rope")], [lt("rr0")])
                tt(rv[1], x2, sbb, ALU.mult, [src_lt, lt("rope")], [lt("rr1")])
                tt(rv[2], x2, cb, ALU.mult, [src_lt, lt("rope")], [lt("rr2")])
                tt(rv[3], x1, sbb, ALU.mult, [src_lt, lt("rope")], [lt("rr3")])
                tt(dst3[:, :, 0:half], rv[0], rv[1], ALU.subtract, [lt("rr0"), lt("rr1")], [dst_lt])
                tt(dst3[:, :, half:2 * half], rv[2], rv[3], ALU.add, [lt("rr2"), lt("rr3")], [dst_lt])

            prr = [0]
            def proj_chunk(col0, ncols, epilogue, wname="w_in", kcs=KC, hsrc=None, hsrc_lt=None):
                W = w_bf[wname]
                s, wl = load_w(ft, [(lambda w: w[:, 0:kcs * ncols].rearrange("p (a b) -> p a b", a=kcs),
                                     W[:, col0:col0 + ncols].rearrange("(a p) c -> p a c", p=128))], [lt("wbf_" + wname)])
                wv = ft.wsl[s][:, 0:kcs * ncols].rearrange("p (a b) -> p a b", a=kcs)
                pending = None
                for t in range(NT):
                    bi = prr[0] % 6
                    prr[0] += 1
                    for kc in range(kcs):
                        mm(pbank[bi][:, 0:ncols], ft.hT[:, kc, t * 128:(t + 1) * 128], wv[:, kc, :], kc == 0, kc == kcs - 1,
                           [wl, lt("hT")], [pl(bi)])
                    if pending is not None:
                        pending()
                    pending = epilogue(t, pbank[bi], pl(bi))
                if pending is not None:
                    pending()

            def qk_epilogue(gvec, gname, stT, st_lt, c, blk0):
                def ep(t, pb, pbl):
                    psv = pb[:].rearrange("p (a b) -> p a b", a=4)
                    act(tmpA[:, 0:512], pb[:], AF.Square, [pbl], [lt("tmpA")])
                    dve(lambda e: e.tensor_reduce(out=st2[:, 0:4], in_=tmpA[:, 0:512].rearrange("p (a b) -> p a b", a=4),
                                                  axis=AX.X, op=ALU.add), [lt("tmpA")], [lt("st2")])
                    act(st2[:, 4:8], st2[:, 0:4], AF.Sqrt, [lt("st2")], [lt("st2b")], scale=1.0 / 128, bias=EPS)
                    dve(lambda e: e.reciprocal(st2[:, 8:12], st2[:, 4:8]), [lt("st2b")], [lt("st2c")])
                    tb = tmpB[:, 0:512].rearrange("p (a b) -> p a b", a=4)
                    tt(tb, psv, st2[:, 8:12].unsqueeze(2).broadcast_to([128, 4, 128]), ALU.mult, [pbl, lt("st2c")], [lt("tmpB")])
                    tt(tb, tb, gvec[:].unsqueeze(1).broadcast_to([128, 4, 128]), ALU.mult, [lt("tmpB"), lt("c_" + gname)], [lt("tmpB")])
                    qb_, qbn = qbs[t % 2]
                    qbv = qb_.rearrange("p (a b) -> p a b", a=4)
                    act(qb_, tmpB[:, 0:512], AF.Copy, [lt("tmpB")], [lt(qbn)])
                    rope_apply(tb, qbv, 4, 16, cosA[:, blk0 + t, :], sinA[:, blk0 + t, :], lt("tmpB"), lt(qbn))
                    def deferred(t=t, qb_=qb_, qbn=qbn):
                        transposes(ft, [qb_[:, j * 128:(j + 1) * 128] for j in range(4)], [lt(qbn)],
                                   lambda i0, m, t=t: stT[:, 4 * c + i0:4 * c + i0 + m, t * 128:(t + 1) * 128], st_lt)
                    return deferred
                return ep

            def gelu_to(dst, pb, pbl, dst_lt, ncols=512):
                act(tmpB[:, 0:ncols], pb[:, 0:ncols], AF.Square, [pbl], [lt("tmpB")])
                ts(tmpB[:, 0:ncols], tmpB[:, 0:ncols], 0.044715, 1.0, ALU.mult, ALU.add, [lt("tmpB")], [lt("tmpB")])
                tt(tmpB[:, 0:ncols], tmpB[:, 0:ncols], pb[:, 0:ncols], ALU.mult, [lt("tmpB"), pbl], [lt("tmpB")])
                act(tmpB[:, 0:ncols], tmpB[:, 0:ncols], AF.Sigmoid, [lt("tmpB")], [lt("tmpB")], scale=1.5957691216057308)
                tt(dst, tmpB[:, 0:ncols], pb[:, 0:ncols], ALU.mult, [lt("tmpB"), pbl], [dst_lt])

            ngroups = NG_OWN if stage <= 1 else NG_ALL
            for g in range(ngroups):
                own = g < NG_OWN
                blk0 = g * NT
                dma("sp", d_x, ft.xg[:], x_in[g * T:(g + 1) * T, :].rearrange("(t p) d -> p t d", p=128), [], [lt("xg")])
                rmsnorm_T(ft, "norm_ffn1")
                ffn(ft, w_bf["ffn1_w_gate"], w_bf["ffn1_w_up"], w_bf["ffn1_w_down"], [], fine="ffn1")
                if stage <= 1:
                    dma("pool", d_out, out_d[g * T:(g + 1) * T, :].rearrange("(t p) d -> p t d", p=128), ft.xg[:],
                        [lt("xg")], [lt("out")])
                    continue
                if own:
                    dma("pool", d_st["xmid"], xmid_s[g * T:(g + 1) * T, :].rearrange("(t p) d -> p t d", p=128), ft.xg[:],
                        [lt("xg")], [lt("xmid_s")])
                rmsnorm_T(ft, "norm_mix")
                def kiwi_ep(t, pb, pbl):
                    act(tmpA[:, 0:64], pb[:, 0:64], AF.Square, [pbl], [lt("tmpA"), lt("st2")], accum_out=st2[:, 0:1])
                    act(st2[:, 4:5], st2[:, 0:1], AF.Sqrt, [lt("st2")], [lt("st2b")], scale=1.0 / 64, bias=EPS)
                    dve(lambda e: e.reciprocal(st2[:, 8:9], st2[:, 4:5]), [lt("st2b")], [lt("st2c")])
                    stt(tmpB[:, 0:64], pb[:, 0:64], st2[:, 8:9], gki[:], ALU.mult, ALU.mult, [pbl, lt("st2c"), lt("c_idx_k_norm")], [lt("tmpB")])
                    dve(lambda e: e.tensor_copy(out=kib[:], in_=tmpB[:, 0:64]), [lt("tmpB")], [lt("kib")])
                    rope_apply(tmpB[:, 0:64].rearrange("p (a b) -> p a b", a=1), kib[:].rearrange("p (a b) -> p a b", a=1),
                               1, 8, cosI[:, blk0 + t, :], sinI[:, blk0 + t, :], lt("tmpB"), lt("kib"))
                    transposes(ft, [kib[:, 0:64]], [lt("kib")],
                               lambda i0, m, t=t: kiT_st[:, t * 128:(t + 1) * 128].rearrange("p (a b) -> p a b", a=1), lt("kiT_st"))
                    if own:
                        sc = (16 ** -0.5) * (64 ** -0.5)
                        act(wabs[:, t, :], pb[:, 64:80], AF.Abs, [pbl], [lt("wabs")], scale=sc)
                        ts(sgn_st[:, t, :], pb[:, 64:80], 0.0, 2.0, ALU.is_ge, ALU.mult, [pbl], [lt("sgn_st")])
                        ts(sgn_st[:, t, :], sgn_st[:, t, :], -1.0, None, ALU.add, None, [lt("sgn_st")], [lt("sgn_st")])
                proj_chunk(C_KI, 80, kiwi_ep)
                dma("pool", d_st["kiT"], kiT_s[:, g * T:(g + 1) * T], kiT_st[:], [lt("kiT_st")], [lt("kiT_s")])
                if own:
                    dma("pool", d_st["sgn"], sgn_s[g * T:(g + 1) * T, :].rearrange("(t p) c -> p t c", p=128), sgn_st[:],
                        [lt("sgn_st")], [lt("sgn_s")])
                if own:
                    for c in range(2):
                        proj_chunk(C_Q + c * 512, 512, qk_epilogue(gq, "q_norm", qT_st, lt("qT_st"), c, blk0))
                    dma("pool", d_st["qT"], qT_s.rearrange("(h d) n -> d h n", d=128)[:, :, g * T:(g + 1) * T], qT_st,
                        [lt("qT_st")], [lt("qT_s")])
                for c in range(2):
                    proj_chunk(C_K + c * 512, 512, qk_epilogue(gk, "k_norm", kT_st, lt("kT_st"), c, blk0))
                dma("pool", d_st["kT"], kT_s.rearrange("(h d) n -> d h n", d=128)[:, :, g * T:(g + 1) * T], kT_st,
                    [lt("kT_st")], [lt("kT_s")])
                for t4 in range(NT):
                    dve(lambda e, t4=t4: e.memset(v_st[:, t4, :, 128:129], 1.0), [], [lt("v_st")])
                for c in range(2):
                    def v_ep(t, pb, pbl, c=c):
                        act(v_st[:, t, 4 * c:4 * c + 4, 0:128], pb[:].rearrange("p (a b) -> p a b", a=4), AF.Copy, [pbl], [lt("v_st")])
                    proj_chunk(C_V + c * 512, 512, v_ep)
                for h8 in range(8):
                    dma("pool", d_st["v"], v_s[h8, :, blk0:blk0 + NT, :], v_st[:, :, h8, :],
                        [lt("v_st")], [lt("v_s")])
                if not own:
                    continue
                for c in range(2):
                    def qi_ep(t, pb, pbl, c=c):
                        act(tmpA[:, 0:512], pb[:], AF.Copy, [pbl], [lt("tmpA")])
                        ta = tmpA[:, 0:512].rearrange("p (a b) -> p a b", a=8)
                        tbv = tmpB[:, 0:512].rearrange("p (a b) -> p a b", a=8)
                        dve(lambda e: e.tensor_copy(out=tmpB[:, 0:512], in_=tmpA[:, 0:512]), [lt("tmpA")], [lt("tmpB")])
                        rope_apply(ta, tbv, 8, 8, cosI[:, blk0 + t, :], sinI[:, blk0 + t, :], lt("tmpA"), lt("tmpB"))
                        qb_, qbn = qbs[t % 2]
                        qbv = qb_.rearrange("p (a b) -> p a b", a=8)
                        tt(qbv, tbv, wabs[:, t, 8 * c:8 * c + 8].unsqueeze(2).broadcast_to([128, 8, 64]), ALU.mult,
                           [lt("tmpB"), lt("wabs")], [lt(qbn)])
                        def deferred(t=t, qb_=qb_, qbn=qbn, c=c):
                            transposes(ft, [qb_[:, j * 128:(j + 1) * 128] for j in range(4)], [lt(qbn)],
                                       lambda i0, m, t=t: qiT_st[:, 4 * c + i0:4 * c + i0 + m, t * 128:(t + 1) * 128], lt("qiT_st"))
                        return deferred
                    proj_chunk(C_QI + c * 512, 512, qi_ep)
                dma("pool", d_st["qiT"], qiT_s.rearrange("(h d) n -> d h n", d=128)[:, :, g * T:(g + 1) * T], qiT_st,
                    [lt("qiT_st")], [lt("qiT_s")])
                vsacc = {}
                for c in range(2):
                    def vs_ep(t, pb, pbl, c=c):
                        gelu_to(tmpA[:, c * 512:(c + 1) * 512], pb, pbl, lt("tmpA"))
                        if c == 1:
                            act(tmpC[:, 0:1024], tmpA[:, 0:1024], AF.Square, [lt("tmpA")], [lt("tmpC"), lt("st2")], accum_out=st2[:, 0:1])
                            act(st2[:, 4:5], st2[:, 0:1], AF.Sqrt, [lt("st2")], [lt("st2b")], scale=1.0 / 1024, bias=EPS)
                            dve(lambda e: e.reciprocal(st2[:, 8:9], st2[:, 4:5]), [lt("st2b")], [lt("st2c")])
                            stt(vsn[:, t, :], tmpA[:, 0:1024], st2[:, 8:9], gv[:], ALU.mult, ALU.mult,
                                [lt("tmpA"), lt("st2c"), lt("c_sgu_v_norm")], [lt("vsn")])
                    vsacc[c] = vs_ep
                Wn = w_bf["w_in"]
                sl = []
                for c in range(2):
                    s_, wl_ = load_w(ft, [(lambda w: w[:, 0:KC * 512].rearrange("p (a b) -> p a b", a=KC),
                                           Wn[:, C_VS + c * 512:C_VS + (c + 1) * 512].rearrange("(a p) c -> p a c", p=128))], [lt("wbf_w_in")])
                    sl.append((s_, wl_))
                for t in range(NT):
                    for c in range(2):
                        s_, wl_ = sl[c]
                        wv = ft.wsl[s_][:, 0:KC * 512].rearrange("p (a b) -> p a b", a=KC)
                        bi = prr[0] % 6; prr[0] += 1
                        for kc in range(KC):
                            mm(pbank[bi][:], ft.hT[:, kc, t * 128:(t + 1) * 128], wv[:, kc, :], kc == 0, kc == KC - 1, [wl_, lt("hT")], [pl(bi)])
                        vsacc[c](t, pbank[bi], pl(bi))
                sl = []
                for c in range(2):
                    s_, wl_ = load_w(ft, [(lambda w: w[:, 0:KC * 512].rearrange("p (a b) -> p a b", a=KC),
                                           Wn[:, C_U + c * 512:C_U + (c + 1) * 512].rearrange("(a p) c -> p a c", p=128))], [lt("wbf_w_in")])
                    sl.append((s_, wl_))
                for t in range(NT):
                    for c in range(2):
                        s_, wl_ = sl[c]
                        wv = ft.wsl[s_][:, 0:KC * 512].rearrange("p (a b) -> p a b", a=KC)
                        bi = prr[0] % 6; prr[0] += 1
                        for kc in range(KC):
                            mm(pbank[bi][:], ft.hT[:, kc, t * 128:(t + 1) * 128], wv[:, kc, :], kc == 0, kc == KC - 1, [wl_, lt("hT")], [pl(bi)])
                        gelu_to(tmpA[:, c * 512:(c + 1) * 512], pbank[bi], pl(bi), lt("tmpA"))
                    for hb in range(2):
                        bi = prr[0] % 6; prr[0] += 1
                        for g4 in range(4):
                            g8 = hb * 4 + g4
                            mm(pbank[bi][:, g4 * 128:(g4 + 1) * 128], WsT[:, g8, :], vsn[:, t, g8 * 128:(g8 + 1) * 128], True, True,
                               [lt("WsT"), lt("vsn")], [pl(bi)], sig=(g4 == 3))
                        for g4 in range(4):
                            g8 = hb * 4 + g4
                            stt(yb[:, g8 * 128:(g8 + 1) * 128], pbank[bi][:, g4 * 128:(g4 + 1) * 128], bsT[:, g8:g8 + 1],
                                tmpA[:, g8 * 128:(g8 + 1) * 128], ALU.add, ALU.mult, [pl(bi), lt("bsT"), lt("tmpA")], [lt("yb")])
                    transposes(ft, [yb[:, j * 128:(j + 1) * 128] for j in range(8)], [lt("yb")],
                               lambda i0, m, t=t: ybT[:, i0:i0 + m, t * 128:(t + 1) * 128], lt("ybT"))
                def fm_chunk(col0, wname, kcs, rhs_fn, rhs_lt, consume):
                    W = w_bf[wname]
                    s_, wl_ = load_w(ft, [(lambda w: w[:, 0:kcs * 512].rearrange("p (a b) -> p a b", a=kcs),
                                           W[:, col0:col0 + 512].rearrange("(a p) c -> p a c", p=128))], [lt("wbf_" + wname)])
                    wv = ft.wsl[s_][:, 0:kcs * 512].rearrange("p (a b) -> p a b", a=kcs)
                    for fl in range(4):
                        bi = prr[0] % 6; prr[0] += 1
                        for kc in range(kcs):
                            mm(pbank[bi][:], wv[:, kc, fl * 128:(fl + 1) * 128], rhs_fn(kc), kc == 0, kc == kcs - 1, [wl_, rhs_lt], [pl(bi)])
                        consume(fl, pbank[bi], pl(bi))
                for c in range(4):
                    def ga_c(fl, pb, pbl):
                        act(ost[:, fl, :], pb[:], AF.Sigmoid, [pbl], [lt("ost")])
                    fm_chunk(C_GA + c * 512, "w_in", KC, lambda kc: ft.hT[:, kc, :], lt("hT"), ga_c)
                    dma("pool", d_st["ost"], sgaT_s[c * 512:(c + 1) * 512, g * T:(g + 1) * T].rearrange("(f p) n -> p f n", p=128),
                        ost, [lt("ost")], [lt("sgaT_s")])
                for c in range(4):
                    W = w_bf["w_in"]
                    s1, wl1 = load_w(ft, [(lambda w: w[:, 0:KC * 512].rearrange("p (a b) -> p a b", a=KC),
                                           W[:, C_GB + c * 512:C_GB + (c + 1) * 512].rearrange("(a p) c -> p a c", p=128))], [lt("wbf_w_in")])
                    W2 = w_bf["w_up_sgu"]
                    s2, wl2 = load_w(ft, [(lambda w: w[:, 0:8 * 512].rearrange("p (a b) -> p a b", a=8),
                                           W2[:, c * 512:(c + 1) * 512].rearrange("(a p) c -> p a c", p=128))], [lt("wbf_w_up_sgu")])
                    wv1 = ft.wsl[s1][:, 0:KC * 512].rearrange("p (a b) -> p a b", a=KC)
                    wv2 = ft.wsl[s2][:, 0:8 * 512].rearrange("p (a b) -> p a b", a=8)
                    for fl in range(4):
                        b1 = prr[0] % 6; prr[0] += 1
                        for kc in range(KC):
                            mm(pbank[b1][:], wv1[:, kc, fl * 128:(fl + 1) * 128], ft.hT[:, kc, :], kc == 0, kc == KC - 1, [wl1, lt("hT")], [pl(b1)])
                        act(sgb[:], pbank[b1][:], AF.Sigmoid, [pl(b1)], [lt("sgb")])
                        b2 = prr[0] % 6; prr[0] += 1
                        for kc in range(8):
                            mm(pbank[b2][:], wv2[:, kc, fl * 128:(fl + 1) * 128], ybT[:, kc, :], kc == 0, kc == 7, [wl2, lt("ybT")], [pl(b2)])
                        tt(ost[:, fl, :], pbank[b2][:], sgb[:], ALU.mult, [pl(b2), lt("sgb")], [lt("ost")])
                    dma("pool", d_st["ost"], mbT_s[c * 512:(c + 1) * 512, g * T:(g + 1) * T].rearrange("(f p) n -> p f n", p=128),
                        ost, [lt("ost")], [lt("mbT_s")])
            k.barrier()
        if stage <= 1:
            with nc.Block() as block:
                k.replay(block)
            return nc
        p2 = ExitStack()
        with p2:
            def sb(name, shape, dtype):
                return p2.enter_context(nc.sbuf_tensor(name, shape, dtype))
            identf2 = sb("identf2", [128, 128], F32)
            ident2 = sb("ident2", [128, 128], BF16)
            kiT2 = sb("kiT2", [128, S], BF16)
            kc2 = sb("kc2", [128, 2, 128], BF16)
            qch = sb("qch_sb", [128, NOWN], F32)
            score2 = [sb("score%d" % i_, [128, S], F32) for i_ in range(2)]
            mask01 = sb("mask01", [128, S], BF16)
            maskT = sb("maskT", [128, NBLK, 128], BF16)
            qz = sb("qz", [128, 16, 128], BF16)
            sgq = sb("sgq", [128, 16], F32)
            qTq = sb("qTq", [128, 8, 128], BF16)
            kTh = [sb("kTh%d" % i, [128, S], BF16) for i in range(2)]
            Vh = [sb("Vh%d" % i, [128, NBLK, 129], BF16) for i in range(2)]
            pT = [sb("pT%d" % i, [128, 512], BF16) for i in range(3)]
            pmT = [sb("pmT%d" % i, [128, 512], BF16) for i in range(3)]
            ya = sb("ya", [128, 1024], BF16)
            yaT_st = sb("yaT_st", [128, 8, 128], BF16)
            bst = sb("bst", [128, 16], F32)
            thr_all = sb("thr_all", [128, NOWN], F32)
            cnt_all = sb("cnt_all", [128, NOWN], F32)
            d2 = {n: dsem("p2_" + n) for n in ("c", "qiq", "sgq", "qTq", "kT0", "kT1", "V0", "V1", "yaT", "kc")}
            p1_out = [lt(n) for n in ("kiT_s", "kT_s", "v_s", "qT_s", "qiT_s", "sgn_s")]
            cl = []
            dma("sp", d2["c"], identf2[:], cst_in[:, 0:128], [], [lt("identf2")]); cl.append("identf2")
            dma("sp", d2["c"], kiT2[0:64, :], kiT_s[:, :], p1_out, [lt("kiT2")]); cl.append("kiT2")
            dma("sp", d2["c"], kiT2[64:128, :], kiT_s[:, :], p1_out, [lt("kiT2")])
            dma("sp", d2["c"], qch[:], qch_in[:], [], [lt("qch")]); cl.append("qch")
            for nm in cl:
                lt(nm).writers[d2["c"]] = k.cnt[d2["c"]]
            dve(lambda e: e.tensor_copy(out=ident2[:], in_=identf2[:]), [lt("identf2")], [lt("ident2")])
            dve(lambda e: e.memset(qz[:], 0.0), [], [lt("qiq")])
            R0 = 16.0
            NIT = 28
            SCALE = 128.0 ** -0.5
            hrr = [0]
            irr = [0]
            osb = sb("osb", [128, 8, 132], F32)
            NB2 = NOWN if stage >= 3 else 0

            def geom(i):
                n1 = (i + 1) * 128
                return n1, 2 * n1, 2 * (i + 1)

            Dg = sb("Dg", [128, 16, 128], BF16)
            rtb = [sb("rtb%d" % i_, [128, 512], BF16) for i_ in range(4)]
            negbb = [sb("negbb%d" % i_, [128, 128], BF16) for i_ in range(2)]
            crr = [0]
            LAG = 2

            def idx_stage(i):
                n1, nk, nkb = geom(i)
                score = score2[i % 2]
                sl_ = lt("score%d" % (i % 2))
                segs = [(0, 0, n1), (n1, NOWN * 128, n1)]
                qsrc = qiT_s.rearrange("(hp e d) n -> d e hp n", e=2, d=64)
                qdst = qz[:].rearrange("p (hp e) n -> p e hp n", e=2)
                for e_ in range(2):
                    dma("sp", d2["qiq"], qdst[e_ * 64:(e_ + 1) * 64, e_, :, :], qsrc[:, e_, :, i * 128:(i + 1) * 128], p1_out, [lt("qiq")])
                dma("sp", d2["sgq"], sgq[:], sgn_s[i * 128:(i + 1) * 128, :], p1_out, [lt("sgq")])
                for si_ in range(2):
                    dma("sp", d2["kc"], kc2[:, si_, :], kch_in[:, si_ * NOWN * 128 + i * 128:si_ * NOWN * 128 + (i + 1) * 128].partition_broadcast(128),
                        [], [lt("kc2")])
                k.op("pool", lambda e: e.tensor_tensor(
                    out=Dg[:], in0=ident2[:].unsqueeze(1).broadcast_to([128, 16, 128]),
                    in1=sgq[:].unsqueeze(2).broadcast_to([128, 16, 128]), op=ALU.mult),
                    [lt("ident2"), lt("sgq")], [lt("Dg")])
                for si_, (l0, c0, n) in enumerate(segs):
                    off = 0
                    while off < n:
                        w = min(512, n - off)
                        cc = crr[0] % 2; crr[0] += 1
                        need_bias = (off + w == n)
                        nb_ = negbb[cc]
                        if need_bias:
                            kv = kc2[:, si_, :]
                            k.op("pool", lambda e, nb_=nb_, kv=kv, i=i: e.tensor_scalar(
                                out=nb_[:, 0:128], in0=kv, scalar1=qch[:, i:i + 1], scalar2=NEG, op0=ALU.is_gt, op1=ALU.mult),
                                [lt("kc2"), lt("qch")], [lt("negbb%d" % cc)])
                        sbk = 3 + cc
                        for hh in range(16 + LAG):
                            if hh < 16:
                                h = hh
                                bi = irr[0] % 3; irr[0] += 1
                                mm(pbank[bi][:, 0:w], qz[:, h, :], kiT2[:, c0 + off:c0 + off + w], True, True,
                                   [lt("qiq"), lt("kiT2")], [pl(bi)])
                                act(rtb[h % 4][:, 0:w], pbank[bi][:, 0:w], AF.Relu, [pl(bi)], [lt("rtb%d" % (h % 4))])
                            h2 = hh - LAG
                            if h2 >= 0:
                                last = (h2 == 15) and not need_bias
                                mm(pbank[sbk][:, 0:w], Dg[:, h2, :], rtb[h2 % 4][:, 0:w], h2 == 0, last,
                                   [lt("Dg"), lt("rtb%d" % (h2 % 4))], [pl(sbk)], sig=True)
                        if need_bias:
                            mm(pbank[sbk][:, w - 128:w], ident2[:], nb_[:, 0:128], False, True, [lt("ident2"), lt("negbb%d" % cc)], [pl(sbk)], sig=True)
                        act(score[:, l0 + off:l0 + off + w], pbank[sbk][:, 0:w], AF.Copy, [pl(sbk)], [sl_])
                        off += w

            def bis_stage(i):
                n1, nk, nkb = geom(i)
                score = score2[i % 2]
                sl_ = lt("score%d" % (i % 2))
                dve(lambda e: e.memset(bst[:, 0:1], 0.0), [], [lt("bst_t")])
                for it in range(NIT):
                    step = R0 / (2 ** it)
                    ts(mask01[:, 0:nk], score[:, 0:nk], bst[:, 0:1], None, ALU.is_ge, ALU.add, [sl_, lt("bst_t")],
                       [lt("mask01"), lt("bst_c")], accum_out=bst[:, 1:2])
                    ts(bst[:, 2:3], bst[:, 1:2], TOPK - 0.5, step, ALU.is_ge, ALU.mult, [lt("bst_c")], [lt("bst_m")])
                    dec = -step if it == NIT - 1 else -step / 2
                    stt(bst[:, 0:1], bst[:, 2:3], dec, bst[:, 0:1], ALU.add, ALU.add, [lt("bst_m"), lt("bst_t")], [lt("bst_t")])
                ts(mask01[:, 0:nk], score[:, 0:nk], bst[:, 0:1], None, ALU.is_ge, ALU.add, [sl_, lt("bst_t")],
                   [lt("mask01"), lt("bst_c")], accum_out=bst[:, 1:2])
                if debug:
                    dve(lambda e, i=i: e.tensor_copy(out=thr_all[:, i:i + 1], in_=bst[:, 0:1]), [lt("bst_t")], [lt("thr_all")])
                    dve(lambda e, i=i: e.tensor_copy(out=cnt_all[:, i:i + 1], in_=bst[:, 1:2]), [lt("bst_c")], [lt("cnt_all")])

            def mT_stage(i):
                n1, nk, nkb = geom(i)
                kb = 0
                while kb < nkb:
                    m = min(4, nkb - kb)
                    pbv = pbank[4][:].bitcast(BF16)
                    for j in range(m):
                        k.op("pe", lambda e, pbv=pbv, j=j, kb=kb: e.transpose(
                            pbv[:, j * 128:(j + 1) * 128], mask01[:, (kb + j) * 128:(kb + j + 1) * 128], ident2[:]),
                            [lt("mask01"), lt("ident2")], [pl(4)], sig=(j == m - 1))
                    act(maskT[:, kb:kb + m, :], pbv[:, 0:m * 128].rearrange("p (a b) -> p a b", a=m), AF.Copy, [pl(4)], [lt("maskT")])
                    kb += m

            def att_stage(i):
                n1, nk, nkb = geom(i)
                dma("sp", d2["qTq"], qTq[:], qT_s.rearrange("(h d) n -> d h n", d=128)[:, :, i * 128:(i + 1) * 128], p1_out, [lt("qTq")])
                for h in range(8):
                    s_ = hrr[0] % 2; hrr[0] += 1
                    kl, vl = lt("kTh%d" % s_), lt("Vh%d" % s_)
                    dma("sp", d2["kT%d" % s_], kTh[s_][:, 0:n1], kT_s[h * 128:(h + 1) * 128, 0:n1], p1_out, [kl])
                    dma("sp", d2["kT%d" % s_], kTh[s_][:, n1:nk], kT_s[h * 128:(h + 1) * 128, NOWN * 128:NOWN * 128 + n1], p1_out, [kl])
                    dma("sp", d2["V%d" % s_], Vh[s_][:, 0:i + 1, :], v_s[h, :, 0:i + 1, :], p1_out, [vl])
                    dma("sp", d2["V%d" % s_], Vh[s_][:, i + 1:nkb, :], v_s[h, :, NOWN:NOWN + i + 1, :], p1_out, [vl])
                    G = []
                    kb = 0
                    while kb < nkb:
                        m = min(4, nkb - kb)
                        G.append((kb, m))
                        kb += m
                    SK = 2
                    QB = [5, 6, 0]
                    for gi in range(len(G) + SK):
                        if gi < len(G):
                            kb, m = G[gi]
                            b3 = gi % 3
                            lb = QB[b3]
                            for j in range(m):
                                mm(pbank[lb][:, j * 128:(j + 1) * 128], kTh[s_][:, (kb + j) * 128:(kb + j + 1) * 128], qTq[:, h, :], True, True,
                                   [kl, lt("qTq")], [pl(lb)], sig=(j == m - 1))
                            p_ = pT[b3]; pm_ = pmT[b3]
                            act(p_[:, 0:m * 128], pbank[lb][:, 0:m * 128], AF.Exp, [pl(lb)], [lt("pT%d" % b3)], scale=SCALE)
                            mview = maskT[:, kb:kb + m, :].rearrange("p a b -> p (a b)")
                            k.op("pool", lambda e, pm_=pm_, p_=p_, mview=mview, m=m: e.tensor_tensor(
                                out=pm_[:, 0:m * 128], in0=p_[:, 0:m * 128], in1=mview, op=ALU.mult),
                                [lt("pT%d" % b3), lt("maskT")], [lt("pmT%d" % b3)])
                        g2 = gi - SK
                        if g2 >= 0:
                            kb, m = G[g2]
                            b3 = g2 % 3
                            pm_ = pmT[b3]
                            for j in range(m):
                                mm(pbank[7][:, 0:129], pm_[:, j * 128:(j + 1) * 128], Vh[s_][:, kb + j, :], kb + j == 0, kb + j == nkb - 1,
                                   [lt("pmT%d" % b3), vl], [pl(7)], sig=(j == m - 1))
                    act(osb[:, h, 0:129], pbank[7][:, 0:129], AF.Copy, [pl(7)], [lt("osb")])

            def fin_stage(i):
                dve(lambda e: e.reciprocal(bst[:, 8:16], osb[:, :, 128]), [lt("osb")], [lt("bst_r")])
                tt(ya[:].rearrange("p (a b) -> p a b", a=8), osb[:, :, 0:128], bst[:, 8:16].unsqueeze(2).broadcast_to([128, 8, 128]),
                   ALU.mult, [lt("osb"), lt("bst_r")], [lt("ya")])
                i0 = 0
                while i0 < 8:
                    pbv = pbank[4][:].bitcast(BF16)
                    for j in range(4):
                        k.op("pe", lambda e, pbv=pbv, j=j, i0=i0: e.transpose(
                            pbv[:, j * 128:(j + 1) * 128], ya[:, (i0 + j) * 128:(i0 + j + 1) * 128], ident2[:]),
                            [lt("ya"), lt("ident2")], [pl(4)], sig=(j == 3))
                    act(yaT_st[:, i0:i0 + 4, :], pbv[:, 0:512].rearrange("p (a b) -> p a b", a=4), AF.Copy, [pl(4)], [lt("yaT_st")])
                    i0 += 4
                dma("pool", d2["yaT"], yaT_s.rearrange("(h d) n -> d h n", d=128)[:, :, i * 128:(i + 1) * 128], yaT_st[:],
                    [lt("yaT_st")], [lt("yaT_s")])

            if NB2:
                idx_stage(0); idx_stage(1); bis_stage(0); mT_stage(0)
            for i in range(NB2):
                if i + 2 < NB2:
                    idx_stage(i + 2)
                att_stage(i)
                if i + 1 < NB2:
                    bis_stage(i + 1)
                fin_stage(i)
                if i + 1 < NB2:
                    mT_stage(i + 1)
            if debug:
                dma("pool", d2["yaT"], dbg["thr"][:], thr_all[:], [lt("thr_all")], [lt("dbg_thr")])
                dma("pool", d2["yaT"], dbg["cnt"][:], cnt_all[:], [lt("cnt_all")], [lt("dbg_cnt")])
            k.barrier()

        p3 = ExitStack()
        with p3:
            ft = alloc_ffn(p3, ["norm_ffn2"], "_p3")
            ft.trr = 0
            finish_consts(ft)
            aflat = ft.aT[:].rearrange("p a b -> p (a b)")
            lt("aT")
            off = [0]
            def take(name, shape, dtype):
                n = int(np.prod(shape[1:])) * (4 if dtype == F32 else 2)
                v = aflat[:, off[0] // 2:(off[0] + n) // 2]
                if dtype == F32:
                    v = v.bitcast(F32)
                off[0] += n
                lt(name, "aT")
                if len(shape) == 3:
                    v = v.rearrange("p (a b) -> p a b", a=shape[1])
                return v
            yaT = take("yaT", [128, 8, T], BF16)
            sga_c = take("sga_c", [128, 4, 512], F32)
            mb_c = take("mb_c", [128, 4, 512], F32)
            tmpm = take("tmpm", [128, 512], F32)
            d3 = {n: dsem("p3_" + n) for n in ("ya", "sga", "mb")}
            for g in range(NG_OWN):
                dma("sp", d_x, ft.xg[:], xmid_s[g * T:(g + 1) * T, :].rearrange("(t p) d -> p t d", p=128), [lt("xmid_s")], [lt("xg")])
                if stage >= 3:
                    dma("sp", d3["ya"], yaT, yaT_s.rearrange("(h d) n -> d h n", d=128)[:, :, g * T:(g + 1) * T], [lt("yaT_s")], [lt("yaT")])
                mT = ft.hT
                for c in range(4):
                    dma("sp", d3["sga"], sga_c, sgaT_s[c * 512:(c + 1) * 512, g * T:(g + 1) * T].rearrange("(f p) n -> p f n", p=128),
                        [lt("sgaT_s")], [lt("sga_c")])
                    dma("sp", d3["mb"], mb_c, mbT_s[c * 512:(c + 1) * 512, g * T:(g + 1) * T].rearrange("(f p) n -> p f n", p=128),
                        [lt("mbT_s")], [lt("mb_c")])
                    if stage >= 3:
                        W = w_bf["w_up_attn"]
                        s_, wl_ = load_w(ft, [(lambda w: w[:, 0:8 * 512].rearrange("p (a b) -> p a b", a=8),
                                               W[:, c * 512:(c + 1) * 512].rearrange("(a p) c -> p a c", p=128))], [lt("wbf_w_up_attn")])
                        wv = ft.wsl[s_][:, 0:8 * 512].rearrange("p (a b) -> p a b", a=8)
                    for fl in range(4):
                        if stage >= 3:
                            bi = (c * 4 + fl) % 6
                            for kc in range(8):
                                mm(pbank[bi][:], wv[:, kc, fl * 128:(fl + 1) * 128], yaT[:, kc, :], kc == 0, kc == 7, [wl_, lt("yaT")], [pl(bi)])
                            tt(tmpm[:], pbank[bi][:], sga_c[:, fl, :], ALU.mult, [pl(bi), lt("sga_c")], [lt("tmpm")])
                            tt(mT[:, 4 * c + fl, :], tmpm[:], mb_c[:, fl, :], ALU.add, [lt("tmpm"), lt("mb_c")], [lt("hT")])
                        else:
                            dve(lambda e, c=c, fl=fl: e.tensor_copy(out=mT[:, 4 * c + fl, :], in_=mb_c[:, fl, :]), [lt("mb_c"), lt("sga_c")], [lt("hT")])
                for c in range(4):
                    W = w_bf["w_out"]
                    s_, wl_ = load_w(ft, [(lambda w: w[:, 0:KC * 512].rearrange("p (a b) -> p a b", a=KC),
                                           W[:, c * 512:(c + 1) * 512].rearrange("(a p) c -> p a c", p=128))], [lt("wbf_w_out")])
                    wv = ft.wsl[s_][:, 0:KC * 512].rearrange("p (a b) -> p a b", a=KC)
                    for t in range(NT):
                        bi = (c * 4 + t) % 6
                        for kc in range(KC):
                            mm(pbank[bi][:], mT[:, kc, t * 128:(t + 1) * 128], wv[:, kc, :], kc == 0, kc == KC - 1, [wl_, lt("hT")], [pl(bi)])
                        xs = ft.xg[:, t, c * 512:(c + 1) * 512]
                        tt(xs, pbank[bi][:], xs, ALU.add, [pl(bi), lt("xg")], [lt("xg")])
                rmsnorm_T(ft, "norm_ffn2")
                ffn(ft, w_bf["ffn2_w_gate"], w_bf["ffn2_w_up"], w_bf["ffn2_w_down"],
                    [lt("wbf_ffn2_w_gate"), lt("wbf_ffn2_w_up"), lt("wbf_ffn2_w_down")])
                dma("pool", d_out, out_d[g * T:(g + 1) * T, :].rearrange("(t p) d -> p t d", p=128), ft.xg[:], [lt("xg")], [lt("out")])
            k.barrier()
        if debug:
            d_dbg = dsem("dbg")
            for o, src in dbg_copies:
                dma("pool", d_dbg, o, src, [], [lt("dbgout")])
            k.barrier()
        with nc.Block() as block:
            k.replay(block)
    return nc


_CACHE = {}


def _consts():
    c = np.zeros((128, 152), np.float32)
    c[:, 0:128] = np.eye(128, dtype=np.float32)
    f_a = (np.float32(ROPE_THETA) ** (-(np.arange(0, 32, 2, dtype=np.float32)) / np.float32(32))).astype(np.float32)
    f_i = (np.float32(ROPE_THETA) ** (-(np.arange(0, 16, 2, dtype=np.float32)) / np.float32(16))).astype(np.float32)
    c[:, 128:144] = f_a[None, :]
    c[:, 144:152] = f_i[None, :]
    return c


def kernel(**inputs):
    stage = inputs.pop("_stage", 99)
    debug = inputs.pop("_debug", False)
    x = np.asarray(inputs["x"])
    positions = np.asarray(inputs["positions"])
    if (stage, debug) not in _CACHE:
        _CACHE[(stage, debug)] = build_program(stage, debug)
    nc = _CACHE[(stage, debug)]
    cst = _consts()
    in_maps = []
    for c in range(8):
        b, r = c // 2, c % 2
        order = own_blocks(r) + other_blocks(r)
        tok = (np.asarray(order)[:, None] * 128 + np.arange(128)[None, :]).reshape(-1)
        m = {"x": np.ascontiguousarray(x[b][tok]),
             "pos": np.ascontiguousarray(positions[b][tok].reshape(NBLK, 128).T.astype(np.int32)),
             "kch": np.ascontiguousarray((tok // 64).astype(np.float32)[None, :].astype(ml_dtypes.bfloat16)),
             "qch": np.ascontiguousarray((tok[:NOWN * 128] // 64).astype(np.float32).reshape(NOWN, 128).T),
             "cst": cst}
        for n in ["ffn1_w_gate", "ffn1_w_up", "ffn1_w_down", "w_in", "w_up_attn", "w_up_sgu", "w_out",
                  "ffn2_w_gate", "ffn2_w_up", "ffn2_w_down"]:
            m[n] = np.asarray(inputs[n])[0]
        for n in ["norm_ffn1", "norm_mix", "norm_ffn2", "q_norm", "k_norm", "idx_k_norm", "sgu_v_norm"]:
            m[n] = np.asarray(inputs[n])[0][None, :]
        m["sgu_w_s"] = np.asarray(inputs["sgu_w_s"])[0]
        m["sgu_b_s"] = np.asarray(inputs["sgu_b_s"])[0]
        in_maps.append(m)
    res = run_bass_kernel_spmd(nc, in_maps, core_ids=list(range(8)))
    if debug:
        _CACHE["last_results"] = res.results
    out = np.zeros((4, S, D), np.float32)
    for c in range(8):
        b, r = c // 2, c % 2
        ob = own_blocks(r)
        o = np.asarray(res.results[c]["out"]).reshape(NOWN, 128, D)
        for i, blk in enumerate(ob):
            out[b, blk * 128:(blk + 1) * 128] = o[i]
    return out
```

```python
import numpy as np
import ml_dtypes
import concourse.bass as bass
import concourse.mybir as mybir
from concourse.bass_utils import run_bass_kernel_spmd

F32 = mybir.dt.float32
BF16 = mybir.dt.bfloat16
I32 = mybir.dt.int32
AF = mybir.ActivationFunctionType
ALU = mybir.AluOpType
AX = mybir.AxisListType

D = 2048
KC = 16
FF = 5632
JC = 44
S = 8192
NBLK = 64
NOWN = 32
T = 512
NT = 4
NG_OWN = 8
NG_ALL = 16
IN_COLS = 10320
EPS = 1e-6
ROPE_THETA = 500000.0
TOPK = 256
NEG = -1.0e30
C_Q, C_K, C_V, C_QI, C_KI, C_WI, C_U, C_VS, C_GA, C_GB = 0, 1024, 2048, 3072, 4096, 4160, 4176, 5200, 6224, 8272


class LT:
    def __init__(self, name, parent=None):
        self.name = name
        self.writers = {}
        self.readers = {}
        self.parent = parent
        self.children = []
        if parent is not None:
            parent.children.append(self)

    def rel(self):
        out = [self]
        if self.parent is not None:
            out.append(self.parent)
        out += self.children
        return out


class K:
    def __init__(self, nc):
        self.nc = nc
        self.eng = {}
        self.sems = {}
        self.cnt = {}
        self.prog = {n: [] for n in ("pe", "act", "dve", "pool", "sp")}
        self.waited = {n: {} for n in self.prog}
        self.dma_sems = []
        self.ctx = []

    def add_engine_sem(self, name, sem):
        self.sems[name] = sem
        self.cnt[name] = 0

    def new_dma_sem(self, sem):
        key = "dma%d" % len(self.dma_sems)
        self.dma_sems.append(key)
        self.sems[key] = sem
        self.cnt[key] = 0
        return key

    def _deps(self, eng, reads, writes):
        deps = {}
        def add(d, skip_own=False):
            for k, v in d.items():
                if skip_own and k == eng:
                    continue
                if deps.get(k, 0) < v:
                    deps[k] = v
        for t0 in reads:
            for t in t0.rel():
                add(t.writers)
        for t0 in writes:
            for t in t0.rel():
                add(t.writers, True)
                add(t.readers, True)
        out = []
        for k, v in deps.items():
            if self.waited[eng].get(k, 0) >= v:
                continue
            self.waited[eng][k] = v
            out.append((k, v))
        return out

    def op(self, eng, fn, reads=(), writes=(), sig=True):
        waits = self._deps(eng, reads, writes)
        if sig:
            self.cnt[eng] += 1
            v = self.cnt[eng]
        else:
            v = self.cnt[eng] + 1
        self.prog[eng].append((waits, fn, eng, 1 if sig else 0))
        for t in reads:
            if t.readers.get(eng, 0) < v:
                t.readers[eng] = v
        for t in writes:
            t.writers[eng] = v
        return (eng, v)

    def begin_write(self, t):
        pass

    def dma(self, queue, semkey, fn, reads=(), writes=()):
        waits = self._deps(queue, reads, writes)
        self.cnt[semkey] += 16
        v = self.cnt[semkey]
        self.prog[queue].append((waits, fn, semkey, 16))
        for t in reads:
            if t.readers.get(semkey, 0) < v:
                t.readers[semkey] = v
        for t in writes:
            t.writers[semkey] = v
        return (semkey, v)

    def fresh(self, t):
        return t

    def barrier(self):
        keys = list(self.sems.keys())
        for eng in ("sp", "pool", "act", "dve", "pe"):
            self.final_wait(eng, keys)

    def final_wait(self, eng, keys):
        waits = []
        for k in keys:
            v = self.cnt[k]
            if v > 0 and self.waited[eng].get(k, 0) < v:
                waits.append((k, v))
                self.waited[eng][k] = v
        self.prog[eng].append((waits, None, None, 0))

    def replay(self, block):
        nc = self.nc
        sems = self.sems
        def run(e, lst):
            for waits, fn, inckey, inc in lst:
                for k, v in waits:
                    e.wait_ge(sems[k], v)
                if fn is not None:
                    ins = fn(e)
                    if inc:
                        ins.then_inc(sems[inckey], inc)
        @block.tensor
        def _(e):
            run(e, self.prog["pe"])
        @block.scalar
        def _(e):
            run(e, self.prog["act"])
        @block.vector
        def _(e):
            run(e, self.prog["dve"])
        @block.gpsimd
        def _(e):
            run(e, self.prog["pool"])
        @block.sync
        def _(e):
            run(e, self.prog["sp"])


def own_blocks(r):
    out = []
    for kq in range(NBLK // 4):
        out += [4 * kq + (0 if r == 0 else 1), 4 * kq + (3 if r == 0 else 2)]
    return out


def other_blocks(r):
    return own_blocks(1 - r)


def _round_bits(x, bits):
    import math
    if x == 0:
        return 0.0
    e = math.floor(math.log2(abs(x)))
    q = 2.0 ** (e - bits + 1)
    return round(x / q) * q


def build_program(stage=99, debug=False):
    from contextlib import ExitStack
    nc = bass.Bass("TRN2", target_bir_lowering=False)
    dt = nc.dram_tensor
    x_in = dt("x", [S, D], F32, kind="ExternalInput").ap()
    pos_in = dt("pos", [128, NBLK], I32, kind="ExternalInput").ap()
    kch_in = dt("kch", [1, S], BF16, kind="ExternalInput").ap()
    qch_in = dt("qch", [128, NOWN], F32, kind="ExternalInput").ap()
    cst_in = dt("cst", [128, 128 + 24], F32, kind="ExternalInput").ap()
    w_names = ["ffn1_w_gate", "ffn1_w_up", "ffn1_w_down", "w_in", "w_up_attn", "w_up_sgu", "w_out",
               "ffn2_w_gate", "ffn2_w_up", "ffn2_w_down"]
    w_shapes = {"ffn1_w_gate": [D, FF], "ffn1_w_up": [D, FF], "ffn1_w_down": [FF, D], "w_in": [D, IN_COLS],
                "w_up_attn": [1024, D], "w_up_sgu": [1024, D], "w_out": [D, D],
                "ffn2_w_gate": [D, FF], "ffn2_w_up": [D, FF], "ffn2_w_down": [FF, D]}
    w_f32 = {n: dt(n, w_shapes[n], F32, kind="ExternalInput").ap() for n in w_names}
    vec_names = {"norm_ffn1": D, "norm_mix": D, "norm_ffn2": D, "q_norm": 128, "k_norm": 128,
                 "idx_k_norm": 64, "sgu_v_norm": 1024}
    vec_in = {n: dt(n, [1, l], F32, kind="ExternalInput").ap() for n, l in vec_names.items()}
    ws_in = dt("sgu_w_s", [8, 128, 128], F32, kind="ExternalInput").ap()
    bs_in = dt("sgu_b_s", [8, 128], F32, kind="ExternalInput").ap()
    out_d = dt("out", [NOWN * 128, D], F32, kind="ExternalOutput").ap()
    w_bf = {n: dt(n + "_bf", w_shapes[n], BF16).ap() for n in w_names}
    xmid_s = dt("xmid_s", [NOWN * 128, D], F32).ap()
    qT_s = dt("qT_s", [1024, NOWN * 128], BF16).ap()
    kT_s = dt("kT_s", [1024, S], BF16).ap()
    v_s = dt("v_s", [8, 128, NBLK, 129], BF16).ap()
    kiT_s = dt("kiT_s", [64, S], BF16).ap()
    qiT_s = dt("qiT_s", [1024, NOWN * 128], BF16).ap()
    sgn_s = dt("sgn_s", [NOWN * 128, 16], F32).ap()
    sgaT_s = dt("sgaT_s", [D, NOWN * 128], F32).ap()
    mbT_s = dt("mbT_s", [D, NOWN * 128], F32).ap()
    yaT_s = dt("yaT_s", [1024, NOWN * 128], BF16).ap()
    dbg = {}
    dbg_copies = []
    if debug:
        dbg["thr"] = dt("dbg_thr", [128, NOWN], F32, kind="ExternalOutput").ap()
        dbg["cnt"] = dt("dbg_cnt", [128, NOWN], F32, kind="ExternalOutput").ap()
        for nm, src in (("q", qT_s[:, 0:128]), ("k", kT_s[:, 0:128]), ("ki", kiT_s[:, 0:128]), ("qi", qiT_s[:, 0:128]),
                        ("sgn", sgn_s[0:128, :]), ("sga", sgaT_s[:, 0:128]), ("mb", mbT_s[:, 0:128]), ("ya", yaT_s[:, 0:128]),
                        ("xmid", xmid_s[0:128, :]), ("v", v_s[:, :, 0, :]), ("k1", kT_s[:, 4096:4224])):
            o = dt("dbg_" + nm, list(src.shape), src.dtype, kind="ExternalOutput").ap()
            dbg_copies.append((o, src))

    top = ExitStack()
    with top:
        def sem(name):
            return top.enter_context(nc.semaphore(name))
        k = K(nc)
        for n in ("pe", "act", "dve", "pool", "sp"):
            k.add_engine_sem(n, sem("s_" + n))
        def dsem(name):
            return k.new_dma_sem(sem("d_" + name))
        L = {}
        def lt(name, parent=None):
            if name not in L:
                L[name] = LT(name, L[parent] if parent else None)
            return L[name]
        pbank = [top.enter_context(nc.psum_tensor("pb%d" % i, [128, 512], F32)) for i in range(8)]
        def pl(i):
            return lt("pb%d" % i)

        def mm(out, lhsT, rhs, start, stop, reads, writes, sig=None):
            return k.op("pe", lambda e: e.matmul(out, lhsT, rhs, start=start, stop=stop), reads, writes,
                        sig=(stop if sig is None else sig))
        def act(out, in_, func, reads, writes, **kw):
            return k.op("act", lambda e: e.activation(out=out, in_=in_, func=func, **kw), reads, writes)
        def dve(fn, reads, writes):
            return k.op("dve", fn, reads, writes)
        def tt(out, in0, in1, op, reads, writes):
            return k.op("dve", lambda e: e.tensor_tensor(out=out, in0=in0, in1=in1, op=op), reads, writes)
        def ts(out, in0, s1, s2, op0, op1, reads, writes, **kw):
            if op1 is None:
                return k.op("dve", lambda e: e.tensor_scalar(out=out, in0=in0, scalar1=s1, scalar2=None, op0=op0, **kw), reads, writes)
            return k.op("dve", lambda e: e.tensor_scalar(out=out, in0=in0, scalar1=s1, scalar2=s2, op0=op0, op1=op1, **kw), reads, writes)
        def stt(out, in0, scalar, in1, op0, op1, reads, writes):
            return k.op("dve", lambda e: e.scalar_tensor_tensor(out=out, in0=in0, scalar=scalar, in1=in1, op0=op0, op1=op1), reads, writes)
        def dma(queue, semkey, out, in_, reads, writes, **kw):
            return k.dma(queue, semkey, lambda e: e.dma_start(out=out, in_=in_, **kw), reads, writes)

        d_cast = {n: dsem("cast_" + n) for n in w_names if not n.startswith("ffn1")}
        GU_PIECES = [(0, 1536), (1536, 3072), (3072, 4608), (4608, 5632)]
        d_misc = dsem("misc")
        d_x = dsem("xload")
        NWS = 3
        d_w = [dsem("wsl%d" % i) for i in range(NWS)]
        d_out = dsem("outst")

        for pi, (c0_, c1_) in enumerate(GU_PIECES):
            for n in ("ffn1_w_gate", "ffn1_w_up"):
                ds_ = dsem("cast_%s_p%d" % (n, pi))
                dma("pool", ds_, w_bf[n][:, c0_:c1_], w_f32[n][:, c0_:c1_], [], [lt("wbf_%s_p%d" % (n, pi))])
        for pi in range(4):
            ds_ = dsem("cast_ffn1_w_down_p%d" % pi)
            dma("pool", ds_, w_bf["ffn1_w_down"][pi * 1408:(pi + 1) * 1408, :], w_f32["ffn1_w_down"][pi * 1408:(pi + 1) * 1408, :],
                [], [lt("wbf_ffn1_w_down_p%d" % pi)])
        def cast_weights(names):
            for n in names:
                rows, cols = w_shapes[n]
                f = 1
                while cols // f > 2048 or cols % f:
                    f += 1
                nsplit = 4 if rows >= 2048 else 1
                rs = rows // nsplit
                for i in range(nsplit):
                    s_ap = w_f32[n][i * rs:(i + 1) * rs, :].rearrange("r (a b) -> r a b", a=f)
                    d_ap = w_bf[n][i * rs:(i + 1) * rs, :].rearrange("r (a b) -> r a b", a=f)
                    dma("pool", d_cast[n], d_ap, s_ap, [], [lt("wbf_" + n)])
        cast_weights(["w_in", "w_up_sgu"])
        LATE_CASTS = ["w_up_attn", "w_out", "ffn2_w_gate", "ffn2_w_up", "ffn2_w_down"]

        class FFNTiles:
            pass

        def alloc_ffn(es, gnames, sfx=""):
            ft = FFNTiles()
            def sb(name, shape, dtype):
                return es.enter_context(nc.sbuf_tensor(name + sfx, shape, dtype))
            ft.ident_f = sb("ident_f", [128, 128], F32)
            ft.ident = sb("ident", [128, 128], BF16)
            ft.gbc = {n: sb("g_" + n, [128, D], F32) for n in gnames}
            ft.xg = sb("xg", [128, NT, D], F32)
            ft.xn = [sb("xn%d" % i, [128, D], BF16) for i in range(2)]
            ft.hT = sb("hT", [128, KC, T], BF16)
            ft.aT = sb("aT", [128, JC, T], BF16)
            ft.wsl = [sb("wsl%d" % i, [128, 8192], BF16) for i in range(NWS)]
            ft.sgt = [sb("sgt%d" % i, [128, 512], F32) for i in range(2)]
            ft.st1 = sb("st1", [128, 8], F32)
            ft.junk = sb("junk", [128, D], BF16)
            ft.sb = sb
            ft.wrr = 0
            cl = []
            dma("sp", d_misc, ft.ident_f[:], cst_in[:, 0:128], [], [lt("ident_f")]); cl.append("ident_f")
            for n in gnames:
                dma("sp", d_misc, ft.gbc[n][:], vec_in[n].partition_broadcast(128), [], [lt("g_" + n)]); cl.append("g_" + n)
            ft.cl = cl
            return ft

        def finish_consts(ft):
            for nm in ft.cl:
                lt(nm).writers[d_misc] = k.cnt[d_misc]
            dve(lambda e: e.tensor_copy(out=ft.ident[:], in_=ft.ident_f[:]), [lt("ident_f")], [lt("ident")])

        def load_w(ft, parts, reads_lt):
            s = ft.wrr % NWS
            ft.wrr += 1
            t = lt("wsl%d" % s)
            for dst_fn, src in parts:
                dma("sp", d_w[s], dst_fn(ft.wsl[s]), src, reads_lt, [t])
            return s, t

        def transposes(ft, srcs, src_lts, dst_fn, dst_lt, nparts=128):
            n = len(srcs)
            i0 = 0
            while i0 < n:
                m = min(4, n - i0)
                bi = 6 + (ft.trr % 2)
                ft.trr += 1
                pbv = pbank[bi][:].bitcast(BF16)
                for j in range(m):
                    src = srcs[i0 + j]
                    w = src.shape[1]
                    k.op("pe", lambda e, pbv=pbv, j=j, src=src, w=w: e.transpose(
                        pbv[0:w, j * 128:(j + 1) * 128], src, ft.ident[:]),
                        list(src_lts) + [lt("ident")], [pl(bi)], sig=(j == m - 1))
                w = srcs[i0].shape[1]
                act(dst_fn(i0, m), pbv[0:w, 0:m * 128].rearrange("p (a b) -> p a b", a=m), AF.Copy, [pl(bi)], [dst_lt])
                i0 += m

        def rmsnorm_T(ft, gname):
            g_tile = ft.gbc[gname]
            st1 = ft.st1
            for t in range(NT):
                xt = ft.xg[:, t, :]
                sl = ft.xn[t % 2]
                sl_lt = lt("xn%d" % (t % 2))
                act(ft.junk[:], xt, AF.Square, [lt("xg")], [lt("junk"), lt("st1")], accum_out=st1[:, 0:1])
                act(st1[:, 1:2], st1[:, 0:1], AF.Sqrt, [lt("st1")], [lt("st1b")], scale=1.0 / D, bias=EPS)
                dve(lambda e: e.reciprocal(st1[:, 2:3], st1[:, 1:2]), [lt("st1b")], [lt("st1c")])
                stt(sl[:], xt, st1[:, 2:3], g_tile[:], ALU.mult, ALU.mult, [lt("xg"), lt("st1c"), lt("g_" + gname)], [sl_lt])
                transposes(ft, [sl[:, kc * 128:(kc + 1) * 128] for kc in range(KC)], [sl_lt],
                           lambda i0, m, t=t: ft.hT[:, i0:i0 + m, t * 128:(t + 1) * 128], lt("hT"))

        def ffn(ft, wg, wu, wd, first_readers, fine=None):
            JB = 2
            for jb in range(JC // JB):
                c0 = jb * JB * 128
                if fine:
                    pi_ = [p for p, (a_, b_) in enumerate(GU_PIECES) if a_ <= c0 < b_][0]
                    first_readers = [lt("wbf_%s_w_gate_p%d" % (fine, pi_)), lt("wbf_%s_w_up_p%d" % (fine, pi_))]
                s, wl = load_w(ft, [
                    (lambda w: w[:, 0:4096].rearrange("p (a b) -> p a b", a=KC),
                     wg[:, c0:c0 + 256].rearrange("(a p) c -> p a c", p=128)),
                    (lambda w: w[:, 4096:8192].rearrange("p (a b) -> p a b", a=KC),
                     wu[:, c0:c0 + 256].rearrange("(a p) c -> p a c", p=128))], first_readers)
                wgv = ft.wsl[s][:, 0:4096].rearrange("p (a b) -> p a b", a=KC)
                wuv = ft.wsl[s][:, 4096:8192].rearrange("p (a b) -> p a b", a=KC)
                for jj in range(JB):
                    j = jb * JB + jj
                    bi = (j % 2) * 2
                    pg, pu = pbank[bi], pbank[bi + 1]
                    for kc in range(KC):
                        mm(pg[:], wgv[:, kc, jj * 128:(jj + 1) * 128], ft.hT[:, kc, :], kc == 0, kc == KC - 1,
                           [wl, lt("hT")], [pl(bi)])
                    for kc in range(KC):
                        mm(pu[:], wuv[:, kc, jj * 128:(jj + 1) * 128], ft.hT[:, kc, :], kc == 0, kc == KC - 1,
                           [wl, lt("hT")], [pl(bi + 1)])
                    sg = ft.sgt[j % 2]
                    sgl = lt("sgt%d" % (j % 2))
                    act(sg[:], pg[:], AF.Silu, [pl(bi)], [sgl])
                    tt(ft.aT[:, j, :], sg[:], pu[:], ALU.mult, [sgl, pl(bi + 1)], [lt("aT")])
            for c in range(4):
                for jq in range(4):
                    if fine:
                        first_readers = [lt("wbf_%s_w_down_p%d" % (fine, jq))]
                    s, wl = load_w(ft, [
                        (lambda w: w[:, 0:11 * 512].rearrange("p (a b) -> p a b", a=11),
                         wd[jq * 11 * 128:(jq + 1) * 11 * 128, c * 512:(c + 1) * 512].rearrange("(a p) c -> p a c", p=128))],
                        first_readers)
                    wv = ft.wsl[s][:, 0:11 * 512].rearrange("p (a b) -> p a b", a=11)
                    for ji in range(11):
                        j = jq * 11 + ji
                        for t in range(NT):
                            bi = (c % 2) * 4 + t
                            mm(pbank[bi][:], ft.aT[:, j, t * 128:(t + 1) * 128], wv[:, ji, :], j == 0, j == JC - 1,
                               [wl, lt("aT")], [pl(bi)], sig=(j == JC - 1 or (ji == 10 and t == NT - 1)))
                for t in range(NT):
                    bi = (c % 2) * 4 + t
                    xs = ft.xg[:, t, c * 512:(c + 1) * 512]
                    stt(xs, pbank[bi][:], 0.5, xs, ALU.mult, ALU.add, [pl(bi), lt("xg")], [lt("xg")])

        p1 = ExitStack()
        with p1:
            ft = alloc_ffn(p1, ["norm_ffn1", "norm_mix"])
            ft.trr = 0
            sb = ft.sb
            gq = sb("gq", [128, 128], F32); gk = sb("gk", [128, 128], F32)
            gki = sb("gki", [128, 64], F32); gv = sb("gv", [128, 1024], F32)
            cosA = sb("cosA", [128, NBLK, 16], F32); sinA = sb("sinA", [128, NBLK, 16], F32)
            cosI = sb("cosI", [128, NBLK, 8], F32); sinI = sb("sinI", [128, NBLK, 8], F32)
            WsT = sb("WsT", [128, 8, 128], BF16)
            bsT = sb("bsT", [128, 8], F32)
            invf = sb("invf", [128, 24], F32)
            posi = sb("posi", [128, NBLK], I32)
            st2 = sb("st2", [128, 16], F32)
            wabs = sb("wabs", [128, NT, 16], F32)
            sgn_st = sb("sgn_st", [128, NT, 16], F32)
            rr = [sb("rr%d" % i, [128, 64], F32) for i in range(4)]
            kiT_st = sb("kiT_st", [64, T], BF16)
            kib = sb("kib", [128, 64], BF16)
            for (tile_, nm) in ((gq, "q_norm"), (gk, "k_norm"), (gki, "idx_k_norm"), (gv, "sgu_v_norm")):
                dma("sp", d_misc, tile_[:], vec_in[nm].partition_broadcast(128), [], [lt("c_" + nm)]); ft.cl.append("c_" + nm)
            dma("sp", d_misc, invf[:], cst_in[:, 128:152], [], [lt("invf")]); ft.cl.append("invf")
            dma("sp", d_misc, posi[:], pos_in[:], [], [lt("posi")]); ft.cl.append("posi")
            dma("sp", d_misc, bsT[:], bs_in.rearrange("g t -> t g"), [], [lt("bsT")], allow_slow_non_contiguous=True); ft.cl.append("bsT")
            finish_consts(ft)
            aflat = ft.aT[:].rearrange("p a b -> p (a b)")
            xflat = ft.xg[:].rearrange("p a b -> p (a b)")
            class Arena:
                def __init__(self, flat, esz, parent):
                    self.flat, self.esz, self.off, self.parent = flat, esz, 0, parent
                def take(self, name, shape, dtype):
                    n = int(np.prod(shape[1:])) * (4 if dtype == F32 else 2)
                    assert self.off + n <= self.flat.shape[1] * self.esz, (name, self.off, n)
                    v = self.flat[:, self.off // self.esz:(self.off + n) // self.esz]
                    if self.esz == 2 and dtype == F32:
                        v = v.bitcast(F32)
                    if self.esz == 4 and dtype == BF16:
                        v = v.bitcast(BF16)
                    self.off += n
                    lt(name, self.parent)
                    if len(shape) == 3:
                        v = v.rearrange("p (a b) -> p a b", a=shape[1])
                    return v
            A1 = Arena(aflat, 2, "aT")
            A2 = Arena(xflat, 4, "xg")
            lt("aT"); lt("xg")
            vsn = A1.take("vsn", [128, NT, 1024], BF16)
            ybT = A1.take("ybT", [128, 8, T], BF16)
            qT_st = A1.take("qT_st", [128, 8, T], BF16)
            kT_st = A1.take("kT_st", [128, 8, T], BF16)
            ost = A1.take("ost", [128, 4, 512], F32)
            yb = A1.take("yb", [128, 1024], BF16)
            qb = A1.take("qb", [128, 512], BF16)
            qb2 = A1.take("qb2", [128, 512], BF16)
            qbs = [(qb, "qb"), (qb2, "qb2")]
            v_st = A2.take("v_st", [128, NT * 8 * 129], BF16).rearrange("p (t h d) -> p t h d", t=NT, h=8)
            qiT_st = A2.take("qiT_st", [128, 8, T], BF16)
            tmpA = A2.take("tmpA", [128, 1024], F32)
            tmpB = A2.take("tmpB", [128, 1024], F32)
            tmpC = A2.take("tmpC", [128, 1024], F32)
            sgb = A2.take("sgb", [128, 512], F32)
            d_st = {n: dsem("st_" + n) for n in ("xmid", "qT", "kT", "v", "kiT", "qiT", "sgn", "ost")}

            TWO_PI = 2.0 * np.pi
            C1 = 6.28125
            C2 = _round_bits(TWO_PI - C1, 9)
            C3 = float(np.float32(TWO_PI - C1 - C2))
            MAGIC = 12582912.0
            PI_LO = 3.1415925
            posf = sb("posf", [128, NBLK], F32)
            dve(lambda e: e.tensor_copy(out=posf[:], in_=posi[:]), [lt("posi")], [lt("posf")])
            def sincos(nf, f0, cos_t, sin_t):
                n = NBLK * nf
                ang = tmpA[:, 0:n].rearrange("p (a b) -> p a b", a=NBLK)
                tt(ang, posf[:].unsqueeze(2).broadcast_to([128, NBLK, nf]),
                   invf[:, f0:f0 + nf].unsqueeze(1).broadcast_to([128, NBLK, nf]), ALU.mult,
                   [lt("posf"), lt("invf")], [lt("tmpA")])
                angf = tmpA[:, 0:n]
                kk = tmpB[:, 0:n]
                r = tmpC[:, 0:n]
                for (shift, dst) in ((0.0, sin_t), (0.25, cos_t)):
                    ts(kk, angf, 1.0 / TWO_PI, shift, ALU.mult, ALU.add, [lt("tmpA")], [lt("tmpB")])
                    ts(kk, kk, MAGIC, None, ALU.add, None, [lt("tmpB")], [lt("tmpB")])
                    ts(kk, kk, -MAGIC, None, ALU.add, None, [lt("tmpB")], [lt("tmpB")])
                    stt(r, kk, -C1, angf, ALU.mult, ALU.add, [lt("tmpB"), lt("tmpA")], [lt("tmpC")])
                    stt(r, kk, -C2, r, ALU.mult, ALU.add, [lt("tmpB"), lt("tmpC")], [lt("tmpC")])
                    stt(r, kk, -C3, r, ALU.mult, ALU.add, [lt("tmpB"), lt("tmpC")], [lt("tmpC")])
                    if shift:
                        ts(r, r, float(np.pi / 2), None, ALU.add, None, [lt("tmpC")], [lt("tmpC")])
                    ts(kk, r, PI_LO, -TWO_PI, ALU.is_gt, ALU.mult, [lt("tmpC")], [lt("tmpB")])
                    tt(r, r, kk, ALU.add, [lt("tmpC"), lt("tmpB")], [lt("tmpC")])
                    ts(kk, r, -PI_LO, TWO_PI, ALU.is_lt, ALU.mult, [lt("tmpC")], [lt("tmpB")])
                    tt(r, r, kk, ALU.add, [lt("tmpC"), lt("tmpB")], [lt("tmpC")])
                    ts(r, r, PI_LO, -PI_LO, ALU.min, ALU.max, [lt("tmpC")], [lt("tmpC")])
                    act(dst[:].rearrange("p a b -> p (a b)"), r, AF.Sin, [lt("tmpC")], [lt("rope")])
            sincos(16, 0, cosA, sinA)
            sincos(8, 16, cosI, sinI)
            for g8 in range(8):
                wtmp = tmpA[:, 0:128]
                dma("sp", d_misc, wtmp, ws_in[g8], [], [lt("tmpA")])
                dve(lambda e: e.memset(tmpA[0:64, 64:128], 0.0), [lt("tmpA")], [lt("tmpA")])
                dve(lambda e: e.tensor_copy(out=qb[:, 0:128], in_=tmpA[:, 0:128]), [lt("tmpA")], [lt("qb")])
                transposes(ft, [qb[:, 0:128]], [lt("qb")], lambda i0, m, g8=g8: WsT[:, g8:g8 + 1, :], lt("WsT"))

            def rope_apply(src3, dst3, nh, half, cos2, sin2, src_lt, dst_lt):
                x1 = src3[:, :, 0:half]; x2 = src3[:, :, half:2 * half]
                cb = cos2.unsqueeze(1).broadcast_to([128, nh, half])
                sbb = sin2.unsqueeze(1).broadcast_to([128, nh, half])
                rv = [rr[i][:, 0:nh * half].rearrange("p (a b) -> p a b", a=nh) for i in range(4)]
                tt(rv[0], x1, cb, ALU.mult, [src_lt, lt("rope")], [lt("rr0")])
                tt(rv[1], x2, sbb, ALU.mult, [src_lt, lt("rope")], [lt("rr1")])
                tt(rv[2], x2, cb, ALU.mult, [src_lt, lt("rope")], [lt("rr2")])
                tt(rv[3], x1, sbb, ALU.mult, [src_lt, lt("rope")], [lt("rr3")])
                tt(dst3[:, :, 0:half], rv[0], rv[1], ALU.subtract, [lt("rr0"), lt("rr1")], [dst_lt])
                tt(dst3[:, :, half:2 * half], rv[2], rv[3], ALU.add, [lt("rr2"), lt("rr3")], [dst_lt])

            prr = [0]
            def proj_chunk(col0, ncols, epilogue, wname="w_in", kcs=KC, hsrc=None, hsrc_lt=None):
                W = w_bf[wname]
                s, wl = load_w(ft, [(lambda w: w[:, 0:kcs * ncols].rearrange("p (a b) -> p a b", a=kcs),
                                     W[:, col0:col0 + ncols].rearrange("(a p) c -> p a c", p=128))], [lt("wbf_" + wname)])
                wv = ft.wsl[s][:, 0:kcs * ncols].rearrange("p (a b) -> p a b", a=kcs)
                pending = None
                for t in range(NT):
                    bi = prr[0] % 6
                    prr[0] += 1
                    for kc in range(kcs):
                        mm(pbank[bi][:, 0:ncols], ft.hT[:, kc, t * 128:(t + 1) * 128], wv[:, kc, :], kc == 0, kc == kcs - 1,
                           [wl, lt("hT")], [pl(bi)])
                    if pending is not None:
                        pending()
                    pending = epilogue(t, pbank[bi], pl(bi))
                if pending is not None:
                    pending()

            def qk_epilogue(gvec, gname, stT, st_lt, c, blk0):
                def ep(t, pb, pbl):
                    psv = pb[:].rearrange("p (a b) -> p a b", a=4)
                    act(tmpA[:, 0:512], pb[:], AF.Square, [pbl], [lt("tmpA")])
                    dve(lambda e: e.tensor_reduce(out=st2[:, 0:4], in_=tmpA[:, 0:512].rearrange("p (a b) -> p a b", a=4),
                                                  axis=AX.X, op=ALU.add), [lt("tmpA")], [lt("st2")])
                    act(st2[:, 4:8], st2[:, 0:4], AF.Sqrt, [lt("st2")], [lt("st2b")], scale=1.0 / 128, bias=EPS)
                    dve(lambda e: e.reciprocal(st2[:, 8:12], st2[:, 4:8]), [lt("st2b")], [lt("st2c")])
                    tb = tmpB[:, 0:512].rearrange("p (a b) -> p a b", a=4)
                    tt(tb, psv, st2[:, 8:12].unsqueeze(2).broadcast_to([128, 4, 128]), ALU.mult, [pbl, lt("st2c")], [lt("tmpB")])
                    tt(tb, tb, gvec[:].unsqueeze(1).broadcast_to([128, 4, 128]), ALU.mult, [lt("tmpB"), lt("c_" + gname)], [lt("tmpB")])
                    qb_, qbn = qbs[t % 2]
                    qbv = qb_.rearrange("p (a b) -> p a b", a=4)
                    act(qb_, tmpB[:, 0:512], AF.Copy, [lt("tmpB")], [lt(qbn)])
                    rope_apply(tb, qbv, 4, 16, cosA[:, blk0 + t, :], sinA[:, blk0 + t, :], lt("tmpB"), lt(qbn))
                    def deferred(t=t, qb_=qb_, qbn=qbn):
                        transposes(ft, [qb_[:, j * 128:(j + 1) * 128] for j in range(4)], [lt(qbn)],
                                   lambda i0, m, t=t: stT[:, 4 * c + i0:4 * c + i0 + m, t * 128:(t + 1) * 128], st_lt)
                    return deferred
                return ep

            def gelu_to(dst, pb, pbl, dst_lt, ncols=512):
                act(tmpB[:, 0:ncols], pb[:, 0:ncols], AF.Square, [pbl], [lt("tmpB")])
                ts(tmpB[:, 0:ncols], tmpB[:, 0:ncols], 0.044715, 1.0, ALU.mult, ALU.add, [lt("tmpB")], [lt("tmpB")])
                tt(tmpB[:, 0:ncols], tmpB[:, 0:ncols], pb[:, 0:ncols], ALU.mult, [lt("tmpB"), pbl], [lt("tmpB")])
                act(tmpB[:, 0:ncols], tmpB[:, 0:ncols], AF.Sigmoid, [lt("tmpB")], [lt("tmpB")], scale=1.5957691216057308)
                tt(dst, tmpB[:, 0:ncols], pb[:, 0:ncols], ALU.mult, [lt("tmpB"), pbl], [dst_lt])

            ngroups = NG_OWN if stage <= 1 else NG_ALL
            for g in range(ngroups):
                own = g < NG_OWN
                blk0 = g * NT
                if g == 1:
                    cast_weights(LATE_CASTS)
                dma("sp", d_x, ft.xg[:], x_in[g * T:(g + 1) * T, :].rearrange("(t p) d -> p t d", p=128), [], [lt("xg")])
                rmsnorm_T(ft, "norm_ffn1")
                ffn(ft, w_bf["ffn1_w_gate"], w_bf["ffn1_w_up"], w_bf["ffn1_w_down"], [], fine="ffn1")
                if stage <= 1:
                    dma("pool", d_out, out_d[g * T:(g + 1) * T, :].rearrange("(t p) d -> p t d", p=128), ft.xg[:],
                        [lt("xg")], [lt("out")])
                    continue
                if own:
                    dma("pool", d_st["xmid"], xmid_s[g * T:(g + 1) * T, :].rearrange("(t p) d -> p t d", p=128), ft.xg[:],
                        [lt("xg")], [lt("xmid_s")])
                rmsnorm_T(ft, "norm_mix")
                def kiwi_ep(t, pb, pbl):
                    act(tmpA[:, 0:64], pb[:, 0:64], AF.Square, [pbl], [lt("tmpA"), lt("st2")], accum_out=st2[:, 0:1])
                    act(st2[:, 4:5], st2[:, 0:1], AF.Sqrt, [lt("st2")], [lt("st2b")], scale=1.0 / 64, bias=EPS)
                    dve(lambda e: e.reciprocal(st2[:, 8:9], st2[:, 4:5]), [lt("st2b")], [lt("st2c")])
                    stt(tmpB[:, 0:64], pb[:, 0:64], st2[:, 8:9], gki[:], ALU.mult, ALU.mult, [pbl, lt("st2c"), lt("c_idx_k_norm")], [lt("tmpB")])
                    dve(lambda e: e.tensor_copy(out=kib[:], in_=tmpB[:, 0:64]), [lt("tmpB")], [lt("kib")])
                    rope_apply(tmpB[:, 0:64].rearrange("p (a b) -> p a b", a=1), kib[:].rearrange("p (a b) -> p a b", a=1),
                               1, 8, cosI[:, blk0 + t, :], sinI[:, blk0 + t, :], lt("tmpB"), lt("kib"))
                    transposes(ft, [kib[:, 0:64]], [lt("kib")],
                               lambda i0, m, t=t: kiT_st[:, t * 128:(t + 1) * 128].rearrange("p (a b) -> p a b", a=1), lt("kiT_st"))
                    if own:
                        sc = (16 ** -0.5) * (64 ** -0.5)
                        act(wabs[:, t, :], pb[:, 64:80], AF.Abs, [pbl], [lt("wabs")], scale=sc)
                        ts(sgn_st[:, t, :], pb[:, 64:80], 0.0, 2.0, ALU.is_ge, ALU.mult, [pbl], [lt("sgn_st")])
                        ts(sgn_st[:, t, :], sgn_st[:, t, :], -1.0, None, ALU.add, None, [lt("sgn_st")], [lt("sgn_st")])
                proj_chunk(C_KI, 80, kiwi_ep)
                dma("pool", d_st["kiT"], kiT_s[:, g * T:(g + 1) * T], kiT_st[:], [lt("kiT_st")], [lt("kiT_s")])
                if own:
                    dma("pool", d_st["sgn"], sgn_s[g * T:(g + 1) * T, :].rearrange("(t p) c -> p t c", p=128), sgn_st[:],
                        [lt("sgn_st")], [lt("sgn_s")])
                if own:
                    for c in range(2):
                        proj_chunk(C_Q + c * 512, 512, qk_epilogue(gq, "q_norm", qT_st, lt("qT_st"), c, blk0))
                    dma("pool", d_st["qT"], qT_s.rearrange("(h d) n -> d h n", d=128)[:, :, g * T:(g + 1) * T], qT_st,
                        [lt("qT_st")], [lt("qT_s")])
                for c in range(2):
                    proj_chunk(C_K + c * 512, 512, qk_epilogue(gk, "k_norm", kT_st, lt("kT_st"), c, blk0))
                dma("pool", d_st["kT"], kT_s.rearrange("(h d) n -> d h n", d=128)[:, :, g * T:(g + 1) * T], kT_st,
                    [lt("kT_st")], [lt("kT_s")])
                for t4 in range(NT):
                    dve(lambda e, t4=t4: e.memset(v_st[:, t4, :, 128:129], 1.0), [], [lt("v_st")])
                for c in range(2):
                    def v_ep(t, pb, pbl, c=c):
                        act(v_st[:, t, 4 * c:4 * c + 4, 0:128], pb[:].rearrange("p (a b) -> p a b", a=4), AF.Copy, [pbl], [lt("v_st")])
                    proj_chunk(C_V + c * 512, 512, v_ep)
                for h8 in range(8):
                    dma("pool", d_st["v"], v_s[h8, :, blk0:blk0 + NT, :], v_st[:, :, h8, :],
                        [lt("v_st")], [lt("v_s")])
                if not own:
                    continue
                for c in range(2):
                    def qi_ep(t, pb, pbl, c=c):
                        act(tmpA[:, 0:512], pb[:], AF.Copy, [pbl], [lt("tmpA")])
                        ta = tmpA[:, 0:512].rearrange("p (a b) -> p a b", a=8)
                        tbv = tmpB[:, 0:512].rearrange("p (a b) -> p a b", a=8)
                        dve(lambda e: e.tensor_copy(out=tmpB[:, 0:512], in_=tmpA[:, 0:512]), [lt("tmpA")], [lt("tmpB")])
                        rope_apply(ta, tbv, 8, 8, cosI[:, blk0 + t, :], sinI[:, blk0 + t, :], lt("tmpA"), lt("tmpB"))
                        qb_, qbn = qbs[t % 2]
                        qbv = qb_.rearrange("p (a b) -> p a b", a=8)
                        tt(qbv, tbv, wabs[:, t, 8 * c:8 * c + 8].unsqueeze(2).broadcast_to([128, 8, 64]), ALU.mult,
                           [lt("tmpB"), lt("wabs")], [lt(qbn)])
                        def deferred(t=t, qb_=qb_, qbn=qbn, c=c):
                            transposes(ft, [qb_[:, j * 128:(j + 1) * 128] for j in range(4)], [lt(qbn)],
                                       lambda i0, m, t=t: qiT_st[:, 4 * c + i0:4 * c + i0 + m, t * 128:(t + 1) * 128], lt("qiT_st"))
                        return deferred
                    proj_chunk(C_QI + c * 512, 512, qi_ep)
                dma("pool", d_st["qiT"], qiT_s.rearrange("(h d) n -> d h n", d=128)[:, :, g * T:(g + 1) * T], qiT_st,
                    [lt("qiT_st")], [lt("qiT_s")])
                vsacc = {}
                for c in range(2):
                    def vs_ep(t, pb, pbl, c=c):
                        gelu_to(tmpA[:, c * 512:(c + 1) * 512], pb, pbl, lt("tmpA"))
                        if c == 1:
                            act(tmpC[:, 0:1024], tmpA[:, 0:1024], AF.Square, [lt("tmpA")], [lt("tmpC"), lt("st2")], accum_out=st2[:, 0:1])
                            act(st2[:, 4:5], st2[:, 0:1], AF.Sqrt, [lt("st2")], [lt("st2b")], scale=1.0 / 1024, bias=EPS)
                            dve(lambda e: e.reciprocal(st2[:, 8:9], st2[:, 4:5]), [lt("st2b")], [lt("st2c")])
                            stt(vsn[:, t, :], tmpA[:, 0:1024], st2[:, 8:9], gv[:], ALU.mult, ALU.mult,
                                [lt("tmpA"), lt("st2c"), lt("c_sgu_v_norm")], [lt("vsn")])
                    vsacc[c] = vs_ep
                Wn = w_bf["w_in"]
                sl = []
                for c in range(2):
                    s_, wl_ = load_w(ft, [(lambda w: w[:, 0:KC * 512].rearrange("p (a b) -> p a b", a=KC),
                                           Wn[:, C_VS + c * 512:C_VS + (c + 1) * 512].rearrange("(a p) c -> p a c", p=128))], [lt("wbf_w_in")])
                    sl.append((s_, wl_))
                for t in range(NT):
                    for c in range(2):
                        s_, wl_ = sl[c]
                        wv = ft.wsl[s_][:, 0:KC * 512].rearrange("p (a b) -> p a b", a=KC)
                        bi = prr[0] % 6; prr[0] += 1
                        for kc in range(KC):
                            mm(pbank[bi][:], ft.hT[:, kc, t * 128:(t + 1) * 128], wv[:, kc, :], kc == 0, kc == KC - 1, [wl_, lt("hT")], [pl(bi)])
                        vsacc[c](t, pbank[bi], pl(bi))
                sl = []
                for c in range(2):
                    s_, wl_ = load_w(ft, [(lambda w: w[:, 0:KC * 512].rearrange("p (a b) -> p a b", a=KC),
                                           Wn[:, C_U + c * 512:C_U + (c + 1) * 512].rearrange("(a p) c -> p a c", p=128))], [lt("wbf_w_in")])
                    sl.append((s_, wl_))
                for t in range(NT):
                    for c in range(2):
                        s_, wl_ = sl[c]
                        wv = ft.wsl[s_][:, 0:KC * 512].rearrange("p (a b) -> p a b", a=KC)
                        bi = prr[0] % 6; prr[0] += 1
                        for kc in range(KC):
                            mm(pbank[bi][:], ft.hT[:, kc, t * 128:(t + 1) * 128], wv[:, kc, :], kc == 0, kc == KC - 1, [wl_, lt("hT")], [pl(bi)])
                        gelu_to(tmpA[:, c * 512:(c + 1) * 512], pbank[bi], pl(bi), lt("tmpA"))
                    for hb in range(2):
                        bi = prr[0] % 6; prr[0] += 1
                        for g4 in range(4):
                            g8 = hb * 4 + g4
                            mm(pbank[bi][:, g4 * 128:(g4 + 1) * 128], WsT[:, g8, :], vsn[:, t, g8 * 128:(g8 + 1) * 128], True, True,
                               [lt("WsT"), lt("vsn")], [pl(bi)], sig=(g4 == 3))
                        for g4 in range(4):
                            g8 = hb * 4 + g4
                            stt(yb[:, g8 * 128:(g8 + 1) * 128], pbank[bi][:, g4 * 128:(g4 + 1) * 128], bsT[:, g8:g8 + 1],
                                tmpA[:, g8 * 128:(g8 + 1) * 128], ALU.add, ALU.mult, [pl(bi), lt("bsT"), lt("tmpA")], [lt("yb")])
                    transposes(ft, [yb[:, j * 128:(j + 1) * 128] for j in range(8)], [lt("yb")],
                               lambda i0, m, t=t: ybT[:, i0:i0 + m, t * 128:(t + 1) * 128], lt("ybT"))
                def fm_chunk(col0, wname, kcs, rhs_fn, rhs_lt, consume):
                    W = w_bf[wname]
                    s_, wl_ = load_w(ft, [(lambda w: w[:, 0:kcs * 512].rearrange("p (a b) -> p a b", a=kcs),
                                           W[:, col0:col0 + 512].rearrange("(a p) c -> p a c", p=128))], [lt("wbf_" + wname)])
                    wv = ft.wsl[s_][:, 0:kcs * 512].rearrange("p (a b) -> p a b", a=kcs)
                    for fl in range(4):
                        bi = prr[0] % 6; prr[0] += 1
                        for kc in range(kcs):
                            mm(pbank[bi][:], wv[:, kc, fl * 128:(fl + 1) * 128], rhs_fn(kc), kc == 0, kc == kcs - 1, [wl_, rhs_lt], [pl(bi)])
                        consume(fl, pbank[bi], pl(bi))
                for c in range(4):
                    def ga_c(fl, pb, pbl):
                        act(ost[:, fl, :], pb[:], AF.Sigmoid, [pbl], [lt("ost")])
                    fm_chunk(C_GA + c * 512, "w_in", KC, lambda kc: ft.hT[:, kc, :], lt("hT"), ga_c)
                    dma("pool", d_st["ost"], sgaT_s[c * 512:(c + 1) * 512, g * T:(g + 1) * T].rearrange("(f p) n -> p f n", p=128),
                        ost, [lt("ost")], [lt("sgaT_s")])
                for c in range(4):
                    W = w_bf["w_in"]
                    s1, wl1 = load_w(ft, [(lambda w: w[:, 0:KC * 512].rearrange("p (a b) -> p a b", a=KC),
                                           W[:, C_GB + c * 512:C_GB + (c + 1) * 512].rearrange("(a p) c -> p a c", p=128))], [lt("wbf_w_in")])
                    W2 = w_bf["w_up_sgu"]
                    s2, wl2 = load_w(ft, [(lambda w: w[:, 0:8 * 512].rearrange("p (a b) -> p a b", a=8),
                                           W2[:, c * 512:(c + 1) * 512].rearrange("(a p) c -> p a c", p=128))], [lt("wbf_w_up_sgu")])
                    wv1 = ft.wsl[s1][:, 0:KC * 512].rearrange("p (a b) -> p a b", a=KC)
                    wv2 = ft.wsl[s2][:, 0:8 * 512].rearrange("p (a b) -> p a b", a=8)
                    for fl in range(4):
                        b1 = prr[0] % 6; prr[0] += 1
                        for kc in range(KC):
                            mm(pbank[b1][:], wv1[:, kc, fl * 128:(fl + 1) * 128], ft.hT[:, kc, :], kc == 0, kc == KC - 1, [wl1, lt("hT")], [pl(b1)])
                        act(sgb[:], pbank[b1][:], AF.Sigmoid, [pl(b1)], [lt("sgb")])
                        b2 = prr[0] % 6; prr[0] += 1
                        for kc in range(8):
                            mm(pbank[b2][:], wv2[:, kc, fl * 128:(fl + 1) * 128], ybT[:, kc, :], kc == 0, kc == 7, [wl2, lt("ybT")], [pl(b2)])
                        tt(ost[:, fl, :], pbank[b2][:], sgb[:], ALU.mult, [pl(b2), lt("sgb")], [lt("ost")])
                    dma("pool", d_st["ost"], mbT_s[c * 512:(c + 1) * 512, g * T:(g + 1) * T].rearrange("(f p) n -> p f n", p=128),
                        ost, [lt("ost")], [lt("mbT_s")])
            k.barrier()
        if stage <= 1:
            with nc.Block() as block:
                k.replay(block)
            return nc
        p2 = ExitStack()
        with p2:
            def sb(name, shape, dtype):
                return p2.enter_context(nc.sbuf_tensor(name, shape, dtype))
            identf2 = sb("identf2", [128, 128], F32)
            ident2 = sb("ident2", [128, 128], BF16)
            kiT2 = sb("kiT2", [128, S], BF16)
            kc2 = sb("kc2", [128, 2, 128], BF16)
            qch = sb("qch_sb", [128, NOWN], F32)
            score2 = [sb("score%d" % i_, [128, S], F32) for i_ in range(2)]
            mask01 = sb("mask01", [128, S], BF16)
            maskT = sb("maskT", [128, NBLK, 128], BF16)
            qz = sb("qz", [128, 16, 128], BF16)
            sgq = sb("sgq", [128, 16], F32)
            qTq = sb("qTq", [128, 8, 128], BF16)
            kTh = [sb("kTh%d" % i, [128, S], BF16) for i in range(2)]
            Vh = [sb("Vh%d" % i, [128, NBLK, 129], BF16) for i in range(2)]
            pT = [sb("pT%d" % i, [128, 512], BF16) for i in range(3)]
            pmT = [sb("pmT%d" % i, [128, 512], BF16) for i in range(3)]
            ya = sb("ya", [128, 1024], BF16)
            yaT_st = sb("yaT_st", [128, 8, 128], BF16)
            bst = sb("bst", [128, 16], F32)
            thr_all = sb("thr_all", [128, NOWN], F32)
            cnt_all = sb("cnt_all", [128, NOWN], F32)
            d2 = {n: dsem("p2_" + n) for n in ("c", "qiq", "sgq", "qTq", "kT0", "kT1", "V0", "V1", "yaT", "kc")}
            p1_out = [lt(n) for n in ("kiT_s", "kT_s", "v_s", "qT_s", "qiT_s", "sgn_s")]
            cl = []
            dma("sp", d2["c"], identf2[:], cst_in[:, 0:128], [], [lt("identf2")]); cl.append("identf2")
            dma("sp", d2["c"], kiT2[0:64, :], kiT_s[:, :], p1_out, [lt("kiT2")]); cl.append("kiT2")
            dma("sp", d2["c"], kiT2[64:128, :], kiT_s[:, :], p1_out, [lt("kiT2")])
            dma("sp", d2["c"], qch[:], qch_in[:], [], [lt("qch")]); cl.append("qch")
            for nm in cl:
                lt(nm).writers[d2["c"]] = k.cnt[d2["c"]]
            dve(lambda e: e.tensor_copy(out=ident2[:], in_=identf2[:]), [lt("identf2")], [lt("ident2")])
            dve(lambda e: e.memset(qz[:], 0.0), [], [lt("qiq")])
            R0 = 16.0
            NIT = 28
            SCALE = 128.0 ** -0.5
            hrr = [0]
            irr = [0]
            osb = sb("osb", [128, 8, 132], F32)
            NB2 = NOWN if stage >= 3 else 0

            def geom(i):
                n1 = (i + 1) * 128
                return n1, 2 * n1, 2 * (i + 1)

            Dg = sb("Dg", [128, 16, 128], BF16)
            rtb = [sb("rtb%d" % i_, [128, 512], BF16) for i_ in range(4)]
            negbb = [sb("negbb%d" % i_, [128, 128], BF16) for i_ in range(2)]
            crr = [0]
            LAG = 2

            def idx_stage(i):
                n1, nk, nkb = geom(i)
                score = score2[i % 2]
                sl_ = lt("score%d" % (i % 2))
                segs = [(0, 0, n1), (n1, NOWN * 128, n1)]
                qsrc = qiT_s.rearrange("(hp e d) n -> d e hp n", e=2, d=64)
                qdst = qz[:].rearrange("p (hp e) n -> p e hp n", e=2)
                for e_ in range(2):
                    dma("sp", d2["qiq"], qdst[e_ * 64:(e_ + 1) * 64, e_, :, :], qsrc[:, e_, :, i * 128:(i + 1) * 128], p1_out, [lt("qiq")])
                dma("sp", d2["sgq"], sgq[:], sgn_s[i * 128:(i + 1) * 128, :], p1_out, [lt("sgq")])
                for si_ in range(2):
                    dma("sp", d2["kc"], kc2[:, si_, :], kch_in[:, si_ * NOWN * 128 + i * 128:si_ * NOWN * 128 + (i + 1) * 128].partition_broadcast(128),
                        [], [lt("kc2")])
                k.op("pool", lambda e: e.tensor_tensor(
                    out=Dg[:], in0=ident2[:].unsqueeze(1).broadcast_to([128, 16, 128]),
                    in1=sgq[:].unsqueeze(2).broadcast_to([128, 16, 128]), op=ALU.mult),
                    [lt("ident2"), lt("sgq")], [lt("Dg")])
                for si_, (l0, c0, n) in enumerate(segs):
                    off = 0
                    while off < n:
                        w = min(512, n - off)
                        cc = crr[0] % 2; crr[0] += 1
                        need_bias = (off + w == n)
                        nb_ = negbb[cc]
                        if need_bias:
                            kv = kc2[:, si_, :]
                            k.op("pool", lambda e, nb_=nb_, kv=kv, i=i: e.tensor_scalar(
                                out=nb_[:, 0:128], in0=kv, scalar1=qch[:, i:i + 1], scalar2=NEG, op0=ALU.is_gt, op1=ALU.mult),
                                [lt("kc2"), lt("qch")], [lt("negbb%d" % cc)])
                        sbk = 3 + cc
                        for hh in range(16 + LAG):
                            if hh < 16:
                                h = hh
                                bi = irr[0] % 3; irr[0] += 1
                                mm(pbank[bi][:, 0:w], qz[:, h, :], kiT2[:, c0 + off:c0 + off + w], True, True,
                                   [lt("qiq"), lt("kiT2")], [pl(bi)])
                                act(rtb[h % 4][:, 0:w], pbank[bi][:, 0:w], AF.Relu, [pl(bi)], [lt("rtb%d" % (h % 4))])
                            h2 = hh - LAG
                            if h2 >= 0:
                                last = (h2 == 15) and not need_bias
                                mm(pbank[sbk][:, 0:w], Dg[:, h2, :], rtb[h2 % 4][:, 0:w], h2 == 0, last,
                                   [lt("Dg"), lt("rtb%d" % (h2 % 4))], [pl(sbk)], sig=True)
                        if need_bias:
                            mm(pbank[sbk][:, w - 128:w], ident2[:], nb_[:, 0:128], False, True, [lt("ident2"), lt("negbb%d" % cc)], [pl(sbk)], sig=True)
                        act(score[:, l0 + off:l0 + off + w], pbank[sbk][:, 0:w], AF.Copy, [pl(sbk)], [sl_])
                        off += w

            def bis_stage(i):
                n1, nk, nkb = geom(i)
                score = score2[i % 2]
                sl_ = lt("score%d" % (i % 2))
                dve(lambda e: e.memset(bst[:, 0:1], 0.0), [], [lt("bst_t")])
                for it in range(NIT):
                    step = R0 / (2 ** it)
                    ts(mask01[:, 0:nk], score[:, 0:nk], bst[:, 0:1], None, ALU.is_ge, ALU.add, [sl_, lt("bst_t")],
                       [lt("mask01"), lt("bst_c")], accum_out=bst[:, 1:2])
                    ts(bst[:, 2:3], bst[:, 1:2], TOPK - 0.5, step, ALU.is_ge, ALU.mult, [lt("bst_c")], [lt("bst_m")])
                    dec = -step if it == NIT - 1 else -step / 2
                    stt(bst[:, 0:1], bst[:, 2:3], dec, bst[:, 0:1], ALU.add, ALU.add, [lt("bst_m"), lt("bst_t")], [lt("bst_t")])
                ts(mask01[:, 0:nk], score[:, 0:nk], bst[:, 0:1], None, ALU.is_ge, ALU.add, [sl_, lt("bst_t")],
                   [lt("mask01"), lt("bst_c")], accum_out=bst[:, 1:2])
                if debug:
                    dve(lambda e, i=i: e.tensor_copy(out=thr_all[:, i:i + 1], in_=bst[:, 0:1]), [lt("bst_t")], [lt("thr_all")])
                    dve(lambda e, i=i: e.tensor_copy(out=cnt_all[:, i:i + 1], in_=bst[:, 1:2]), [lt("bst_c")], [lt("cnt_all")])

            def mT_stage(i):
                n1, nk, nkb = geom(i)
                kb = 0
                while kb < nkb:
                    m = min(4, nkb - kb)
                    pbv = pbank[4][:].bitcast(BF16)
                    for j in range(m):
                        k.op("pe", lambda e, pbv=pbv, j=j, kb=kb: e.transpose(
                            pbv[:, j * 128:(j + 1) * 128], mask01[:, (kb + j) * 128:(kb + j + 1) * 128], ident2[:]),
                            [lt("mask01"), lt("ident2")], [pl(4)], sig=(j == m - 1))
                    act(maskT[:, kb:kb + m, :], pbv[:, 0:m * 128].rearrange("p (a b) -> p a b", a=m), AF.Copy, [pl(4)], [lt("maskT")])
                    kb += m

            def att_stage(i):
                n1, nk, nkb = geom(i)
                dma("sp", d2["qTq"], qTq[:], qT_s.rearrange("(h d) n -> d h n", d=128)[:, :, i * 128:(i + 1) * 128], p1_out, [lt("qTq")])
                for h in range(8):
                    s_ = hrr[0] % 2; hrr[0] += 1
                    kl, vl = lt("kTh%d" % s_), lt("Vh%d" % s_)
                    dma("sp", d2["kT%d" % s_], kTh[s_][:, 0:n1], kT_s[h * 128:(h + 1) * 128, 0:n1], p1_out, [kl])
                    dma("sp", d2["kT%d" % s_], kTh[s_][:, n1:nk], kT_s[h * 128:(h + 1) * 128, NOWN * 128:NOWN * 128 + n1], p1_out, [kl])
                    dma("sp", d2["V%d" % s_], Vh[s_][:, 0:i + 1, :], v_s[h, :, 0:i + 1, :], p1_out, [vl])
                    dma("sp", d2["V%d" % s_], Vh[s_][:, i + 1:nkb, :], v_s[h, :, NOWN:NOWN + i + 1, :], p1_out, [vl])
                    G = []
                    kb = 0
                    while kb < nkb:
                        m = min(4, nkb - kb)
                        G.append((kb, m))
                        kb += m
                    SK = 2
                    QB = [5, 6, 0]
                    for gi in range(len(G) + SK):
                        if gi < len(G):
                            kb, m = G[gi]
                            b3 = gi % 3
                            lb = QB[b3]
                            for j in range(m):
                                mm(pbank[lb][:, j * 128:(j + 1) * 128], kTh[s_][:, (kb + j) * 128:(kb + j + 1) * 128], qTq[:, h, :], True, True,
                                   [kl, lt("qTq")], [pl(lb)], sig=(j == m - 1))
                            p_ = pT[b3]; pm_ = pmT[b3]
                            act(p_[:, 0:m * 128], pbank[lb][:, 0:m * 128], AF.Exp, [pl(lb)], [lt("pT%d" % b3)], scale=SCALE)
                            mview = maskT[:, kb:kb + m, :].rearrange("p a b -> p (a b)")
                            k.op("pool", lambda e, pm_=pm_, p_=p_, mview=mview, m=m: e.tensor_tensor(
                                out=pm_[:, 0:m * 128], in0=p_[:, 0:m * 128], in1=mview, op=ALU.mult),
                                [lt("pT%d" % b3), lt("maskT")], [lt("pmT%d" % b3)])
                        g2 = gi - SK
                        if g2 >= 0:
                            kb, m = G[g2]
                            b3 = g2 % 3
                            pm_ = pmT[b3]
                            for j in range(m):
                                mm(pbank[7][:, 0:129], pm_[:, j * 128:(j + 1) * 128], Vh[s_][:, kb + j, :], kb + j == 0, kb + j == nkb - 1,
                                   [lt("pmT%d" % b3), vl], [pl(7)], sig=(j == m - 1))
                    act(osb[:, h, 0:129], pbank[7][:, 0:129], AF.Copy, [pl(7)], [lt("osb")])

            def fin_stage(i):
                dve(lambda e: e.reciprocal(bst[:, 8:16], osb[:, :, 128]), [lt("osb")], [lt("bst_r")])
                tt(ya[:].rearrange("p (a b) -> p a b", a=8), osb[:, :, 0:128], bst[:, 8:16].unsqueeze(2).broadcast_to([128, 8, 128]),
                   ALU.mult, [lt("osb"), lt("bst_r")], [lt("ya")])
                i0 = 0
                while i0 < 8:
                    pbv = pbank[4][:].bitcast(BF16)
                    for j in range(4):
                        k.op("pe", lambda e, pbv=pbv, j=j, i0=i0: e.transpose(
                            pbv[:, j * 128:(j + 1) * 128], ya[:, (i0 + j) * 128:(i0 + j + 1) * 128], ident2[:]),
                            [lt("ya"), lt("ident2")], [pl(4)], sig=(j == 3))
                    act(yaT_st[:, i0:i0 + 4, :], pbv[:, 0:512].rearrange("p (a b) -> p a b", a=4), AF.Copy, [pl(4)], [lt("yaT_st")])
                    i0 += 4
                dma("pool", d2["yaT"], yaT_s.rearrange("(h d) n -> d h n", d=128)[:, :, i * 128:(i + 1) * 128], yaT_st[:],
                    [lt("yaT_st")], [lt("yaT_s")])

            if NB2:
                idx_stage(0); idx_stage(1); bis_stage(0); mT_stage(0)
            for i in range(NB2):
                if i + 2 < NB2:
                    idx_stage(i + 2)
                att_stage(i)
                if i + 1 < NB2:
                    bis_stage(i + 1)
                fin_stage(i)
                if i + 1 < NB2:
                    mT_stage(i + 1)
            if debug:
                dma("pool", d2["yaT"], dbg["thr"][:], thr_all[:], [lt("thr_all")], [lt("dbg_thr")])
                dma("pool", d2["yaT"], dbg["cnt"][:], cnt_all[:], [lt("cnt_all")], [lt("dbg_cnt")])
            k.barrier()

        p3 = ExitStack()
        with p3:
            ft = alloc_ffn(p3, ["norm_ffn2"], "_p3")
            ft.trr = 0
            finish_consts(ft)
            aflat = ft.aT[:].rearrange("p a b -> p (a b)")
            lt("aT")
            off = [0]
            def take(name, shape, dtype):
                n = int(np.prod(shape[1:])) * (4 if dtype == F32 else 2)
                v = aflat[:, off[0] // 2:(off[0] + n) // 2]
                if dtype == F32:
                    v = v.bitcast(F32)
                off[0] += n
                lt(name, "aT")
                if len(shape) == 3:
                    v = v.rearrange("p (a b) -> p a b", a=shape[1])
                return v
            yaT = take("yaT", [128, 8, T], BF16)
            sga_c = take("sga_c", [128, 4, 512], F32)
            mb_c = take("mb_c", [128, 4, 512], F32)
            tmpm = take("tmpm", [128, 512], F32)
            d3 = {n: dsem("p3_" + n) for n in ("ya", "sga", "mb")}
            for g in range(NG_OWN):
                dma("sp", d_x, ft.xg[:], xmid_s[g * T:(g + 1) * T, :].rearrange("(t p) d -> p t d", p=128), [lt("xmid_s")], [lt("xg")])
                if stage >= 3:
                    dma("sp", d3["ya"], yaT, yaT_s.rearrange("(h d) n -> d h n", d=128)[:, :, g * T:(g + 1) * T], [lt("yaT_s")], [lt("yaT")])
                mT = ft.hT
                for c in range(4):
                    dma("sp", d3["sga"], sga_c, sgaT_s[c * 512:(c + 1) * 512, g * T:(g + 1) * T].rearrange("(f p) n -> p f n", p=128),
                        [lt("sgaT_s")], [lt("sga_c")])
                    dma("sp", d3["mb"], mb_c, mbT_s[c * 512:(c + 1) * 512, g * T:(g + 1) * T].rearrange("(f p) n -> p f n", p=128),
                        [lt("mbT_s")], [lt("mb_c")])
                    if stage >= 3:
                        W = w_bf["w_up_attn"]
                        s_, wl_ = load_w(ft, [(lambda w: w[:, 0:8 * 512].rearrange("p (a b) -> p a b", a=8),
                                               W[:, c * 512:(c + 1) * 512].rearrange("(a p) c -> p a c", p=128))], [lt("wbf_w_up_attn")])
                        wv = ft.wsl[s_][:, 0:8 * 512].rearrange("p (a b) -> p a b", a=8)
                    for fl in range(4):
                        if stage >= 3:
                            bi = (c * 4 + fl) % 6
                            for kc in range(8):
                                mm(pbank[bi][:], wv[:, kc, fl * 128:(fl + 1) * 128], yaT[:, kc, :], kc == 0, kc == 7, [wl_, lt("yaT")], [pl(bi)])
                            tt(tmpm[:], pbank[bi][:], sga_c[:, fl, :], ALU.mult, [pl(bi), lt("sga_c")], [lt("tmpm")])
                            tt(mT[:, 4 * c + fl, :], tmpm[:], mb_c[:, fl, :], ALU.add, [lt("tmpm"), lt("mb_c")], [lt("hT")])
                        else:
                            dve(lambda e, c=c, fl=fl: e.tensor_copy(out=mT[:, 4 * c + fl, :], in_=mb_c[:, fl, :]), [lt("mb_c"), lt("sga_c")], [lt("hT")])
                for c in range(4):
                    W = w_bf["w_out"]
                    s_, wl_ = load_w(ft, [(lambda w: w[:, 0:KC * 512].rearrange("p (a b) -> p a b", a=KC),
                                           W[:, c * 512:(c + 1) * 512].rearrange("(a p) c -> p a c", p=128))], [lt("wbf_w_out")])
                    wv = ft.wsl[s_][:, 0:KC * 512].rearrange("p (a b) -> p a b", a=KC)
                    for t in range(NT):
                        bi = (c * 4 + t) % 6
                        for kc in range(KC):
                            mm(pbank[bi][:], mT[:, kc, t * 128:(t + 1) * 128], wv[:, kc, :], kc == 0, kc == KC - 1, [wl_, lt("hT")], [pl(bi)])
                        xs = ft.xg[:, t, c * 512:(c + 1) * 512]
                        tt(xs, pbank[bi][:], xs, ALU.add, [pl(bi), lt("xg")], [lt("xg")])
                rmsnorm_T(ft, "norm_ffn2")
                ffn(ft, w_bf["ffn2_w_gate"], w_bf["ffn2_w_up"], w_bf["ffn2_w_down"],
                    [lt("wbf_ffn2_w_gate"), lt("wbf_ffn2_w_up"), lt("wbf_ffn2_w_down")])
                dma("pool", d_out, out_d[g * T:(g + 1) * T, :].rearrange("(t p) d -> p t d", p=128), ft.xg[:], [lt("xg")], [lt("out")])
            k.barrier()
        if debug:
            d_dbg = dsem("dbg")
            for o, src in dbg_copies:
                dma("pool", d_dbg, o, src, [], [lt("dbgout")])
            k.barrier()
        with nc.Block() as block:
            k.replay(block)
    return nc


_CACHE = {}


def _consts():
    c = np.zeros((128, 152), np.float32)
    c[:, 0:128] = np.eye(128, dtype=np.float32)
    f_a = (np.float32(ROPE_THETA) ** (-(np.arange(0, 32, 2, dtype=np.float32)) / np.float32(32))).astype(np.float32)
    f_i = (np.float32(ROPE_THETA) ** (-(np.arange(0, 16, 2, dtype=np.float32)) / np.float32(16))).astype(np.float32)
    c[:, 128:144] = f_a[None, :]
    c[:, 144:152] = f_i[None, :]
    return c


def kernel(**inputs):
    stage = inputs.pop("_stage", 99)
    debug = inputs.pop("_debug", False)
    x = np.asarray(inputs["x"])
    positions = np.asarray(inputs["positions"])
    if (stage, debug) not in _CACHE:
        _CACHE[(stage, debug)] = build_program(stage, debug)
    nc = _CACHE[(stage, debug)]
    cst = _consts()
    in_maps = []
    for c in range(8):
        b, r = c // 2, c % 2
        order = own_blocks(r) + other_blocks(r)
        tok = (np.asarray(order)[:, None] * 128 + np.arange(128)[None, :]).reshape(-1)
        m = {"x": np.ascontiguousarray(x[b][tok]),
             "pos": np.ascontiguousarray(positions[b][tok].reshape(NBLK, 128).T.astype(np.int32)),
             "kch": np.ascontiguousarray((tok // 64).astype(np.float32)[None, :].astype(ml_dtypes.bfloat16)),
             "qch": np.ascontiguousarray((tok[:NOWN * 128] // 64).astype(np.float32).reshape(NOWN, 128).T),
             "cst": cst}
        for n in ["ffn1_w_gate", "ffn1_w_up", "ffn1_w_down", "w_in", "w_up_attn", "w_up_sgu", "w_out",
                  "ffn2_w_gate", "ffn2_w_up", "ffn2_w_down"]:
            m[n] = np.asarray(inputs[n])[0]
        for n in ["norm_ffn1", "norm_mix", "norm_ffn2", "q_norm", "k_norm", "idx_k_norm", "sgu_v_norm"]:
            m[n] = np.asarray(inputs[n])[0][None, :]
        m["sgu_w_s"] = np.asarray(inputs["sgu_w_s"])[0]
        m["sgu_b_s"] = np.asarray(inputs["sgu_b_s"])[0]
        in_maps.append(m)
    res = run_bass_kernel_spmd(nc, in_maps, core_ids=list(range(8)))
    if debug:
        _CACHE["last_results"] = res.results
    out = np.zeros((4, S, D), np.float32)
    for c in range(8):
        b, r = c // 2, c % 2
        ob = own_blocks(r)
        o = np.asarray(res.results[c]["out"]).reshape(NOWN, 128, D)
        for i, blk in enumerate(ob):
            out[b, blk * 128:(blk + 1) * 128] = o[i]
    return out
```
